# Optimizing a Trainium2 kernel written in Bass

```python
import math
import jax, jax.numpy as jnp
from jax import lax
import numpy as np

D_MODEL = 2048
BATCH = 4
SEQ = 2048
DEPTH = 1
DEC_BATCH = 32
DEC_SEQ = 4
PAST_LEN = 16384
PAGE_SIZE = 128

M_HEADS = 4
M_WIDTH = D_MODEL
M_HEAD_DIM = M_WIDTH // M_HEADS
M_CHUNK = 64
A_HEADS = 32
A_HEAD_DIM = 64
A_KV_HEADS = 4
A_GROUP = A_HEADS // A_KV_HEADS
A_WIDTH = A_HEADS * A_HEAD_DIM
A_KV_WIDTH = A_KV_HEADS * A_HEAD_DIM
WINDOW = 128
DN_ALPHA = (2.0 * DEPTH) ** 0.25
DN_BETA = (8.0 * DEPTH) ** -0.25
LN_EPS = 1e-5

SPLITS = (M_WIDTH, M_WIDTH, M_WIDTH, M_WIDTH, M_WIDTH,
          A_WIDTH, A_KV_WIDTH, A_KV_WIDTH, A_WIDTH,
          D_MODEL, D_MODEL,
          M_HEADS, M_HEADS)
SPLIT_SCALES = (1.0, 1.0, DN_BETA, 1.0, 1.0,
                1.0, 1.0, DN_BETA, 1.0,
                1.0, 1.0,
                1.0, 1.0)
IN_COLS = sum(SPLITS)

kernel_name = 'hybrid_mlstm_swa_decode_step'


def _layernorm(x, w, b):
    xf = x.astype(jnp.float32)
    mu = jnp.mean(xf, axis=-1, keepdims=True)
    var = jnp.mean(jnp.square(xf - mu), axis=-1, keepdims=True)
    return (xf - mu) * lax.rsqrt(var + LN_EPS) * w.astype(jnp.float32) + b.astype(jnp.float32)


def _alibi_slopes():
    return 2.0 ** (-8.0 * jnp.arange(1, A_HEADS + 1, dtype=jnp.float32) / A_HEADS)


def _in_proj(x, w_in, b_igate, b_fgate):
    B, T, _ = x.shape
    u = jnp.einsum('btd,dc->btc', x, w_in)
    parts = []
    start = 0
    for width in SPLITS:
        parts.append(u[..., start:start + width])
        start += width
    q_m, k_m, v_m, o_m, z_m, q_a, k_a, v_a, z_a, g_m, g_a, i_pre, f_pre = parts
    mh = lambda a: a.reshape(B, T, M_HEADS, M_HEAD_DIM)
    ig = (i_pre + b_igate).astype(jnp.float32)
    lf = jax.nn.log_sigmoid((f_pre + b_fgate).astype(jnp.float32))
    return (mh(q_m), mh(k_m), mh(v_m), o_m, z_m,
            q_a.reshape(B, T, A_KV_HEADS, A_GROUP, A_HEAD_DIM),
            k_a.reshape(B, T, A_KV_HEADS, A_HEAD_DIM),
            v_a.reshape(B, T, A_KV_HEADS, A_HEAD_DIM),
            z_a, g_m, g_a, ig, lf)


def _mlstm_chunk(carry, inp):
    C, n, m = carry
    q, k, v, ig, lf = inp
    L = q.shape[2]
    b = jnp.cumsum(lf, axis=-1)
    causal = jnp.tril(jnp.ones((L, L), dtype=bool))
    log_d = jnp.where(causal, b[..., :, None] - b[..., None, :] + ig[..., None, :], -jnp.inf)
    log_inter = b + m[..., None]
    m_row = jnp.maximum(jnp.max(log_d, axis=-1), log_inter)
    d = jnp.exp(log_d - m_row[..., None])
    inter = jnp.exp(log_inter - m_row)
    s = jnp.einsum('bhtd,bhsd->bhts', q, k) * d
    num = jnp.einsum('bhts,bhsd->bhtd', s, v) + inter[..., None] * jnp.einsum('bhtk,bhkv->bhtv', q, C)
    den = jnp.sum(s, axis=-1) + inter * jnp.einsum('bhtk,bhk->bht', q, n)
    h = num / jnp.maximum(jnp.abs(den), jnp.exp(-m_row))[..., None]
    b_last = b[..., -1]
    log_w = b_last[..., None] - b + ig
    m_new = jnp.maximum(b_last + m, jnp.max(log_w, axis=-1))
    w = jnp.exp(log_w - m_new[..., None])
    decay = jnp.exp(b_last + m - m_new)
    C_new = decay[..., None, None] * C + jnp.einsum('bhsk,bhsv->bhkv', w[..., None] * k, v)
    n_new = decay[..., None] * n + jnp.einsum('bhs,bhsk->bhk', w, k)
    return (C_new, n_new, m_new), h


def _mlstm(q, k, v, ig, lf, C0, n0, m0):
    B, T = q.shape[:2]
    L = math.gcd(M_CHUNK, T)
    nc = T // L

    def to_chunks(a):
        a = a.astype(jnp.float32).reshape((B, nc, L) + a.shape[2:])
        return jnp.moveaxis(jnp.moveaxis(a, 1, 0), 3, 2)

    qc = to_chunks(q) * (M_HEAD_DIM ** -0.5)
    carry0 = (C0.astype(jnp.float32), n0.astype(jnp.float32), m0.astype(jnp.float32))
    (C, n, m), h = lax.scan(_mlstm_chunk, carry0,
                            (qc, to_chunks(k), to_chunks(v), to_chunks(ig), to_chunks(lf)))
    h = jnp.swapaxes(jnp.moveaxis(h, 0, 1), 2, 3).reshape(B, T, M_HEADS, M_HEAD_DIM)
    return h, C, n, m


def _sink_attend(q, k, v, dist, valid, slopes, sinks):
    s = jnp.einsum('...qhgd,...khd->...hgqk', q, k).astype(jnp.float32) * (A_HEAD_DIM ** -0.5)
    s = s - slopes.reshape(A_KV_HEADS, A_GROUP)[:, :, None, None] * dist[..., None, None, :, :].astype(jnp.float32)
    s = jnp.where(valid[..., None, None, :, :], s, -jnp.inf)
    sink = jnp.broadcast_to(sinks.astype(jnp.float32).reshape(A_KV_HEADS, A_GROUP)[:, :, None, None],
                            s.shape[:-1] + (1,))
    p = jax.nn.softmax(jnp.concatenate([s, sink], axis=-1), axis=-1)[..., :-1]
    return jnp.einsum('...hgqk,...khd->...qhgd', p.astype(v.dtype), v)


def _swa_prompt(q, k, v, slopes, sinks):
    B, S = q.shape[:2]
    nb = S // WINDOW
    qb = q.reshape(B, nb, WINDOW, A_KV_HEADS, A_GROUP, A_HEAD_DIM)
    kb = k.reshape(B, nb, WINDOW, A_KV_HEADS, A_HEAD_DIM)
    vb = v.reshape(B, nb, WINDOW, A_KV_HEADS, A_HEAD_DIM)
    pad = ((0, 0), (1, 0), (0, 0), (0, 0), (0, 0))
    kk = jnp.concatenate([jnp.pad(kb, pad)[:, :-1], kb], axis=2)
    vv = jnp.concatenate([jnp.pad(vb, pad)[:, :-1], vb], axis=2)
    qi = jnp.arange(WINDOW)[:, None]
    kj = jnp.arange(2 * WINDOW)[None, :]
    dist = qi + WINDOW - kj
    blk = jnp.arange(nb)[:, None, None]
    valid = (dist >= 0) & (dist < WINDOW) & ((blk > 0) | (kj >= WINDOW))
    out = _sink_attend(qb, kk, vv, dist, valid, slopes, sinks)
    return out.reshape(B, S, A_WIDTH)


def _swa_sample(q, k_new, v_new, k_buf, v_buf, slopes, sinks):
    B, T = q.shape[:2]
    Wb = k_buf.shape[1]
    k_all = jnp.concatenate([k_buf.astype(k_new.dtype), k_new], axis=1)
    v_all = jnp.concatenate([v_buf.astype(v_new.dtype), v_new], axis=1)
    qi = jnp.arange(T)[:, None]
    kj = jnp.arange(Wb + T)[None, :]
    dist = Wb + qi - kj
    valid = (dist >= 0) & (dist < WINDOW)
    out = _sink_attend(q, k_all, v_all, dist, valid, slopes, sinks)
    return out.reshape(B, T, A_WIDTH), k_all[:, T:], v_all[:, T:]


def _merge(x, h_m, o_m, z_m, h_a, z_a, g_m, g_a, mlstm_norm_w, w_proj_m, w_proj_a, w_out, ln_w, ln_b):
    B, T, _ = x.shape
    mu = jnp.mean(h_m, axis=-1, keepdims=True)
    var = jnp.mean(jnp.square(h_m - mu), axis=-1, keepdims=True)
    hn = (h_m - mu) * lax.rsqrt(var + LN_EPS) * mlstm_norm_w.astype(jnp.float32).reshape(M_HEADS, M_HEAD_DIM)
    br_m = hn.reshape(B, T, M_WIDTH) * jax.nn.sigmoid(o_m) * jax.nn.silu(z_m)
    br_a = h_a * jax.nn.silu(z_a)
    y = (jax.nn.sigmoid(g_m) * jnp.einsum('btc,cd->btd', br_m, w_proj_m)
         + jax.nn.sigmoid(g_a) * jnp.einsum('btc,cd->btd', br_a, w_proj_a))
    out = jnp.einsum('btc,cd->btd', y, w_out)
    return _layernorm(DN_ALPHA * x + out, ln_w, ln_b).astype(x.dtype)


def setup_inputs(seed: int = 0) -> dict:
    key = jax.random.key(seed)
    ks = jax.random.split(key, 18)
    f32 = jnp.float32
    wb = min(WINDOW, PAST_LEN)
    col_scale = jnp.concatenate([jnp.full((w,), s, f32) for w, s in zip(SPLITS, SPLIT_SCALES)])
    return {
        'x_prompt': jax.random.normal(ks[0], (BATCH, SEQ, D_MODEL), f32),
        'x_sample': jax.random.normal(ks[1], (DEC_BATCH, DEC_SEQ, D_MODEL), f32),
        'state_mlstm_C': jax.random.normal(ks[2], (DEC_BATCH, M_HEADS, M_HEAD_DIM, M_HEAD_DIM), f32),
        'state_mlstm_n': jax.random.normal(ks[3], (DEC_BATCH, M_HEADS, M_HEAD_DIM), f32),
        'state_mlstm_m': 0.5 * jax.random.normal(ks[4], (DEC_BATCH, M_HEADS), f32),
        'state_attn_k': jax.random.normal(ks[5], (DEC_BATCH, wb, A_KV_HEADS, A_HEAD_DIM), f32),
        'state_attn_v': 0.6 * jax.random.normal(ks[6], (DEC_BATCH, wb, A_KV_HEADS, A_HEAD_DIM), f32),
        'w_in': jax.random.normal(ks[7], (D_MODEL, IN_COLS), f32) * (D_MODEL ** -0.5) * col_scale,
        'b_igate': 0.1 * jax.random.normal(ks[8], (M_HEADS,), f32),
        'b_fgate': jnp.linspace(3.0, 6.0, M_HEADS, dtype=f32) + 0.1 * jax.random.normal(ks[9], (M_HEADS,), f32),
        'mlstm_norm_w': 1.0 + 0.1 * jax.random.normal(ks[10], (M_WIDTH,), f32),
        'attn_sinks': 0.5 * jax.random.normal(ks[11], (A_HEADS,), f32),
        'w_proj_m': jax.random.normal(ks[12], (M_WIDTH, D_MODEL), f32) * (M_WIDTH ** -0.5) * DN_BETA,
        'w_proj_a': jax.random.normal(ks[13], (A_WIDTH, D_MODEL), f32) * (A_WIDTH ** -0.5) * DN_BETA,
        'w_out': jax.random.normal(ks[14], (D_MODEL, D_MODEL), f32) * (D_MODEL ** -0.5) * DN_BETA,
        'ln_w': 1.0 + 0.1 * jax.random.normal(ks[15], (D_MODEL,), f32),
        'ln_b': 0.1 * jax.random.normal(ks[16], (D_MODEL,), f32),
    }


def reference(x_prompt, x_sample, state_mlstm_C, state_mlstm_n, state_mlstm_m, state_attn_k, state_attn_v,
              w_in, b_igate, b_fgate, mlstm_norm_w, attn_sinks, w_proj_m, w_proj_a, w_out, ln_w, ln_b):
    slopes = _alibi_slopes()

    y_p = x_prompt
    for _ in range(DEPTH):
        (q_m, k_m, v_m, o_m, z_m, q_a, k_a, v_a, z_a, g_m, g_a, ig, lf) = _in_proj(y_p, w_in, b_igate, b_fgate)
        Bp, Sp = y_p.shape[:2]
        C0 = jnp.zeros((Bp, M_HEADS, M_HEAD_DIM, M_HEAD_DIM), jnp.float32)
        n0 = jnp.zeros((Bp, M_HEADS, M_HEAD_DIM), jnp.float32)
        m0 = jnp.zeros((Bp, M_HEADS), jnp.float32)
        h_m, C_p, n_p, m_p = _mlstm(q_m, k_m, v_m, ig, lf, C0, n0, m0)
        h_a = _swa_prompt(q_a, k_a, v_a, slopes, attn_sinks)
        buf = min(WINDOW, Sp)
        k_buf_p = k_a[:, Sp - buf:]
        v_buf_p = v_a[:, Sp - buf:]
        y_p = _merge(y_p, h_m, o_m, z_m, h_a, z_a, g_m, g_a, mlstm_norm_w, w_proj_m, w_proj_a, w_out, ln_w, ln_b)

    y_s = x_sample
    for _ in range(DEPTH):
        (q_m, k_m, v_m, o_m, z_m, q_a, k_a, v_a, z_a, g_m, g_a, ig, lf) = _in_proj(y_s, w_in, b_igate, b_fgate)
        h_m, C_s, n_s, m_s = _mlstm(q_m, k_m, v_m, ig, lf, state_mlstm_C, state_mlstm_n, state_mlstm_m)
        h_a, k_buf_s, v_buf_s = _swa_sample(q_a, k_a, v_a, state_attn_k, state_attn_v, slopes, attn_sinks)
        y_s = _merge(y_s, h_m, o_m, z_m, h_a, z_a, g_m, g_a, mlstm_norm_w, w_proj_m, w_proj_a, w_out, ln_w, ln_b)

    return (y_p, y_s, C_p, n_p, m_p, k_buf_p, v_buf_p, C_s, n_s, m_s, k_buf_s, v_buf_s)
```

```python
import numpy as np
from contextlib import ExitStack
import concourse.bass as bass
import concourse.mybir as mybir
from concourse.bass_utils import run_bass_kernel_spmd

F32 = mybir.dt.float32
BF16 = mybir.dt.bfloat16
AF = mybir.ActivationFunctionType
ALU = mybir.AluOpType
AX = mybir.AxisListType

D = 2048
NT = 1024
NSM = 16
TOK = NT + NSM
IN_COLS = 18952
GW = 256
DN_ALPHA = 2.0 ** 0.25
LN_EPS = 1e-5
C_QM, C_KM, C_VM, C_OM, C_ZM = 0, 2048, 4096, 6144, 8192
C_QA, C_KA, C_VA, C_ZA = 10240, 12288, 12544, 12800
C_GM, C_GA, C_IG = 14848, 16896, 18944
SLOPES = [float(2.0 ** (-8.0 * (i + 1) / 32)) for i in range(32)]
NCST = 1032
BIG = 1.0e9
NEG = -1.0e30
import os
STOP = os.environ.get('KSTOP', '')
KG = float(os.environ.get('KG', '99'))
KCORES = int(os.environ.get('KCORES', '8'))


LOOKAHEAD = int(os.environ.get("KLOOK", "2048"))
SYNC_LAT = 0.05


class Buf:
    __slots__ = ("name", "w", "r", "dsem", "dcount", "excl")

    def __init__(self, name="", dsem=None, excl=False):
        self.excl = excl
        self.name = name
        self.w = None
        self.r = []
        self.dsem = dsem
        self.dcount = 0


class Op:
    __slots__ = ("idx", "eng", "builds", "deps", "cost", "is_dma", "sembuf", "epoch", "open", "ticket",
                 "start", "finish", "done", "barrier")

    def __init__(self, idx, eng, epoch):
        self.idx = idx; self.eng = eng; self.builds = []; self.deps = {}; self.cost = 0.0
        self.is_dma = False; self.sembuf = None; self.epoch = epoch; self.open = False
        self.ticket = None; self.start = 0.0; self.finish = 0.0; self.done = False; self.barrier = False


class EngQ:
    def __init__(self, name, sem):
        self.name = name
        self.sem = sem
        self.count = 0
        self.seen = {}
        self.q = []
        self.openop = None


class Sched:
    def __init__(self, nc, sems):
        self.nc = nc
        self.E = {k: EngQ(k, s) for k, s in sems.items()}
        self.dbufs = []
        self.ops = []
        self.epoch = 0

    def dbuf(self, name, sem):
        b = Buf(name, sem)
        self.dbufs.append(b)
        return b

    def _adddeps(self, op, reads, writes):
        ex = [b for b in reads if b.excl]
        if ex:
            reads = [b for b in reads if not b.excl]
            writes = list(writes) + ex
        RANK = {"raw": 3, "waw": 2, "wawdj": 1, "war": 0}

        def put(d, k):
            old_ = op.deps.get(d)
            if old_ is None or RANK[k] > RANK[old_]:
                op.deps[d] = k
        for b in reads:
            if b.w is not None and b.w is not op:
                put(b.w, "raw")
        for b in writes:
            if b.w is not None and b.w is not op:
                put(b.w, "wawdj" if b.name.startswith("dj_") else "waw")
            for r in b.r:
                if r is not op:
                    put(r, "war")
        for b in reads:
            if not b.r or b.r[-1] is not op:
                b.r.append(op)
        for b in writes:
            b.w = op
            b.r = []

    def op(self, eng, build, reads=(), writes=(), signal=True, cost=None):
        E = self.E[eng]
        o = E.openop
        if o is None:
            o = Op(len(self.ops), eng, self.epoch)
            self.ops.append(o)
        o.builds.append(build)
        if cost is None:
            cost = 0.07 if eng == "pe" else 0.15
        o.cost += cost
        self._adddeps(o, reads, writes)
        E.openop = None if signal else o

    def dma(self, eng, out, in_, reads=(), writes=(), sembuf=None, cost=3.0, **kw):
        o = Op(len(self.ops), eng, self.epoch)
        self.ops.append(o)
        o.builds.append(lambda e: e.dma_start(out=out, in_=in_, **kw))
        o.is_dma = True
        o.sembuf = sembuf
        o.cost = cost
        self._adddeps(o, reads, writes)

    def barrier(self):
        for E in self.E.values():
            assert E.openop is None
        self.epoch += 1

    def _schedule_epoch(self, ops, free_at):
        n = len(ops)
        pos = 0
        sched = [False] * n
        order = []
        while pos < n:
            best = None; best_t = None
            hi = min(n, pos + LOOKAHEAD)
            for j in range(pos, hi):
                if sched[j]:
                    continue
                o = ops[j]
                t = free_at[o.eng]
                ok = True
                for d in o.deps:
                    if d.epoch != o.epoch:
                        continue
                    if not d.done:
                        ok = False
                        break
                    f = d.finish + (SYNC_LAT if (d.eng != o.eng or d.is_dma) else 0.0)
                    if f > t:
                        t = f
                if not ok:
                    continue
                if best is None or t < best_t - 1e-9:
                    best, best_t = j, t
                    if t <= free_at[o.eng] + 1e-9 and j == pos:
                        break
            o = ops[best]
            sched[best] = True
            o.done = True
            o.start = best_t
            if o.is_dma:
                o.finish = best_t + o.cost
                free_at[o.eng] = best_t + 0.15
            else:
                o.finish = best_t + o.cost
                free_at[o.eng] = o.finish
            order.append(o)
            while pos < n and sched[pos]:
                pos += 1
        return order

    def flush(self, block):
        for E in self.E.values():
            assert E.openop is None
        nep = self.epoch + 1
        by_ep = [[] for _ in range(nep)]
        for o in self.ops:
            by_ep[o.epoch].append(o)
        free_at = {k: 0.0 for k in self.E}
        prog = {k: [] for k in self.E}
        for ep in range(nep):
            order = self._schedule_epoch(by_ep[ep], free_at)
            if os.environ.get("KDUMP") and ep == 0:
                for o in order[:400]:
                    print("SCHED", o.idx, o.eng, "dma" if o.is_dma else "", round(o.start, 2), round(o.finish, 2), round(o.cost, 2), len(o.builds))
            tmax = max(free_at.values())
            for o in order:
                if o.finish > tmax:
                    tmax = o.finish
            for k in free_at:
                free_at[k] = tmax
            for o in order:
                E = self.E[o.eng]
                if o.is_dma:
                    o.sembuf.dcount += 16
                    o.ticket = (o.sembuf.dsem, o.sembuf.dcount)
                else:
                    E.count += 1
                    o.ticket = (E.sem, E.count)
            for o in order:
                E = self.E[o.eng]
                need = {}
                for d, kind in o.deps.items():
                    if d.epoch != o.epoch:
                        continue
                    if (not d.is_dma) and d.eng == o.eng and (kind in ("war", "wawdj") or o.eng == "pe"):
                        continue
                    sem, val = d.ticket
                    if E.seen.get(sem, 0) >= val:
                        continue
                    if need.get(sem, 0) < val:
                        need[sem] = val
                for sem, val in need.items():
                    E.seen[sem] = val
                inc = (o.ticket[0], 16) if o.is_dma else (o.ticket[0], 1)
                prog[o.eng].append((list(need.items()), o.builds, inc))
            for E in self.E.values():
                need = {}
                for O in self.E.values():
                    if O is not E and O.count > 0 and E.seen.get(O.sem, 0) < O.count:
                        need[O.sem] = O.count
                for b in self.dbufs:
                    if b.dcount > 0 and E.seen.get(b.dsem, 0) < b.dcount:
                        need[b.dsem] = b.dcount
                for sem, val in need.items():
                    E.seen[sem] = val
                prog[E.name].append((list(need.items()), [], None))

        def run(name):
            def body(e):
                for waits, builds, inc in prog[name]:
                    for sem, val in waits:
                        e.wait_ge(sem, val)
                    ins = None
                    for b in builds:
                        ins = b(e)
                    if ins is not None and inc is not None:
                        ins.then_inc(inc[0], inc[1])
            return body
        block.tensor(run("pe"))
        block.scalar(run("act"))
        block.vector(run("dve"))
        block.gpsimd(run("pool"))
        block.sync(run("sp"))


def build_program():
    nc = bass.Bass("TRN2", target_bir_lowering=False)

    def din(name, shape):
        return nc.dram_tensor(name, list(shape), F32, kind="ExternalInput").ap()

    def dout(name, shape):
        return nc.dram_tensor(name, list(shape), F32, kind="ExternalOutput").ap()

    xT_d = din("xT", [D, TOK])
    xTp_d = din("xTp", [D, NT])
    xtok_d = din("xtok", [TOK, D])
    w_in = din("w_in", [D, IN_COLS])
    w_pm = din("w_proj_m", [D, D])
    w_pa = din("w_proj_a", [D, D])
    w_o = din("w_out", [D, D])
    cst_d = din("cst", [128, NCST])
    dprev0_d = din("dprev0", [128, 128])
    flag_d = din("flag", [128, 1])
    bi_d = din("b_igate", [4])
    bf_d = din("b_fgate", [4])
    nw_d = din("nw", [128, 16])
    sinks_d = din("attn_sinks", [32])
    lnw_d = din("ln_w", [D])
    lnb_d = din("ln_b", [D])
    sC_d = din("sC", [4, 4, 512, 512])
    sn_d = din("sn", [4, 4, 128, 4])
    sm_d = din("sm", [16])
    sk_d = din("sk", [4, 128, 256])
    sv_d = din("sv", [4, 128, 256])

    y_o = dout("y", [TOK, D])
    Cp_o = dout("Cp", [4, 512, 512])
    np_o = dout("np", [4, 128, 4])
    mp_o = dout("mp", [4, 1])
    kbp_o = dout("kbp", [128, 256])
    vbp_o = dout("vbp", [128, 256])
    Cs_o = dout("Cs", [4, 4, 512, 512])
    ns_o = dout("ns", [4, 4, 128, 4])
    ms_o = dout("ms", [16, 1])
    kbs_o = dout("kbs", [4, 128, 256])
    vbs_o = dout("vbs", [4, 128, 256])

    with ExitStack() as st:
        sems = {k: st.enter_context(nc.semaphore(k)) for k in ["pe", "act", "dve", "pool", "sp"]}
        S = Sched(nc, sems)
        ARENA_B = 207 * 1024 + 512
        arena = st.enter_context(nc.sbuf_tensor("arena", [128, ARENA_B // 2], BF16))
        alloc_ptr = [0]

        def alloc(nbytes):
            o = alloc_ptr[0]
            alloc_ptr[0] = o + ((nbytes + 31) // 32) * 32
            assert alloc_ptr[0] <= ARENA_B, ("SBUF overflow", alloc_ptr[0])
            return o

        def V(off, dt, shape):
            n = 1
            for s_ in shape:
                n *= s_
            sz = 4 if dt == F32 else 2
            ap = arena[:, off // 2: off // 2 + n * sz // 2]
            if dt == F32:
                ap = ap.bitcast(F32)
            if len(shape) == 2:
                return ap.rearrange("p (a b) -> p a b", a=shape[0])
            if len(shape) == 3:
                return ap.rearrange("p (a b c) -> p a b c", a=shape[0], b=shape[1])
            return ap

        def T(dt, shape):
            n = 1
            for s_ in shape:
                n *= s_
            return V(alloc(n * (4 if dt == F32 else 2)), dt, shape)

        def dB(name):
            return S.dbuf(name, st.enter_context(nc.semaphore("d_" + name)))

        psA = [st.enter_context(nc.psum_tensor(f"psA{i}", [128, 512], F32)) for i in range(2)]
        psB = st.enter_context(nc.psum_tensor("psB", [128, 2048], F32))
        psC = st.enter_context(nc.psum_tensor("psC", [128, 512], F32))
        psT = st.enter_context(nc.psum_tensor("psT", [128, 1024], BF16))
        psT32 = psT[:, :].bitcast(F32)
        BpsA = [Buf("psA0", excl=True), Buf("psA1", excl=True)]
        BpsB4 = [Buf("psB%d" % i, excl=True) for i in range(4)]
        BpsT = Buf("psT", excl=True)
        Bc_abc = Bc_bm = Bc_st = Bc_dq = Bc_md = Bc_nup = Bc_g = Buf("psC", excl=True)

        xT = T(BF16, [16, TOK]); BxT = dB("xT")
        bufY = T(BF16, [16, TOK]); BY = dB("bufY")
        br_m = T(BF16, [16, TOK]); Bbrm4 = [Buf("dj_br_m%d" % i) for i in range(4)]
        br_a_off = alloc(16 * TOK * 2)
        br_a = V(br_a_off, BF16, [16, TOK]); Bbra4 = [Buf("dj_br_a%d" % i) for i in range(4)]
        wsl = [T(BF16, [16, GW]) for _ in range(2)]
        Bw = [dB("w0"), dB("w1")]
        cst = T(F32, [NCST]); Bcst = dB("cst")
        identb = T(BF16, [128])
        onesb = T(BF16, [2]); Bones = Buf("ones")
        dprev0 = T(F32, [128])
        flag = T(F32, [1])
        bib = T(F32, [4]); bfb = T(F32, [4])
        nw = T(F32, [16])
        es = T(F32, [32]); Bes = Buf("es")
        gtmp2 = [T(BF16, [512]) for _ in range(2)]; Bgt2 = [Buf("gtA"), Buf("gtB")]
        work_off = alloc_ptr[0]
        block = st.enter_context(nc.Block())

        ident = cst[:, 0:128]; Umat = cst[:, 128:256]; mask = cst[:, 256:384]; maskT = cst[:, 384:512]
        E127 = cst[:, 512:640]; E3 = cst[:, 640:768]
        dprev = cst[:, 768:896]; dcur = cst[:, 896:1024]; dsb = cst[:, 1024:1028]; dsn = cst[:, 1028:1032]

        S.dma("pool", cst, cst_d, writes=[Bcst], sembuf=Bcst)
        S.dma("pool", dprev0, dprev0_d, writes=[Bcst], sembuf=Bcst)
        S.dma("pool", flag, flag_d, writes=[Bcst], sembuf=Bcst)
        S.dma("pool", bib, bi_d.partition_broadcast(128), writes=[Bcst], sembuf=Bcst)
        S.dma("pool", bfb, bf_d.partition_broadcast(128), writes=[Bcst], sembuf=Bcst)
        S.dma("pool", nw, nw_d, writes=[Bcst], sembuf=Bcst)
        S.dma("pool", es, sinks_d.partition_broadcast(128), writes=[Bcst], sembuf=Bcst)
        S.dma("pool", identb, cst_d[:, 0:128], writes=[Bcst], sembuf=Bcst)
        S.dma("pool", xT, xT_d.rearrange("(k p) t -> p k t", p=128), writes=[BxT], sembuf=BxT, cost=30.0)
        S.dma("pool", bufY[:, :, 0:NT], xTp_d.rearrange("(k p) t -> p k t", p=128), writes=[BY], sembuf=BY, cost=30.0)
        S.op("dve", lambda e: e.memset(onesb, 1.0), writes=[Bones])
        S.op("act", lambda e: e.activation(es, es, AF.Exp), reads=[Bcst], writes=[Bes])

        if STOP == "I":
            S.barrier(); S.flush(block); return nc
        wctr = [0]

        def load_w(src, c0, ncols):
            i = wctr[0] % 2
            wctr[0] += 1
            S.dma("pool", wsl[i][:, :, 0:ncols], src[:, c0:c0 + ncols].rearrange("(k p) c -> p k c", p=128),
                  writes=[Bw[i]], sembuf=Bw[i], cost=9.0)
            return wsl[i], Bw[i]

        pctr = [0]

        def next_psA():
            i = pctr[0] % 2
            pctr[0] += 1
            return psA[i], BpsA[i]

        def mm_group(out_ap, pairs, reads, wbuf, split=None):
            n = len(pairs)
            c_ = max(64, int(out_ap.shape[-1])) / 1900.0
            for i, (l, r) in enumerate(pairs):
                sig = (i == n - 1) or (split is not None and (i % split) == split - 1)
                S.op("pe", (lambda e, l=l, r=r, i=i: e.matmul(out_ap, l, r, start=(i == 0), stop=(i == n - 1))),
                     reads=reads, writes=[wbuf], signal=sig, cost=c_)

        def fmode(w, wB, ncols, rhs_fn, rhsB, blocks, epi, lhs_cols=None):
            for cc in range(ncols // 128):
                for (c0, n) in blocks:
                    ps, pB = next_psA()
                    if lhs_cols is None:
                        lf_ = lambda kc: w[:, kc, cc * 128:(cc + 1) * 128]
                    else:
                        lf_ = lambda kc: lhs_cols(kc, cc)
                    mm_group(ps[:, 0:n], [(lf_(kc), rhs_fn(kc, c0, n)) for kc in range(16)], [wB] + rhsB, pB, split=8)
                    epi(cc, c0, n, ps[:, 0:n], pB)

        def tmode(w, wB, ncols, lhs_fn, lhsB, M, epi):
            ps, pB = next_psA()
            mm_group(ps[0:M, 0:ncols], [(lhs_fn(kc), w[:, kc, 0:ncols]) for kc in range(16)], [wB] + lhsB, pB)
            epi(ps[0:M, 0:ncols], pB)

        BLK = [(0, 347), (347, 347), (694, 346)]
        xrhs = lambda kc, c0, n: xT[:, kc, c0:c0 + n]

        bgst = {"gen": None, "k": 0}

        def gate_items(cbase, func, dst, dstB4, g_list):
            loaded = {}
            for i, g in enumerate(g_list):
                if g not in loaded:
                    loaded[g] = load_w(w_in, cbase + g * GW, GW)
                if i + 1 < len(g_list) and g_list[i + 1] not in loaded:
                    loaded[g_list[i + 1]] = load_w(w_in, cbase + g_list[i + 1] * GW, GW)
                w, wB = loaded[g]
                for cc in range(2):
                    ch = g * 2 + cc
                    dB_ = dstB4[ch // 4]
                    for (c0, n) in BLK:
                        ps, pB = next_psA()
                        gi = bgst["k"] % 2
                        bgst["k"] += 1
                        gt_, gB_ = gtmp2[gi], Bgt2[gi]
                        mm_group(ps[:, 0:n], [(w[:, kc, cc * 128:(cc + 1) * 128], xT[:, kc, c0:c0 + n]) for kc in range(16)],
                                 [wB, BxT], pB, split=8)
                        S.op("act", lambda e, ps=ps, n=n, gt_=gt_: e.activation(gt_[:, 0:n], ps[:, 0:n], AF.Exp, scale=-1.0), reads=[pB], writes=[gB_], cost=0.5)
                        S.op("act", lambda e, n=n, gt_=gt_: e.activation(gt_[:, 0:n], gt_[:, 0:n], AF.Ln, bias=1.0), reads=[gB_], writes=[gB_], cost=0.5)
                        S.op("act", lambda e, n=n, gt_=gt_: e.activation(gt_[:, 0:n], gt_[:, 0:n], AF.Exp, scale=-1.0), reads=[gB_], writes=[gB_], cost=0.5)
                        S.op("dve", lambda e, ch=ch, c0=c0, n=n, gt_=gt_: e.tensor_tensor(dst[:, ch, c0:c0 + n], dst[:, ch, c0:c0 + n], gt_[:, 0:n], ALU.mult),
                             reads=[gB_, dB_], writes=[dB_], cost=0.6)
                        if func == AF.Silu:
                            S.op("dve", lambda e, ps=ps, ch=ch, c0=c0, n=n: e.tensor_tensor(dst[:, ch, c0:c0 + n], dst[:, ch, c0:c0 + n], ps[:, 0:n], ALU.mult),
                                 reads=[pB, dB_], writes=[dB_], cost=0.6)
                        yield

        def chain(*gens):
            for g_ in gens:
                for _ in g_:
                    yield

        def bg_step(k=1):
            for _ in range(k):
                if bgst["gen"] is None:
                    return
                try:
                    next(bgst["gen"])
                except StopIteration:
                    bgst["gen"] = None
                    return

        def bg_drain():
            while bgst["gen"] is not None:
                bg_step()

        W0 = work_off
        alloc_ptr[0] = W0
        g_prev = {k: T(F32, [8, 4]) for k in ("ig", "lf", "b", "a", "z", "t")}
        g_own = {k: T(F32, [8, 4]) for k in ("ig", "lf", "b", "a", "z", "t")}
        g_smp = {k: T(F32, [4, 4]) for k in ("ig", "lf", "b", "a", "z", "t")}
        Bg = Buf("gates")
        wg = T(BF16, [16, 8]); Bwg = dB("wg")
        SK = ("cm", "mloc", "BL", "ML", "m", "mprev", "mrow", "bm", "inter", "emr", "dec", "t")
        pg = {k: T(F32, [8, 4]) for k in SK + ("G", "wG")}
        og = {k: T(F32, [8, 4]) for k in SK}
        sg_ = {k: T(F32, [4, 4]) for k in SK}
        zero4 = T(F32, [4]); msmp = T(F32, [4, 4])
        minit = T(F32, [4]); Bpg = Buf("pg")
        logdG = [T(F32, [128]) for _ in range(2)]; BlogdG = [Buf("lgA"), Buf("lgB")]
        work1 = alloc_ptr[0]
        S.dma("pool", wg, w_in[:, C_IG:C_IG + 8].rearrange("(k p) c -> p k c", p=128), writes=[Bwg], sembuf=Bwg)

        def gates(L, n, lhs_fn, lhsB, gt):
            for c in range(n):
                mm_group(psC[0:L, c * 8:(c + 1) * 8], [(lhs_fn(kc, c), wg[:, kc, :]) for kc in range(16)],
                         [Bwg] + lhsB, Bc_g)
            if KG < 1:
                return
            pv = psC[0:L, 0:n * 8].rearrange("p (c g) -> p c g", g=8)
            bi3 = bib[0:L, None, :].to_broadcast([L, n, 4])
            bf3 = bfb[0:L, None, :].to_broadcast([L, n, 4])
            ig, lf, b, a, z, t = (gt[k][0:L] for k in ("ig", "lf", "b", "a", "z", "t"))
            S.op("dve", lambda e: e.tensor_tensor(ig, pv[:, :, 0:4], bi3, ALU.add), reads=[Bc_g, Bcst], writes=[Bg])
            S.op("dve", lambda e: e.tensor_tensor(z, pv[:, :, 4:8], bf3, ALU.add), reads=[Bc_g, Bcst], writes=[Bg])
            if KG < 2:
                return
            S.op("dve", lambda e: e.scalar_tensor_tensor(t, z, -1.0, z, ALU.mult, ALU.max), reads=[Bg], writes=[Bg])
            if KG < 2.2:
                return
            S.op("act", lambda e: e.activation(t, t, AF.Exp, scale=-1.0), reads=[Bg], writes=[Bg])
            if KG < 2.4:
                return
            S.op("act", lambda e: e.activation(t, t, AF.Ln, bias=1.0), reads=[Bg], writes=[Bg])
            if KG < 2.6:
                return
            S.op("dve", lambda e: e.tensor_scalar_min(lf, z, 0.0), reads=[Bg], writes=[Bg])
            if KG < 2.75:
                return
            S.op("dve", lambda e: e.tensor_tensor(lf, lf, t, ALU.subtract), reads=[Bg], writes=[Bg])
            if KG < 3:
                return
            lf2 = gt["lf"][0:L].rearrange("p c h -> p (c h)")
            S.op("pe", lambda e: e.matmul(psC[0:L, 256:256 + n * 4], Umat[0:L, 0:L], lf2, start=True, stop=True),
                 reads=[Bg, Bcst], writes=[Bc_st])
            if KG < 3.2:
                return
            b2 = gt["b"][0:L].rearrange("p c h -> p (c h)")
            S.op("dve", lambda e: e.tensor_copy(b2, psC[0:L, 256:256 + n * 4]), reads=[Bc_st], writes=[Bg])
            if KG < 3.4:
                return
            S.op("dve", lambda e: e.tensor_tensor(a, ig, b, ALU.subtract), reads=[Bg], writes=[Bg])

        gates(128, 8, lambda kc, c: bufY[:, kc, c * 128:(c + 1) * 128], [BY], g_prev)
        gates(128, 8, lambda kc, c: xT[:, kc, c * 128:(c + 1) * 128], [BxT], g_own)
        gates(4, 4, lambda kc, c: xT[:, kc, NT + 4 * c:NT + 4 * c + 4], [BxT], g_smp)
        f2 = lambda ap: ap.rearrange("p c h -> p (c h)")

        def stab(gt, n, L, Elast, m0, recur, X):
            b = gt["b"]; k4 = n * 4
            ps, pB = psB[:, 0:512], BpsB4[0]
            S.op("pe", lambda e, ps=ps: e.transpose(ps[0:k4, 0:L], f2(gt["a"][0:L]), ident[0:L, 0:L]), reads=[Bg, Bcst], writes=[pB], cost=0.3)
            S.op("dve", lambda e, ps=ps: e.tensor_copy(logdG[0][0:k4, 0:L], ps[0:k4, 0:L]), reads=[pB], writes=[BlogdG[0]])
            S.op("dve", lambda e: e.tensor_tensor_scan(logdG[1][0:k4, 0:L], logdG[0][0:k4, 0:L], logdG[0][0:k4, 0:L], NEG, ALU.max, ALU.max),
                 reads=[BlogdG[0]], writes=[BlogdG[1]])
            ps2, pB2 = psB[:, 512:1024], BpsB4[1]
            S.op("pe", lambda e, ps2=ps2: e.transpose(ps2[0:L, 0:k4], logdG[1][0:k4, 0:L], ident[0:k4, 0:k4]), reads=[BlogdG[1], Bcst], writes=[pB2], cost=0.3)
            S.op("dve", lambda e, ps2=ps2: e.tensor_copy(f2(X["cm"][0:L]), ps2[0:L, 0:k4]), reads=[pB2], writes=[Bpg])
            S.op("dve", lambda e: e.tensor_tensor(X["mloc"][0:L], b[0:L], X["cm"][0:L], ALU.add), reads=[Bg, Bpg], writes=[Bpg])
            S.op("pe", lambda e: e.matmul(psC[:, 0:k4], Elast[0:L, 0:128], f2(b[0:L]), start=True, stop=True), reads=[Bg, Bcst], writes=[Bc_g])
            S.op("dve", lambda e: e.tensor_copy(f2(X["BL"]), psC[:, 0:k4]), reads=[Bc_g], writes=[Bpg])
            S.op("pe", lambda e: e.matmul(psC[:, 32:32 + k4], Elast[0:L, 0:128], f2(X["mloc"][0:L]), start=True, stop=True),
                 reads=[Bpg, Bcst], writes=[Bc_g])
            S.op("dve", lambda e: e.tensor_copy(f2(X["ML"]), psC[:, 32:32 + k4]), reads=[Bc_g], writes=[Bpg])
            if recur:
                S.op("dve", lambda e: e.tensor_copy(X["mprev"][:, 0, :], m0), reads=[Bpg, Bcst], writes=[Bpg])
                for c in range(n):
                    S.op("dve", lambda e, c=c: e.tensor_tensor(X["t"][:, c, :], X["BL"][:, c, :], X["mprev"][:, c, :], ALU.add), reads=[Bpg], writes=[Bpg])
                    S.op("dve", lambda e, c=c: e.tensor_tensor(X["m"][:, c, :], X["t"][:, c, :], X["ML"][:, c, :], ALU.max), reads=[Bpg], writes=[Bpg])
                    if c + 1 < n:
                        S.op("dve", lambda e, c=c: e.tensor_copy(X["mprev"][:, c + 1, :], X["m"][:, c, :]), reads=[Bpg], writes=[Bpg])
            else:
                S.op("dve", lambda e: e.tensor_copy(X["mprev"], m0), reads=[Bpg, Bcst], writes=[Bpg])
                S.op("dve", lambda e: e.tensor_tensor(X["t"], X["BL"], X["mprev"], ALU.add), reads=[Bpg], writes=[Bpg])
                S.op("dve", lambda e: e.tensor_tensor(X["m"], X["t"], X["ML"], ALU.max), reads=[Bpg], writes=[Bpg])
            S.op("dve", lambda e: e.tensor_tensor(X["dec"], X["t"], X["m"], ALU.subtract), reads=[Bpg], writes=[Bpg])
            S.op("act", lambda e: e.activation(X["dec"], X["dec"], AF.Exp), reads=[Bpg], writes=[Bpg])
            S.op("dve", lambda e: e.tensor_tensor(X["inter"][0:L], b[0:L], X["mprev"][0:L], ALU.add), reads=[Bg, Bpg], writes=[Bpg])
            S.op("dve", lambda e: e.tensor_tensor(X["mrow"][0:L], X["mloc"][0:L], X["inter"][0:L], ALU.max), reads=[Bpg], writes=[Bpg])
            S.op("dve", lambda e: e.tensor_tensor(X["inter"][0:L], X["inter"][0:L], X["mrow"][0:L], ALU.subtract), reads=[Bpg], writes=[Bpg])
            S.op("act", lambda e: e.activation(X["inter"][0:L], X["inter"][0:L], AF.Exp), reads=[Bpg], writes=[Bpg])
            S.op("dve", lambda e: e.tensor_tensor(X["bm"][0:L], b[0:L], X["mrow"][0:L], ALU.subtract), reads=[Bg, Bpg], writes=[Bpg])
            S.op("act", lambda e: e.activation(X["emr"][0:L], X["mrow"][0:L], AF.Exp, scale=-1.0), reads=[Bpg], writes=[Bpg])

        S.op("dve", lambda e: e.memset(zero4, 0.0), writes=[Bpg])
        stab(g_prev, 8, 128, E127, zero4, True, pg)
        P_ = lambda k: pg[k]
        S.op("dve", lambda e: e.tensor_tensor(P_("G"), P_("ML"), P_("m"), ALU.subtract), reads=[Bpg], writes=[Bpg])
        S.op("act", lambda e: e.activation(P_("G"), P_("G"), AF.Exp), reads=[Bpg], writes=[Bpg])
        S.op("dve", lambda e: e.memset(P_("t")[:, 7, :], 1.0), reads=[Bpg], writes=[Bpg])
        for c in range(6, -1, -1):
            S.op("dve", lambda e, c=c: e.tensor_tensor(P_("t")[:, c, :], P_("t")[:, c + 1, :], P_("dec")[:, c + 1, :], ALU.mult), reads=[Bpg], writes=[Bpg])
        S.op("dve", lambda e: e.tensor_tensor(P_("G"), P_("G"), P_("t"), ALU.mult), reads=[Bpg], writes=[Bpg])
        S.op("dve", lambda e: e.tensor_tensor(P_("wG"), g_prev["a"], P_("BL"), ALU.add), reads=[Bg, Bpg], writes=[Bpg])
        S.op("dve", lambda e: e.tensor_tensor(P_("wG"), P_("wG"), P_("ML"), ALU.subtract), reads=[Bpg], writes=[Bpg])
        S.op("act", lambda e: e.activation(P_("wG"), P_("wG"), AF.Exp), reads=[Bpg], writes=[Bpg])
        S.op("dve", lambda e: e.tensor_tensor(P_("wG"), P_("wG"), P_("G"), ALU.mult), reads=[Bpg], writes=[Bpg])
        S.op("dve", lambda e: e.tensor_scalar(minit, P_("m")[:, 7, :], flag[:, 0:1], None, ALU.mult), reads=[Bpg, Bcst], writes=[Bpg])
        stab(g_own, 8, 128, E127, minit, True, og)
        S.dma("pool", f2(msmp), sm_d.partition_broadcast(128), writes=[Bcst], sembuf=Bcst)
        stab(g_smp, 4, 4, E3, msmp, False, sg_)

        if STOP == "G":
            S.barrier(); S.flush(block); return nc
        alloc_ptr[0] = work1
        qTh = T(BF16, [4, TOK]); BqT = Buf("dj_qTh")
        kTh = T(BF16, [4, TOK]); BkT = Buf("dj_kTh")
        kprev = T(BF16, [8, 512]); Bkprev = Buf("dj_kprev")
        vprev = T(BF16, [8, 512]); Bvprev = Buf("dj_vprev")
        Cst = T(F32, [4, 512]); BC = dB("Cst"); BC4 = [Buf("Cst%d" % i) for i in range(4)]
        Cbf = T(BF16, [4, 512]); BCb4 = [Buf("Cbf%d" % i) for i in range(4)]
        nst = T(F32, [4]); Bn = dB("nst")
        nbf = T(BF16, [4]); Bnb = Buf("nbf")
        mcol = T(F32, [1]); Bm = dB("mcol")
        dec = T(F32, [1]); Bdec = Buf("dec")
        logd = T(F32, [128]); Blogd = Buf("logd")
        DT = T(F32, [128]); BDT = Buf("DT")
        PT = T(BF16, [128]); BPT = Buf("PT")
        sm2 = [T(F32, [18]) for _ in range(2)]; Bsm2 = [Buf("smA"), Buf("smB")]
        stats = T(F32, [6]); mv = T(F32, [2]); Bst = Buf("stats")
        kw = T(BF16, [512]); Bkw = Buf("kw")
        assert alloc_ptr[0] <= ARENA_B
        save = alloc_ptr[0]
        alloc_ptr[0] = br_a_off
        ktok = T(BF16, [12, 512]); Bktok = Buf("dj_ktok")
        vtok = T(BF16, [12, 512]); Bvtok = Buf("dj_vtok")
        numI2 = [T(F32, [512]) for _ in range(2)]; BnumI2 = [Buf("numIA"), Buf("numIB")]
        t1 = T(F32, [512]); Bt1 = Buf("t1")
        hnorm = T(BF16, [512]); Bhn = Buf("hnorm")
        kwx = T(BF16, [512]); kw2 = [kw, kwx]; Bkw2 = [Bkw, Buf("kwx")]
        assert alloc_ptr[0] <= br_a_off + 16 * TOK * 2, alloc_ptr[0] - br_a_off
        alloc_ptr[0] = save

        class UC:
            pass

        def unit_ctx(L, gt, X, ci, h, ktok_ap, vtok_ap, Bkv, qcols, slot):
            u = UC()
            u.L, u.gt, u.X, u.ci, u.h, u.ktok, u.vtok, u.Bkv, u.qcols, u.slot = L, gt, X, ci, h, ktok_ap, vtok_ap, Bkv, qcols, slot
            u.qf = lambda kc: qTh[:, kc, qcols:qcols + L]
            u.kf = lambda kc: kTh[:, kc, qcols:qcols + L]
            u.numI = numI2[slot]; u.BnumI = BnumI2[slot]
            u.kw = kw2[slot]; u.Bkw = Bkw2[slot]
            u.sm = sm2[slot]; u.Bsm = Bsm2[slot]
            return u

        def unit_P(u):
            L, X, ci, h = u.L, u.X, u.ci, u.h
            a_col = u.gt["a"][0:L, ci, h:h + 1]
            bm = X["bm"][0:L, ci, h:h + 1]
            S.op("pe", lambda e: e.matmul(psC[0:L, 128:128 + L], bm.to_broadcast([L, L]), ident[0:L, 0:L], start=True, stop=True),
                 reads=[Bpg, Bcst], writes=[Bc_bm])
            S.op("dve", lambda e: e.scalar_tensor_tensor(logd[0:L, 0:L], psC[0:L, 128:128 + L], a_col, maskT[0:L, 0:L], ALU.add, ALU.add),
                 reads=[Bc_bm, Bg, Bcst], writes=[Blogd])
            S.op("act", lambda e: e.activation(DT[0:L, 0:L], logd[0:L, 0:L], AF.Exp), reads=[Blogd], writes=[BDT])
            mm_group(psC[0:L, 256:256 + L], [(u.kf(kc), u.qf(kc)) for kc in range(4)], [BqT, BkT], Bc_st)
            S.op("dve", lambda e: e.tensor_tensor(PT[0:L, 0:L], psC[0:L, 256:256 + L], DT[0:L, 0:L], ALU.mult),
                 reads=[Bc_st, BDT], writes=[BPT])
            S.op("pe", lambda e: e.matmul(psT32[0:L, :], PT[0:L, 0:L], u.vtok, start=True, stop=True),
                 reads=[BPT] + u.Bkv, writes=[BpsT])
            S.op("act", lambda e: e.activation(u.numI[0:L], psT32[0:L, :], AF.Copy), reads=[BpsT], writes=[u.BnumI], cost=0.7)
            S.op("pe", lambda e: e.matmul(psC[0:L, 384:385], PT[0:L, 0:L], onesb[0:L, 0:1], start=True, stop=True),
                 reads=[BPT, Bones], writes=[Bc_dq])
            S.op("dve", lambda e: e.tensor_copy(u.sm[0:L, 0:1], psC[0:L, 384:385]), reads=[Bc_dq], writes=[u.Bsm])
            S.op("dve", lambda e: e.tensor_scalar(u.kw[0:L], u.ktok, DT[0:L, L - 1:L], None, ALU.mult), reads=u.Bkv + [BDT], writes=[u.Bkw], cost=0.7)

        def unit_E1(u):
            L = u.L
            i = u.slot
            mm_group(psA[i][0:L, :], [(u.qf(kc), Cbf[:, kc, :]) for kc in range(4)], [BqT] + BCb4, BpsA[i])
            mm_group(psC[0:L, 385:386], [(u.qf(kc), nbf[:, kc:kc + 1]) for kc in range(4)], [BqT, Bnb], Bc_dq)
            S.op("dve", lambda e: e.tensor_copy(u.sm[0:L, 1:2], psC[0:L, 385:386]), reads=[Bc_dq], writes=[u.Bsm])

        def unit_CC(u, want_bf=True):
            L, X, ci, h = u.L, u.X, u.ci, u.h
            for kc in range(4):
                S.op("pe", lambda e, kc=kc: e.matmul(psB[:, kc * 512:(kc + 1) * 512], u.kw[0:L, kc * 128:(kc + 1) * 128], u.vtok,
                                                     start=True, stop=True), reads=[u.Bkw] + u.Bkv, writes=[BpsB4[kc]], cost=0.27)
            for kc in range(4):
                S.op("pe", lambda e, kc=kc: e.matmul(psC[:, 392 + kc:393 + kc], u.kw[0:L, kc * 128:(kc + 1) * 128], onesb[0:L, 0:1],
                                                     start=True, stop=True), reads=[u.Bkw, Bones], writes=[Bc_nup], signal=(kc == 3))
            dcol = X["dec"][:, ci, h:h + 1]
            for kc in range(4):
                S.op("dve", lambda e, kc=kc: e.scalar_tensor_tensor(Cst[:, kc, :], Cst[:, kc, :], dcol, psB[:, kc * 512:(kc + 1) * 512], ALU.mult, ALU.add),
                     reads=[BC4[kc], Bpg, BpsB4[kc]], writes=[BC4[kc]], cost=0.65)
                if want_bf:
                    S.op("act", lambda e, kc=kc: e.activation(Cbf[:, kc, :], Cst[:, kc, :], AF.Copy), reads=[BC4[kc]], writes=[BCb4[kc]], cost=0.55)
            S.op("dve", lambda e: e.scalar_tensor_tensor(nst, nst, dcol, psC[:, 392:396], ALU.mult, ALU.add), reads=[Bn, Bpg, Bc_nup], writes=[Bn])
            if want_bf:
                S.op("act", lambda e: e.activation(nbf, nst, AF.Copy), reads=[Bn], writes=[Bnb])

        def unit_E2(u):
            L, X, ci, h, qcols = u.L, u.X, u.ci, u.h, u.qcols
            sm = u.sm
            inter = X["inter"][0:L, ci, h:h + 1]; emr = X["emr"][0:L, ci, h:h + 1]
            den = sm[0:L, 2:3]; dabs = sm[0:L, 3:4]; r_ = sm[0:L, 4:5]; ir = sm[0:L, 5:6]
            lnv = sm[0:L, 6:7]; rstd = sm[0:L, 7:8]; nmr = sm[0:L, 8:9]
            st6 = sm[0:L, 10:16]; mvv = sm[0:L, 16:18]
            Bs = u.Bsm
            i = u.slot
            S.op("dve", lambda e: e.scalar_tensor_tensor(den, sm[0:L, 1:2], inter, sm[0:L, 0:1], ALU.mult, ALU.add), reads=[Bs, Bpg], writes=[Bs])
            S.op("dve", lambda e: e.scalar_tensor_tensor(dabs, den, -1.0, den, ALU.mult, ALU.max), reads=[Bs], writes=[Bs])
            S.op("dve", lambda e: e.tensor_tensor(dabs, dabs, emr, ALU.max), reads=[Bs, Bpg], writes=[Bs])
            S.op("dve", lambda e: e.reciprocal(r_, dabs), reads=[Bs], writes=[Bs])
            S.op("dve", lambda e: e.tensor_tensor(ir, inter, r_, ALU.mult), reads=[Bs, Bpg], writes=[Bs])
            S.op("act", lambda e: e.activation(t1[0:L], psA[i][0:L, :], AF.Copy, scale=ir), reads=[BpsA[i], Bs], writes=[Bt1], cost=0.8)
            S.op("dve", lambda e: e.scalar_tensor_tensor(u.numI[0:L], u.numI[0:L], r_, t1[0:L], ALU.mult, ALU.add),
                 reads=[u.BnumI, Bs, Bt1], writes=[u.BnumI], cost=0.75)
            S.op("dve", lambda e: e.bn_stats(st6, u.numI[0:L]), reads=[u.BnumI], writes=[Bs], cost=0.7)
            S.op("dve", lambda e: e.bn_aggr(mvv, st6), reads=[Bs], writes=[Bs])
            S.op("act", lambda e: e.activation(lnv, mvv[:, 1:2], AF.Ln, bias=LN_EPS), reads=[Bs], writes=[Bs])
            S.op("act", lambda e: e.activation(rstd, lnv, AF.Exp, scale=-0.5), reads=[Bs], writes=[Bs])
            S.op("dve", lambda e: e.scalar_tensor_tensor(nmr, mvv[:, 0:1], -1.0, rstd, ALU.mult, ALU.mult), reads=[Bs], writes=[Bs])
            S.op("act", lambda e: e.activation(hnorm[0:L], u.numI[0:L], AF.Identity, bias=nmr, scale=rstd),
                 reads=[u.BnumI, Bs], writes=[Bhn], cost=0.8)
            for kc in range(4):
                S.op("pe", lambda e, kc=kc: e.transpose(psT[:, kc * 128:kc * 128 + L], hnorm[0:L, kc * 128:(kc + 1) * 128],
                                                        identb[0:L, 0:L]),
                     reads=[Bhn, Bcst], writes=[BpsT], signal=(kc == 3))
            pv = psT[:, 0:512].rearrange("p (k l) -> p k l", k=4)[:, :, 0:L]
            S.op("dve", lambda e: e.tensor_tensor(br_m[:, 4 * h:4 * h + 4, qcols:qcols + L], pv,
                                                  nw[:, 4 * h:4 * h + 4, None].to_broadcast([128, 4, L]), ALU.mult),
                 reads=[BpsT, Bcst], writes=[Bbrm4[h]], cost=0.7)

        for h in range(1 if STOP == 'M0' else 4):
            for half in range(2):
                w, wB = load_w(w_in, C_QM + h * 512 + half * GW, GW)

                def epi_q(cc, c0, n, ps, pB, half=half):
                    S.op("act", lambda e: e.activation(qTh[:, half * 2 + cc, c0:c0 + n], ps, AF.Copy, scale=float(512 ** -0.5)),
                         reads=[pB], writes=[BqT])
                fmode(w, wB, GW, xrhs, [BxT], BLK, epi_q)
            for half in range(2):
                w, wB = load_w(w_in, C_KM + h * 512 + half * GW, GW)

                def epi_k(cc, c0, n, ps, pB, half=half):
                    S.op("act", lambda e: e.activation(kTh[:, half * 2 + cc, c0:c0 + n], ps, AF.Copy), reads=[pB], writes=[BkT])
                fmode(w, wB, GW, xrhs, [BxT], BLK, epi_k)
                for c in range(8):
                    tmode(w, wB, GW, lambda kc, c=c: bufY[:, kc, c * 128:(c + 1) * 128], [BY], 128,
                          lambda ps, pB, c=c, half=half: S.op("act", lambda e: e.activation(kprev[:, c, half * GW:(half + 1) * GW], ps, AF.Copy),
                                                              reads=[pB], writes=[Bkprev]))
            for c in range(8):
                for kc in range(4):
                    S.op("pe", lambda e, c=c, kc=kc: e.transpose(psT[:, kc * 128:(kc + 1) * 128], kTh[:, kc, c * 128:(c + 1) * 128], identb),
                         reads=[BkT, Bcst], writes=[BpsT], signal=(kc == 3))
                S.op("act", lambda e, c=c: e.activation(ktok[:, c, :], psT[:, 0:512], AF.Copy), reads=[BpsT], writes=[Bktok])
            for s_ in range(4):
                for kc in range(4):
                    S.op("pe", lambda e, s_=s_, kc=kc: e.transpose(psT[0:4, kc * 128:(kc + 1) * 128], kTh[:, kc, NT + 4 * s_:NT + 4 * s_ + 4], identb),
                         reads=[BkT, Bcst], writes=[BpsT], signal=(kc == 3))
                S.op("act", lambda e, s_=s_: e.activation(ktok[0:4, 8 + s_, :], psT[0:4, 0:512], AF.Copy), reads=[BpsT], writes=[Bktok])
            for half in range(2):
                w, wB = load_w(w_in, C_VM + h * 512 + half * GW, GW)
                for c in range(8):
                    tmode(w, wB, GW, lambda kc, c=c: bufY[:, kc, c * 128:(c + 1) * 128], [BY], 128,
                          lambda ps, pB, c=c, half=half: S.op("act", lambda e: e.activation(vprev[:, c, half * GW:(half + 1) * GW], ps, AF.Copy),
                                                              reads=[pB], writes=[Bvprev]))
                for c in range(8):
                    tmode(w, wB, GW, lambda kc, c=c: xT[:, kc, c * 128:(c + 1) * 128], [BxT], 128,
                          lambda ps, pB, c=c, half=half: S.op("act", lambda e: e.activation(vtok[:, c, half * GW:(half + 1) * GW], ps, AF.Copy),
                                                              reads=[pB], writes=[Bvtok]))
                for s_ in range(4):
                    tmode(w, wB, GW, lambda kc, s_=s_: xT[:, kc, NT + 4 * s_:NT + 4 * s_ + 4], [BxT], 4,
                          lambda ps, pB, s_=s_, half=half: S.op("act", lambda e: e.activation(vtok[0:4, 8 + s_, half * GW:(half + 1) * GW], ps, AF.Copy),
                                                                reads=[pB], writes=[Bvtok]))
            for c in range(8):
                kwb, kwB = kw2[c % 2], Bkw2[c % 2]
                S.op("dve", lambda e, kwb=kwb, c=c, h=h: e.tensor_scalar(kwb, kprev[:, c, :], pg["wG"][:, c, h:h + 1], None, ALU.mult),
                     reads=[Bkprev, Bpg], writes=[kwB])
                for kc in range(4):
                    S.op("pe", lambda e, kwb=kwb, c=c, kc=kc: e.matmul(psB[:, kc * 512:(kc + 1) * 512], kwb[:, kc * 128:(kc + 1) * 128],
                                                                      vprev[:, c, :], start=(c == 0), stop=(c == 7)),
                         reads=[kwB, Bvprev], writes=[BpsB4[kc]], signal=False)
                for kc in range(4):
                    S.op("pe", lambda e, kwb=kwb, c=c, kc=kc: e.matmul(psC[:, 392 + kc:393 + kc], kwb[:, kc * 128:(kc + 1) * 128],
                                                                      onesb[:, 0:1], start=(c == 0 and kc == 0), stop=(c == 7),
                                                                      skip_group_check=True),
                         reads=[kwB, Bones], writes=[Bc_nup], signal=(kc == 3))
            Cf = Cst.rearrange("p k v -> p (k v)")
            for kc in range(4):
                S.op("dve", lambda e, kc=kc: e.tensor_scalar(Cst[:, kc, :], psB[:, kc * 512:(kc + 1) * 512], flag[:, 0:1], None, ALU.mult),
                     reads=[BpsB4[kc], Bcst], writes=[BC4[kc]], cost=0.6)
                S.op("act", lambda e, kc=kc: e.activation(Cbf[:, kc, :], Cst[:, kc, :], AF.Copy), reads=[BC4[kc]], writes=[BCb4[kc]], cost=0.55)
            S.op("dve", lambda e: e.tensor_scalar(nst, psC[:, 392:396], flag[:, 0:1], None, ALU.mult), reads=[Bc_nup, Bcst], writes=[Bn])
            S.op("dve", lambda e, h=h: e.tensor_copy(mcol, minit[:, h:h + 1]), reads=[Bpg], writes=[Bm])
            S.op("act", lambda e: e.activation(nbf, nst, AF.Copy), reads=[Bn], writes=[Bnb])
            if os.environ.get("KDBG") == "pre":
                S.dma("sp", Cp_o[h].rearrange("(k p) v -> p k v", p=128), Cst, reads=BC4, sembuf=BC)
                S.dma("sp", np_o[h], nst, reads=[Bn], sembuf=Bn)
                S.dma("sp", mp_o[h:h + 1, :], mcol[0:1, :], reads=[Bm], sembuf=Bm)
                continue
            if h > 0:
                gl = [2 * (h - 1), 2 * (h - 1) + 1]
                bgst["gen"] = chain(gate_items(C_OM, AF.Sigmoid, br_m, Bbrm4, gl), gate_items(C_ZM, AF.Silu, br_m, Bbrm4, gl))
            us = [unit_ctx(128, g_own, og, c, h, ktok[:, c, :], vtok[:, c, :], [Bktok, Bvtok], c * 128, c % 2) for c in range(8)]
            unit_P(us[0])
            for c in range(8):
                unit_E1(us[c])
                unit_CC(us[c], want_bf=(c < 7))
                if c + 1 < 8:
                    unit_P(us[c + 1])
                unit_E2(us[c])
                bg_step(2)
            S.dma("sp", Cp_o[h].rearrange("(k p) v -> p k v", p=128), Cst, reads=BC4, sembuf=BC)
            S.dma("sp", np_o[h], nst, reads=[Bn], sembuf=Bn)
            S.dma("sp", mp_o[h:h + 1, :], og["m"][0:1, 7, h:h + 1], reads=[Bpg], sembuf=Bm)
            ss = [unit_ctx(4, g_smp, sg_, s_, h, ktok[0:4, 8 + s_, :], vtok[0:4, 8 + s_, :], [Bktok, Bvtok], NT + 4 * s_, s_ % 2) for s_ in range(4)]
            unit_P(ss[0])
            for s_ in range(4):
                S.dma("sp", Cst, sC_d[s_, h].rearrange("(k p) v -> p k v", p=128), writes=BC4, sembuf=BC)
                S.dma("sp", nst, sn_d[s_, h], writes=[Bn], sembuf=Bn)
                for kc in range(4):
                    S.op("act", lambda e, kc=kc: e.activation(Cbf[:, kc, :], Cst[:, kc, :], AF.Copy), reads=[BC4[kc]], writes=[BCb4[kc]], cost=0.55)
                S.op("act", lambda e: e.activation(nbf, nst, AF.Copy), reads=[Bn], writes=[Bnb])
                unit_E1(ss[s_])
                unit_CC(ss[s_], want_bf=False)
                S.dma("sp", Cs_o[s_, h].rearrange("(k p) v -> p k v", p=128), Cst, reads=BC4, sembuf=BC)
                S.dma("sp", ns_o[s_, h], nst, reads=[Bn], sembuf=Bn)
                S.dma("sp", ms_o[s_ * 4 + h:s_ * 4 + h + 1, :], sg_["m"][0:1, s_, h:h + 1], reads=[Bpg], sembuf=Bm)
                if s_ + 1 < 4:
                    unit_P(ss[s_ + 1])
                unit_E2(ss[s_])
                bg_step(2)
            bg_drain()

        if STOP in ("M", "M0"):
            S.barrier(); S.flush(block); return nc
        work2 = work1
        if STOP == "OZ":
            S.barrier(); S.flush(block); return nc
        S.barrier()
        alloc_ptr[0] = W0
        TOKA = 128 + TOK
        kT2 = T(BF16, [4, TOKA]); BkT2 = Buf("dj_kT2")
        vtokA = T(BF16, [9, 256]); BvA = Buf("dj_vtokA")
        vext = T(BF16, [9, 192]); Bvext = Buf("vext")
        qTa = T(BF16, [4, TOK]); BqTa = Buf("dj_qTa")
        pT = [T(BF16, [1024]) for _ in range(4)]; BpT = [Buf("pT%d" % i) for i in range(4)]
        Eprev = T(BF16, [1024]); Ecur = T(BF16, [1024]); Esb = T(BF16, [32]); Esn = T(BF16, [32])
        BEt = Buf("Etab")
        BpsBh = [Buf("psB_lo", excl=True), Buf("psB_hi", excl=True)]
        kbuf = T(BF16, [4, 256]); Bkbuf = dB("kbuf")
        vbuf = T(BF16, [4, 256]); Bvbuf = dB("vbuf")
        kdup = T(BF16, [128]); Bkdup = Buf("kdup")
        kTb = T(BF16, [4, 128]); BkTb = Buf("kTb")
        vbext = T(BF16, [4, 192]); Bvbext = Buf("vbext")
        vstok = T(BF16, [4, 256]); Bvstok = Buf("vstok")
        vexts = T(BF16, [4, 192]); Bvexts = Buf("vexts")
        kvout = T(F32, [512]); Bkvout = dB("kvout")
        kvs = T(F32, [256]); Bkvs = dB("kvs")
        rden = kvout; Brden = Bkvout
        rsh = T(F32, [512]); Brsh = Buf("rsh")
        assert alloc_ptr[0] <= ARENA_B, alloc_ptr[0]
        Bcp = dB("d2d")

        S.dma("pool", kbuf, sk_d.rearrange("s k c -> k s c"), writes=[Bkbuf], sembuf=Bkbuf)
        S.dma("pool", vbuf, sv_d.rearrange("s k c -> k s c"), writes=[Bvbuf], sembuf=Bvbuf)
        for s_ in range(4):
            S.dma("sp", kbs_o[s_, 0:124, :], sk_d[s_, 4:128, :], sembuf=Bcp)
            S.dma("sp", vbs_o[s_, 0:124, :], sv_d[s_, 4:128, :], sembuf=Bcp)
        S.op("dve", lambda e: e.memset(vext.rearrange("p a b -> p (a b)"), 1.0), writes=[Bvext])
        S.op("dve", lambda e: e.memset(vbext.rearrange("p a b -> p (a b)"), 1.0), writes=[Bvbext])
        S.op("dve", lambda e: e.memset(vexts.rearrange("p a b -> p (a b)"), 1.0), writes=[Bvexts])

        wk, wkB = load_w(w_in, C_KA, 256)
        KSRC = [(bufY, BY, 896, 128, 0), (xT, BxT, 0, 512, 128), (xT, BxT, 512, 512, 640), (xT, BxT, 1024, NSM, 1152)]
        for cc in range(2):
            for (src, srcB, c0, n, d0) in KSRC:
                ps, pB = next_psA()
                mm_group(ps[:, 0:n], [(wk[:, kc, cc * 128:(cc + 1) * 128], src[:, kc, c0:c0 + n]) for kc in range(16)], [wkB, srcB], pB)
                for (dr, hh_, sr) in ((slice(0, 64), 2 * cc, slice(0, 64)), (slice(64, 128), 2 * cc, slice(0, 64)),
                                      (slice(64, 128), 2 * cc + 1, slice(64, 128)), (slice(0, 64), 2 * cc + 1, slice(64, 128))):
                    S.op("act", lambda e, ps=ps, dr=dr, hh_=hh_, sr=sr, n=n, d0=d0: e.activation(kT2[dr, hh_, d0:d0 + n], ps[sr, 0:n], AF.Copy),
                         reads=[pB], writes=[BkT2])
        tmode(wk, wkB, 256, lambda kc: xT[:, kc, 896:1024], [BxT], 128,
              lambda ps, pB: S.op("act", lambda e: e.activation(kvout[:, 0:256], ps, AF.Copy), reads=[pB], writes=[Bkvout]))
        for s_ in range(4):
            tmode(wk, wkB, 256, lambda kc, s_=s_: xT[:, kc, NT + 4 * s_:NT + 4 * s_ + 4], [BxT], 4,
                  lambda ps, pB, s_=s_: S.op("act", lambda e: e.activation(kvs[0:4, :], ps, AF.Copy), reads=[pB], writes=[Bkvs]))
            S.dma("sp", kbs_o[s_, 124:128, :], kvs[0:4, :], reads=[Bkvs], sembuf=Bkvs)
        wv, wvB = load_w(w_in, C_VA, 256)
        VSRC = [(bufY, BY, 896)] + [(xT, BxT, c * 128) for c in range(8)]
        for bi_, (src, srcB, c0) in enumerate(VSRC):
            def epi_v(ps, pB, bi_=bi_):
                S.op("act", lambda e: e.activation(vtokA[:, bi_, :], ps, AF.Copy), reads=[pB], writes=[BvA])
                if bi_ == 8:
                    S.op("act", lambda e: e.activation(kvout[:, 256:512], ps, AF.Copy), reads=[pB], writes=[Bkvout])
            tmode(wv, wvB, 256, lambda kc, src=src, c0=c0: src[:, kc, c0:c0 + 128], [srcB], 128, epi_v)
        for s_ in range(4):
            def epi_vs(ps, pB, s_=s_):
                S.op("act", lambda e: e.activation(vstok[0:4, s_, :], ps, AF.Copy), reads=[pB], writes=[Bvstok])
                S.op("act", lambda e: e.activation(kvs[0:4, :], ps, AF.Copy), reads=[pB], writes=[Bkvs])
            tmode(wv, wvB, 256, lambda kc, s_=s_: xT[:, kc, NT + 4 * s_:NT + 4 * s_ + 4], [BxT], 4, epi_vs)
            S.dma("sp", vbs_o[s_, 124:128, :], kvs[0:4, :], reads=[Bkvs], sembuf=Bkvs)
        S.dma("sp", kbp_o, kvout[:, 0:256], reads=[Bkvout], sembuf=Bkvout)
        S.dma("sp", vbp_o, kvout[:, 256:512], reads=[Bkvout], sembuf=Bkvout)

        if STOP == "A1":
            S.barrier(); S.flush(block); return nc

        def attend_A(h, N, qc0, kbs, kbB, slot):
            for bi_, (kTap, vap, Eap, Kn) in enumerate(kbs):
                useflag = Kn < 0
                Kn = abs(Kn)
                pt, ptB = pT[slot * 2 + bi_], BpT[slot * 2 + bi_]
                for g in range(8):
                    hf = g % 2
                    pc = (g % 2) * 512 + (g // 2) * N
                    S.op("pe", lambda e, g=g, hf=hf, bi_=bi_, kTap=kTap, Kn=Kn, pc=pc: e.matmul(
                        psB[0:Kn, bi_ * 1024 + pc:bi_ * 1024 + pc + N], kTap[hf * 64:(hf + 1) * 64, :],
                        qTa[hf * 64:(hf + 1) * 64, g // 2, qc0:qc0 + N], start=True, stop=True),
                        reads=kbB + [BqTa], writes=[BpsBh[bi_]], signal=(g == 7))
                src = psB[0:Kn, bi_ * 1024:(bi_ + 1) * 1024].rearrange("p (t c) -> p t c", t=2)[:, :, 0:4 * N]
                dst = pt[0:Kn, 0:8 * N].rearrange("p (t c) -> p t c", t=2)
                S.op("act", lambda e, src=src, dst=dst: e.activation(dst, src, AF.Exp), reads=[BpsBh[bi_]], writes=[ptB], cost=1.0)
                S.op("dve", lambda e, pt=pt, Kn=Kn, Eap=Eap: e.tensor_tensor(pt[0:Kn, 0:8 * N], pt[0:Kn, 0:8 * N], Eap[0:Kn, 0:8 * N], ALU.mult),
                     reads=[ptB, BEt], writes=[ptB], cost=1.0)
                if useflag:
                    S.op("dve", lambda e, pt=pt: e.tensor_scalar(pt[:, 0:1024], pt[:, 0:1024], flag[:, 0:1], None, ALU.mult),
                         reads=[ptB, Bcst], writes=[ptB])
            return (h, N, qc0, kbs, kbB, slot)

        def attend_B(ctx):
            h, N, qc0, kbs, kbB, slot = ctx
            v3 = lambda ap: ap.rearrange("p (j n) -> p j n", n=N)
            for hf in range(2):
                ps, pB = psA[hf], BpsA[hf]
                pairs = []
                for bi_, (kTap, vap, Eap, Kn) in enumerate(kbs):
                    Kn = abs(Kn)
                    lo = 64 if hf == 0 else 0
                    rhs = pT[slot * 2 + bi_][0:Kn, hf * 4 * N:(hf + 1) * 4 * N]
                    pairs.append((vap[0:Kn, lo:lo + 128], rhs))
                mm_group(ps[:, 0:4 * N], pairs, kbB + [BpT[slot * 2], BpT[slot * 2 + 1]], pB)
                nr = slice(0, 64) if hf == 0 else slice(64, 128)
                dr = slice(64, 128) if hf == 0 else slice(0, 64)
                es_b = es[dr, 8 * h + hf:8 * h + 8:2][:, :, None].to_broadcast([64, 4, N])
                S.op("dve", lambda e, ps=ps, dr=dr, es_b=es_b: e.tensor_tensor(v3(rden[dr, 0:4 * N]), v3(ps[dr, 0:4 * N]), es_b, ALU.add),
                     reads=[pB, Bes], writes=[Brden])
                S.op("act", lambda e, dr=dr: e.activation(rden[dr, 0:4 * N], rden[dr, 0:4 * N], AF.Ln), reads=[Brden], writes=[Brden])
                S.op("act", lambda e, dr=dr, nr=nr: e.activation(rsh[nr, 0:4 * N], rden[dr, 0:4 * N], AF.Exp, scale=-1.0), reads=[Brden], writes=[Brsh])
                S.op("dve", lambda e, ps=ps, nr=nr: e.tensor_tensor(br_a[nr, 4 * h:4 * h + 4, qc0:qc0 + N], v3(ps[nr, 0:4 * N]),
                                                                  v3(rsh[nr, 0:4 * N]), ALU.mult),
                     reads=[pB, Brsh], writes=[Bbra4[h]])

        for h in range(4):
            bg_drain()
            if h == 0:
                bgst["gen"] = chain(gate_items(C_OM, AF.Sigmoid, br_m, Bbrm4, [6, 7]), gate_items(C_ZM, AF.Silu, br_m, Bbrm4, [6, 7]))
            else:
                bgst["gen"] = gate_items(C_ZA, AF.Silu, br_a, Bbra4, [2 * (h - 1), 2 * (h - 1) + 1])
            S.op("act", lambda e, h=h: e.activation(vext[:, :, 64:128], vtokA[:, :, h * 64:(h + 1) * 64], AF.Copy),
                 reads=[BvA], writes=[Bvext])
            for g in range(8):
                sl = -SLOPES[8 * h + g]
                c128 = (g % 2) * 512 + (g // 2) * 128
                c4 = (g % 2) * 16 + (g // 2) * 4
                for (Et, dt_, K_, n_, c_) in ((Eprev, dprev, 128, 128, c128), (Ecur, dcur, 128, 128, c128),
                                              (Esb, dsb, 128, 4, c4), (Esn, dsn, 4, 4, c4)):
                    S.op("act", lambda e, Et=Et, dt_=dt_, K_=K_, n_=n_, c_=c_, sl=sl: e.activation(Et[0:K_, c_:c_ + n_], dt_[0:K_, 0:n_], AF.Exp, scale=sl),
                         reads=[Bcst], writes=[BEt])
            for half in range(2):
                w, wB = load_w(w_in, C_QA + h * 512 + half * GW, GW)

                def epi_qa(cc, c0, n, ps, pB, half=half):
                    S.op("act", lambda e: e.activation(qTa[:, half * 2 + cc, c0:c0 + n], ps, AF.Copy, scale=0.125),
                         reads=[pB], writes=[BqTa])
                fmode(w, wB, GW, xrhs, [BxT], BLK, epi_qa)
            for s_ in range(4):
                S.op("act", lambda e, s_=s_, h=h: e.activation(kdup.rearrange("p (t d) -> p t d", t=2),
                                                              kbuf[:, s_, None, h * 64:(h + 1) * 64].to_broadcast([128, 2, 64]), AF.Copy),
                     reads=[Bkbuf], writes=[Bkdup])
                S.op("pe", lambda e: e.transpose(psT[:, 0:128], kdup, identb), reads=[Bkdup, Bcst], writes=[BpsT])
                S.op("act", lambda e, s_=s_: e.activation(kTb[:, s_, :], psT[:, 0:128], AF.Copy), reads=[BpsT], writes=[BkTb])
                S.op("act", lambda e, s_=s_, h=h: e.activation(vbext[:, s_, 64:128], vbuf[:, s_, h * 64:(h + 1) * 64], AF.Copy),
                     reads=[Bvbuf], writes=[Bvbext])
                S.op("act", lambda e, s_=s_, h=h: e.activation(vexts[0:4, s_, 64:128], vstok[0:4, s_, h * 64:(h + 1) * 64], AF.Copy),
                     reads=[Bvstok], writes=[Bvexts])
            blocks = []
            for qb in range(8):
                blocks.append((h, 128, qb * 128,
                               [(kT2[:, h, qb * 128:(qb + 1) * 128], vext[:, qb, :], Eprev, 128 if qb > 0 else -128),
                                (kT2[:, h, (qb + 1) * 128:(qb + 2) * 128], vext[:, qb + 1, :], Ecur, 128)], [BkT2, Bvext]))
            for s_ in range(4):
                blocks.append((h, 4, NT + 4 * s_,
                               [(kTb[:, s_, :], vbext[:, s_, :], Esb, 128),
                                (kT2[:, h, 1152 + 4 * s_:1152 + 4 * s_ + 4], vexts[:, s_, :], Esn, 4)],
                               [BkTb, Bvbext, BkT2, Bvexts]))
            prev_ctx = None
            for i, blk in enumerate(blocks):
                ctx = attend_A(*blk, slot=i % 2)
                bg_step(1)
                if prev_ctx is not None:
                    attend_B(prev_ctx)
                    bg_step(1)
                prev_ctx = ctx
            attend_B(prev_ctx)
        bg_drain()
        if STOP in ("A3", "A"):
            S.barrier(); S.flush(block); return nc
        S.barrier()
        alloc_ptr[0] = work2
        yT = bufY
        bgst["gen"] = gate_items(C_ZA, AF.Silu, br_a, Bbra4, [6, 7])
        bg_drain()
        sg = T(BF16, [2, TOK]); Bsg = Buf("dj_sg")
        ytmp = T(F32, [2, TOK]); Byt = Buf("ytmp")
        tmp2 = T(F32, [512]); Bt2 = Buf("tmp2")
        for g in range(8):
            def epi_sig(cc, c0, n, ps, pB):
                S.op("act", lambda e: e.activation(sg[:, cc, c0:c0 + n], ps, AF.Sigmoid), reads=[pB], writes=[Bsg])

            def epi_pm(cc, c0, n, ps, pB):
                S.op("dve", lambda e: e.tensor_tensor(ytmp[:, cc, c0:c0 + n], ps, sg[:, cc, c0:c0 + n], ALU.mult),
                     reads=[pB, Bsg], writes=[Byt])

            def epi_pa(cc, c0, n, ps, pB, g=g):
                S.op("dve", lambda e: e.tensor_tensor(tmp2[:, 0:n], ps, sg[:, cc, c0:c0 + n], ALU.mult), reads=[pB, Bsg], writes=[Bt2])
                S.op("dve", lambda e: e.tensor_tensor(yT[:, 2 * g + cc, c0:c0 + n], ytmp[:, cc, c0:c0 + n], tmp2[:, 0:n], ALU.add),
                     reads=[Bt2, Byt], writes=[BY])
            w, wB = load_w(w_in, C_GM + g * GW, GW)
            fmode(w, wB, GW, xrhs, [BxT], BLK, epi_sig)
            w, wB = load_w(w_pm, g * GW, GW)
            fmode(w, wB, GW, lambda kc, c0, n: br_m[:, kc, c0:c0 + n], Bbrm4, BLK, epi_pm)
            w, wB = load_w(w_in, C_GA + g * GW, GW)
            fmode(w, wB, GW, xrhs, [BxT], BLK, epi_sig)
            w, wB = load_w(w_pa, g * GW, GW)
            fmode(w, wB, GW, lambda kc, c0, n: br_a[:, kc, c0:c0 + n], Bbra4, BLK, epi_pa)

        if STOP == "P":
            S.barrier(); S.flush(block); return nc
        S.barrier()
        acc_p = V(br_a_off - 16 * TOK * 2, F32, [8, 2048])
        alloc_ptr[0] = work1
        acc_s = T(F32, [2048])
        lnw_t = T(F32, [2048]); lnb_t = T(F32, [2048]); Bln = dB("ln")
        xbs = [T(F32, [2048]) for _ in range(2)]; Bxbs = [dB("xb0"), dB("xb1")]
        st24 = T(F32, [24]); mv2 = T(F32, [8]); Bs2 = Buf("st2")
        Bacc = [Buf("dj_acc%d" % i) for i in range(9)]
        assert alloc_ptr[0] <= ARENA_B, alloc_ptr[0]
        S.dma("sp", lnw_t, lnw_d.partition_broadcast(128), writes=[Bln], sembuf=Bln)
        S.dma("sp", lnb_t, lnb_d.partition_broadcast(128), writes=[Bln], sembuf=Bln)
        accv = lambda tb: (acc_p[:, tb, :] if tb < 8 else acc_s)

        def ln_block(tb):
            M = 128 if tb < 8 else NSM
            a_ = accv(tb)[0:M]
            x_ = xbs[tb % 2][0:M]; Bx = Bxbs[tb % 2]
            S.dma("sp", x_, xtok_d[tb * 128:tb * 128 + M, :], writes=[Bx], sembuf=Bx)
            S.op("dve", lambda e: e.scalar_tensor_tensor(a_, x_, float(DN_ALPHA), a_, ALU.mult, ALU.add),
                 reads=[Bx, Bacc[tb]], writes=[Bacc[tb]])
            for c in range(4):
                S.op("dve", lambda e, c=c: e.bn_stats(st24[0:M, c * 6:(c + 1) * 6], a_[:, c * 512:(c + 1) * 512]),
                     reads=[Bacc[tb]], writes=[Bs2])
            S.op("dve", lambda e: e.bn_aggr(mv2[0:M, 0:2], st24[0:M, 0:24]), reads=[Bs2], writes=[Bs2])
            S.op("act", lambda e: e.activation(mv2[0:M, 2:3], mv2[0:M, 1:2], AF.Ln, bias=LN_EPS), reads=[Bs2], writes=[Bs2])
            S.op("act", lambda e: e.activation(mv2[0:M, 3:4], mv2[0:M, 2:3], AF.Exp, scale=-0.5), reads=[Bs2], writes=[Bs2])
            S.op("dve", lambda e: e.scalar_tensor_tensor(mv2[0:M, 4:5], mv2[0:M, 0:1], -1.0, mv2[0:M, 3:4], ALU.mult, ALU.mult),
                 reads=[Bs2], writes=[Bs2])
            S.op("act", lambda e: e.activation(x_, a_, AF.Identity, bias=mv2[0:M, 4:5], scale=mv2[0:M, 3:4]),
                 reads=[Bacc[tb], Bs2, Bx], writes=[Bx])
            S.op("dve", lambda e: e.tensor_tensor(x_, x_, lnw_t[0:M], ALU.mult), reads=[Bx, Bln], writes=[Bx])
            S.op("pool", lambda e: e.tensor_tensor(x_, x_, lnb_t[0:M], ALU.add), reads=[Bx, Bln], writes=[Bx])
            S.dma("sp", y_o[tb * 128:tb * 128 + M, :], x_, reads=[Bx], sembuf=Bx)

        def o_item(w, wB, g, tb):
            M = 128 if tb < 8 else NSM
            tmode(w, wB, GW, lambda kc, tb=tb, M=M: yT[:, kc, tb * 128:tb * 128 + M], [BY], M,
                  lambda ps, pB, tb=tb, M=M, g=g: S.op("act", lambda e: e.activation(accv(tb)[0:M, g * GW:(g + 1) * GW], ps, AF.Copy),
                                                       reads=[pB], writes=[Bacc[tb]], cost=0.5))
        for g in range(6):
            w, wB = load_w(w_o, g * GW, GW)
            for tb in range(9):
                o_item(w, wB, g, tb)
        w6, wB6 = load_w(w_o, 6 * GW, GW)
        w7, wB7 = load_w(w_o, 7 * GW, GW)
        for tb in range(9):
            o_item(w6, wB6, 6, tb)
            o_item(w7, wB7, 7, tb)
            ln_block(tb)

        S.barrier()
        S.flush(block)
    return nc


_NC = None


def _consts():
    c = np.zeros((128, NCST), np.float32)
    i = np.arange(128)
    c[:, 0:128] = np.eye(128)
    c[:, 128:256] = (i[:, None] <= i[None, :])
    c[:, 256:384] = np.where(i[None, :] <= i[:, None], 0.0, NEG)
    c[:, 384:512] = np.where(i[:, None] <= i[None, :], 0.0, NEG)
    c[127, 512:640] = 1.0
    c[3, 640:768] = 1.0
    k = i[:, None]; q = i[None, :]
    c[:, 768:896] = np.where(k > q, q + 128 - k, BIG)
    c[:, 896:1024] = np.where(k <= q, q - k, BIG)
    q4 = np.arange(4)[None, :]
    c[:, 1024:1028] = np.where(k > q4, 128 + q4 - k, BIG)
    k4 = np.arange(4)[:, None]
    c[0:4, 1028:1032] = np.where(k4 <= q4, q4 - k4, BIG)
    return c


def kernel(x_prompt, x_sample, state_mlstm_C, state_mlstm_n, state_mlstm_m, state_attn_k, state_attn_v,
           w_in, b_igate, b_fgate, mlstm_norm_w, attn_sinks, w_proj_m, w_proj_a, w_out, ln_w, ln_b):
    global _NC
    f = lambda a: np.ascontiguousarray(np.asarray(a, dtype=np.float32))
    x_prompt, x_sample = f(x_prompt), f(x_sample)
    if _NC is None:
        _NC = build_program()
    cst = _consts()
    shared = {"w_in": f(w_in), "w_proj_m": f(w_proj_m), "w_proj_a": f(w_proj_a), "w_out": f(w_out), "cst": cst,
              "b_igate": f(b_igate), "b_fgate": f(b_fgate), "nw": f(np.asarray(mlstm_norm_w).reshape(16, 128).T),
              "attn_sinks": f(attn_sinks), "ln_w": f(ln_w), "ln_b": f(ln_b)}
    sC = f(state_mlstm_C); sn = f(state_mlstm_n); sm = f(state_mlstm_m)
    sk = f(state_attn_k).reshape(32, 128, 256); sv = f(state_attn_v).reshape(32, 128, 256)
    in_maps = []
    for c in range(8):
        b, half = c // 2, c % 2
        xo = x_prompt[b, half * 1024:(half + 1) * 1024]
        xs = x_sample[4 * c:4 * c + 4].reshape(16, D)
        xtok = np.concatenate([xo, xs], 0)
        xp = x_prompt[b, 0:1024] if half == 1 else np.zeros((1024, D), np.float32)
        m = dict(shared)
        m["xT"] = f(xtok.T)
        m["xTp"] = f(xp.T)
        m["xtok"] = f(xtok)
        m["dprev0"] = f(cst[:, 768:896]) if half == 1 else np.full((128, 128), BIG, np.float32)
        m["flag"] = np.full((128, 1), float(half), np.float32)
        m["sC"] = f(sC[4 * c:4 * c + 4])
        m["sn"] = f(sn[4 * c:4 * c + 4].reshape(4, 4, 4, 128).transpose(0, 1, 3, 2))
        m["sm"] = f(sm[4 * c:4 * c + 4].reshape(16))
        m["sk"] = f(sk[4 * c:4 * c + 4])
        m["sv"] = f(sv[4 * c:4 * c + 4])
        in_maps.append(m)
    res = run_bass_kernel_spmd(_NC, in_maps[:KCORES], core_ids=list(range(KCORES)))
    R = list(res.results) + [res.results[0]] * (8 - KCORES)
    y_p = np.zeros((4, 2048, D), np.float32); y_s = np.zeros((32, 4, D), np.float32)
    C_p = np.zeros((4, 4, 512, 512), np.float32); n_p = np.zeros((4, 4, 512), np.float32); m_p = np.zeros((4, 4), np.float32)
    kb_p = np.zeros((4, 128, 4, 64), np.float32); vb_p = np.zeros((4, 128, 4, 64), np.float32)
    C_s = np.zeros((32, 4, 512, 512), np.float32); n_s = np.zeros((32, 4, 512), np.float32); m_s = np.zeros((32, 4), np.float32)
    kb_s = np.zeros((32, 128, 4, 64), np.float32); vb_s = np.zeros((32, 128, 4, 64), np.float32)
    for c in range(8):
        b, half = c // 2, c % 2
        r = R[c]
        y_p[b, half * 1024:(half + 1) * 1024] = r["y"][0:1024]
        y_s[4 * c:4 * c + 4] = r["y"][1024:1040].reshape(4, 4, D)
        if half == 1:
            C_p[b] = r["Cp"]
            n_p[b] = r["np"].transpose(0, 2, 1).reshape(4, 512)
            m_p[b] = r["mp"][:, 0]
            kb_p[b] = r["kbp"].reshape(128, 4, 64)
            vb_p[b] = r["vbp"].reshape(128, 4, 64)
        C_s[4 * c:4 * c + 4] = r["Cs"]
        n_s[4 * c:4 * c + 4] = r["ns"].transpose(0, 1, 3, 2).reshape(4, 4, 512)
        m_s[4 * c:4 * c + 4] = r["ms"].reshape(4, 4)
        kb_s[4 * c:4 * c + 4] = r["kbs"].reshape(4, 128, 4, 64)
        vb_s[4 * c:4 * c + 4] = r["vbs"].reshape(4, 128, 4, 64)
    return (y_p, y_s, C_p, n_p, m_p, kb_p, vb_p, C_s, n_s, m_s, kb_s, vb_s)
```

```python
import numpy as np
from contextlib import ExitStack
import concourse.bass as bass
import concourse.mybir as mybir
from concourse.bass_utils import run_bass_kernel_spmd

F32 = mybir.dt.float32
BF16 = mybir.dt.bfloat16
AF = mybir.ActivationFunctionType
ALU = mybir.AluOpType
AX = mybir.AxisListType

D = 2048
NT = 1024
NSM = 16
TOK = NT + NSM
IN_COLS = 18952
GW = 256
DN_ALPHA = 2.0 ** 0.25
LN_EPS = 1e-5
C_QM, C_KM, C_VM, C_OM, C_ZM = 0, 2048, 4096, 6144, 8192
C_QA, C_KA, C_VA, C_ZA = 10240, 12288, 12544, 12800
C_GM, C_GA, C_IG = 14848, 16896, 18944
SLOPES = [float(2.0 ** (-8.0 * (i + 1) / 32)) for i in range(32)]
NCST = 1032
BIG = 1.0e9
NEG = -1.0e30
import os
STOP = os.environ.get('KSTOP', '')
KG = float(os.environ.get('KG', '99'))
KCORES = int(os.environ.get('KCORES', '8'))


LOOKAHEAD = int(os.environ.get("KLOOK", "2048"))
SYNC_LAT = 0.12


class Buf:
    __slots__ = ("name", "w", "r", "dsem", "dcount", "excl")

    def __init__(self, name="", dsem=None, excl=False):
        self.excl = excl
        self.name = name
        self.w = None
        self.r = []
        self.dsem = dsem
        self.dcount = 0


class Op:
    __slots__ = ("idx", "eng", "builds", "deps", "cost", "is_dma", "sembuf", "epoch", "open", "ticket",
                 "start", "finish", "done", "barrier")

    def __init__(self, idx, eng, epoch):
        self.idx = idx; self.eng = eng; self.builds = []; self.deps = {}; self.cost = 0.0
        self.is_dma = False; self.sembuf = None; self.epoch = epoch; self.open = False
        self.ticket = None; self.start = 0.0; self.finish = 0.0; self.done = False; self.barrier = False


class EngQ:
    def __init__(self, name, sem):
        self.name = name
        self.sem = sem
        self.count = 0
        self.seen = {}
        self.q = []
        self.openop = None


class Sched:
    def __init__(self, nc, sems):
        self.nc = nc
        self.E = {k: EngQ(k, s) for k, s in sems.items()}
        self.dbufs = []
        self.ops = []
        self.epoch = 0

    def dbuf(self, name, sem):
        b = Buf(name, sem)
        self.dbufs.append(b)
        return b

    def _adddeps(self, op, reads, writes):
        ex = [b for b in reads if b.excl]
        if ex:
            reads = [b for b in reads if not b.excl]
            writes = list(writes) + ex
        RANK = {"raw": 3, "waw": 2, "wawdj": 1, "war": 0}

        def put(d, k):
            old_ = op.deps.get(d)
            if old_ is None or RANK[k] > RANK[old_]:
                op.deps[d] = k
        for b in reads:
            if b.w is not None and b.w is not op:
                put(b.w, "raw")
        for b in writes:
            if b.w is not None and b.w is not op:
                put(b.w, "wawdj" if b.name.startswith("dj_") else "waw")
            for r in b.r:
                if r is not op:
                    put(r, "war")
        for b in reads:
            if not b.r or b.r[-1] is not op:
                b.r.append(op)
        for b in writes:
            b.w = op
            b.r = []

    def op(self, eng, build, reads=(), writes=(), signal=True, cost=None):
        E = self.E[eng]
        o = E.openop
        if o is None:
            o = Op(len(self.ops), eng, self.epoch)
            self.ops.append(o)
        o.builds.append(build)
        if cost is None:
            cost = 0.07 if eng == "pe" else 0.25
        o.cost += cost
        self._adddeps(o, reads, writes)
        E.openop = None if signal else o

    def dma(self, eng, out, in_, reads=(), writes=(), sembuf=None, cost=3.0, **kw):
        o = Op(len(self.ops), eng, self.epoch)
        self.ops.append(o)
        o.builds.append(lambda e: e.dma_start(out=out, in_=in_, **kw))
        o.is_dma = True
        o.sembuf = sembuf
        o.cost = cost
        self._adddeps(o, reads, writes)

    def barrier(self):
        for E in self.E.values():
            assert E.openop is None
        self.epoch += 1

    def _schedule_epoch(self, ops, free_at):
        n = len(ops)
        pos = 0
        sched = [False] * n
        order = []
        while pos < n:
            best = None; best_t = None
            hi = min(n, pos + LOOKAHEAD)
            for j in range(pos, hi):
                if sched[j]:
                    continue
                o = ops[j]
                t = free_at[o.eng]
                ok = True
                for d in o.deps:
                    if d.epoch != o.epoch:
                        continue
                    if not d.done:
                        ok = False
                        break
                    f = d.finish + (SYNC_LAT if (d.eng != o.eng or d.is_dma) else 0.0)
                    if f > t:
                        t = f
                if not ok:
                    continue
                if best is None or t < best_t - 1e-9:
                    best, best_t = j, t
                    if t <= free_at[o.eng] + 1e-9 and j == pos:
                        break
            o = ops[best]
            sched[best] = True
            o.done = True
            o.start = best_t
            if o.is_dma:
                o.finish = best_t + o.cost
                free_at[o.eng] = best_t + 0.15
            else:
                o.finish = best_t + o.cost
                free_at[o.eng] = o.finish
            order.append(o)
            while pos < n and sched[pos]:
                pos += 1
        return order

    def flush(self, block):
        for E in self.E.values():
            assert E.openop is None
        nep = self.epoch + 1
        by_ep = [[] for _ in range(nep)]
        for o in self.ops:
            by_ep[o.epoch].append(o)
        free_at = {k: 0.0 for k in self.E}
        prog = {k: [] for k in self.E}
        for ep in range(nep):
            order = self._schedule_epoch(by_ep[ep], free_at)
            if os.environ.get("KDUMP") and ep == 0:
                for o in order[:400]:
                    print("SCHED", o.idx, o.eng, "dma" if o.is_dma else "", round(o.start, 2), round(o.finish, 2), round(o.cost, 2), len(o.builds))
            tmax = max(free_at.values())
            for o in order:
                if o.finish > tmax:
                    tmax = o.finish
            for k in free_at:
                free_at[k] = tmax
            for o in order:
                E = self.E[o.eng]
                if o.is_dma:
                    o.sembuf.dcount += 16
                    o.ticket = (o.sembuf.dsem, o.sembuf.dcount)
                else:
                    E.count += 1
                    o.ticket = (E.sem, E.count)
            for o in order:
                E = self.E[o.eng]
                need = {}
                for d, kind in o.deps.items():
                    if d.epoch != o.epoch:
                        continue
                    if (not d.is_dma) and d.eng == o.eng and (kind in ("war", "wawdj") or o.eng == "pe"):
                        continue
                    sem, val = d.ticket
                    if E.seen.get(sem, 0) >= val:
                        continue
                    if need.get(sem, 0) < val:
                        need[sem] = val
                for sem, val in need.items():
                    E.seen[sem] = val
                inc = (o.ticket[0], 16) if o.is_dma else (o.ticket[0], 1)
                prog[o.eng].append((list(need.items()), o.builds, inc))
            for E in self.E.values():
                need = {}
                for O in self.E.values():
                    if O is not E and O.count > 0 and E.seen.get(O.sem, 0) < O.count:
                        need[O.sem] = O.count
                for b in self.dbufs:
                    if b.dcount > 0 and E.seen.get(b.dsem, 0) < b.dcount:
                        need[b.dsem] = b.dcount
                for sem, val in need.items():
                    E.seen[sem] = val
                prog[E.name].append((list(need.items()), [], None))

        def run(name):
            def body(e):
                for waits, builds, inc in prog[name]:
                    for sem, val in waits:
                        e.wait_ge(sem, val)
                    ins = None
                    for b in builds:
                        ins = b(e)
                    if ins is not None and inc is not None:
                        ins.then_inc(inc[0], inc[1])
            return body
        block.tensor(run("pe"))
        block.scalar(run("act"))
        block.vector(run("dve"))
        block.gpsimd(run("pool"))
        block.sync(run("sp"))


def build_program():
    nc = bass.Bass("TRN2", target_bir_lowering=False)

    def din(name, shape):
        return nc.dram_tensor(name, list(shape), F32, kind="ExternalInput").ap()

    def dout(name, shape):
        return nc.dram_tensor(name, list(shape), F32, kind="ExternalOutput").ap()

    xT_d = din("xT", [D, TOK])
    xTp_d = din("xTp", [D, NT])
    xtok_d = din("xtok", [TOK, D])
    w_in = din("w_in", [D, IN_COLS])
    w_pm = din("w_proj_m", [D, D])
    w_pa = din("w_proj_a", [D, D])
    w_o = din("w_out", [D, D])
    cst_d = din("cst", [128, NCST])
    dprev0_d = din("dprev0", [128, 128])
    flag_d = din("flag", [128, 1])
    bi_d = din("b_igate", [4])
    bf_d = din("b_fgate", [4])
    nw_d = din("nw", [128, 16])
    sinks_d = din("attn_sinks", [32])
    lnw_d = din("ln_w", [D])
    lnb_d = din("ln_b", [D])
    sC_d = din("sC", [4, 4, 512, 512])
    sn_d = din("sn", [4, 4, 128, 4])
    sm_d = din("sm", [16])
    sk_d = din("sk", [4, 128, 256])
    sv_d = din("sv", [4, 128, 256])

    y_o = dout("y", [TOK, D])
    Cp_o = dout("Cp", [4, 512, 512])
    np_o = dout("np", [4, 128, 4])
    mp_o = dout("mp", [4, 1])
    kbp_o = dout("kbp", [128, 256])
    vbp_o = dout("vbp", [128, 256])
    Cs_o = dout("Cs", [4, 4, 512, 512])
    ns_o = dout("ns", [4, 4, 128, 4])
    ms_o = dout("ms", [16, 1])
    kbs_o = dout("kbs", [4, 128, 256])
    vbs_o = dout("vbs", [4, 128, 256])

    with ExitStack() as st:
        sems = {k: st.enter_context(nc.semaphore(k)) for k in ["pe", "act", "dve", "pool", "sp"]}
        S = Sched(nc, sems)
        ARENA_B = 207 * 1024 + 512
        arena = st.enter_context(nc.sbuf_tensor("arena", [128, ARENA_B // 2], BF16))
        alloc_ptr = [0]

        def alloc(nbytes):
            o = alloc_ptr[0]
            alloc_ptr[0] = o + ((nbytes + 31) // 32) * 32
            assert alloc_ptr[0] <= ARENA_B, ("SBUF overflow", alloc_ptr[0])
            return o

        def V(off, dt, shape):
            n = 1
            for s_ in shape:
                n *= s_
            sz = 4 if dt == F32 else 2
            ap = arena[:, off // 2: off // 2 + n * sz // 2]
            if dt == F32:
                ap = ap.bitcast(F32)
            if len(shape) == 2:
                return ap.rearrange("p (a b) -> p a b", a=shape[0])
            if len(shape) == 3:
                return ap.rearrange("p (a b c) -> p a b c", a=shape[0], b=shape[1])
            return ap

        def T(dt, shape):
            n = 1
            for s_ in shape:
                n *= s_
            return V(alloc(n * (4 if dt == F32 else 2)), dt, shape)

        def dB(name):
            return S.dbuf(name, st.enter_context(nc.semaphore("d_" + name)))

        psA = [st.enter_context(nc.psum_tensor(f"psA{i}", [128, 512], F32)) for i in range(2)]
        psB = st.enter_context(nc.psum_tensor("psB", [128, 2048], F32))
        psC = st.enter_context(nc.psum_tensor("psC", [128, 512], F32))
        psT = st.enter_context(nc.psum_tensor("psT", [128, 1024], BF16))
        psT32 = psT[:, :].bitcast(F32)
        BpsA = [Buf("psA0", excl=True), Buf("psA1", excl=True)]
        BpsB4 = [Buf("psB%d" % i, excl=True) for i in range(4)]
        BpsT = Buf("psT", excl=True)
        Bc_abc = Bc_bm = Bc_st = Bc_dq = Bc_md = Bc_nup = Bc_g = Buf("psC", excl=True)

        xT = T(BF16, [16, TOK]); BxT = dB("xT")
        bufY = T(BF16, [16, TOK]); BY = dB("bufY")
        br_m = T(BF16, [16, TOK]); Bbrm4 = [Buf("dj_br_m%d" % i) for i in range(4)]
        br_a_off = alloc(16 * TOK * 2)
        br_a = V(br_a_off, BF16, [16, TOK]); Bbra4 = [Buf("dj_br_a%d" % i) for i in range(4)]
        wsl = [T(BF16, [16, GW]) for _ in range(2)]
        Bw = [dB("w0"), dB("w1")]
        cst = T(F32, [NCST]); Bcst = dB("cst")
        identb = T(BF16, [128])
        onesb = T(BF16, [2]); Bones = Buf("ones")
        dprev0 = T(F32, [128])
        flag = T(F32, [1])
        bib = T(F32, [4]); bfb = T(F32, [4])
        nw = T(F32, [16])
        es = T(F32, [32]); Bes = Buf("es")
        gtmp2 = [T(BF16, [512]) for _ in range(2)]; Bgt2 = [Buf("gtA"), Buf("gtB")]
        work_off = alloc_ptr[0]
        block = st.enter_context(nc.Block())

        ident = cst[:, 0:128]; Umat = cst[:, 128:256]; mask = cst[:, 256:384]; maskT = cst[:, 384:512]
        E127 = cst[:, 512:640]; E3 = cst[:, 640:768]
        dprev = cst[:, 768:896]; dcur = cst[:, 896:1024]; dsb = cst[:, 1024:1028]; dsn = cst[:, 1028:1032]

        S.dma("pool", cst, cst_d, writes=[Bcst], sembuf=Bcst)
        S.dma("pool", dprev0, dprev0_d, writes=[Bcst], sembuf=Bcst)
        S.dma("pool", flag, flag_d, writes=[Bcst], sembuf=Bcst)
        S.dma("pool", bib, bi_d.partition_broadcast(128), writes=[Bcst], sembuf=Bcst)
        S.dma("pool", bfb, bf_d.partition_broadcast(128), writes=[Bcst], sembuf=Bcst)
        S.dma("pool", nw, nw_d, writes=[Bcst], sembuf=Bcst)
        S.dma("pool", es, sinks_d.partition_broadcast(128), writes=[Bcst], sembuf=Bcst)
        S.dma("pool", identb, cst_d[:, 0:128], writes=[Bcst], sembuf=Bcst)
        S.dma("pool", xT, xT_d.rearrange("(k p) t -> p k t", p=128), writes=[BxT], sembuf=BxT, cost=30.0)
        S.dma("pool", bufY[:, :, 0:NT], xTp_d.rearrange("(k p) t -> p k t", p=128), writes=[BY], sembuf=BY, cost=30.0)
        S.op("dve", lambda e: e.memset(onesb, 1.0), writes=[Bones])
        S.op("act", lambda e: e.activation(es, es, AF.Exp), reads=[Bcst], writes=[Bes])

        if STOP == "I":
            S.barrier(); S.flush(block); return nc
        wctr = [0]

        def load_w(src, c0, ncols):
            i = wctr[0] % 2
            wctr[0] += 1
            S.dma("pool", wsl[i][:, :, 0:ncols], src[:, c0:c0 + ncols].rearrange("(k p) c -> p k c", p=128),
                  writes=[Bw[i]], sembuf=Bw[i], cost=9.0)
            return wsl[i], Bw[i]

        pctr = [0]

        def next_psA():
            i = pctr[0] % 2
            pctr[0] += 1
            return psA[i], BpsA[i]

        def mm_group(out_ap, pairs, reads, wbuf, split=None):
            n = len(pairs)
            c_ = max(64, int(out_ap.shape[-1])) / 1900.0
            for i, (l, r) in enumerate(pairs):
                sig = (i == n - 1) or (split is not None and (i % split) == split - 1)
                S.op("pe", (lambda e, l=l, r=r, i=i: e.matmul(out_ap, l, r, start=(i == 0), stop=(i == n - 1))),
                     reads=reads, writes=[wbuf], signal=sig, cost=c_)

        def fmode(w, wB, ncols, rhs_fn, rhsB, blocks, epi, lhs_cols=None):
            for cc in range(ncols // 128):
                for (c0, n) in blocks:
                    ps, pB = next_psA()
                    if lhs_cols is None:
                        lf_ = lambda kc: w[:, kc, cc * 128:(cc + 1) * 128]
                    else:
                        lf_ = lambda kc: lhs_cols(kc, cc)
                    mm_group(ps[:, 0:n], [(lf_(kc), rhs_fn(kc, c0, n)) for kc in range(16)], [wB] + rhsB, pB, split=8)
                    epi(cc, c0, n, ps[:, 0:n], pB)

        def tmode(w, wB, ncols, lhs_fn, lhsB, M, epi):
            ps, pB = next_psA()
            mm_group(ps[0:M, 0:ncols], [(lhs_fn(kc), w[:, kc, 0:ncols]) for kc in range(16)], [wB] + lhsB, pB)
            epi(ps[0:M, 0:ncols], pB)

        BLK = [(0, 347), (347, 347), (694, 346)]
        xrhs = lambda kc, c0, n: xT[:, kc, c0:c0 + n]

        bgst = {"gen": None, "k": 0}

        def gate_items(cbase, func, dst, dstB4, g_list):
            loaded = {}
            for i, g in enumerate(g_list):
                if g not in loaded:
                    loaded[g] = load_w(w_in, cbase + g * GW, GW)
                if i + 1 < len(g_list) and g_list[i + 1] not in loaded:
                    loaded[g_list[i + 1]] = load_w(w_in, cbase + g_list[i + 1] * GW, GW)
                w, wB = loaded[g]
                for cc in range(2):
                    ch = g * 2 + cc
                    dB_ = dstB4[ch // 4]
                    for (c0, n) in BLK:
                        ps, pB = next_psA()
                        gi = bgst["k"] % 2
                        bgst["k"] += 1
                        gt_, gB_ = gtmp2[gi], Bgt2[gi]
                        mm_group(ps[:, 0:n], [(w[:, kc, cc * 128:(cc + 1) * 128], xT[:, kc, c0:c0 + n]) for kc in range(16)],
                                 [wB, BxT], pB, split=8)
                        S.op("act", lambda e, ps=ps, n=n, gt_=gt_: e.activation(gt_[:, 0:n], ps[:, 0:n], AF.Exp, scale=-1.0), reads=[pB], writes=[gB_], cost=0.5)
                        S.op("act", lambda e, n=n, gt_=gt_: e.activation(gt_[:, 0:n], gt_[:, 0:n], AF.Ln, bias=1.0), reads=[gB_], writes=[gB_], cost=0.5)
                        S.op("act", lambda e, n=n, gt_=gt_: e.activation(gt_[:, 0:n], gt_[:, 0:n], AF.Exp, scale=-1.0), reads=[gB_], writes=[gB_], cost=0.5)
                        S.op("dve", lambda e, ch=ch, c0=c0, n=n, gt_=gt_: e.tensor_tensor(dst[:, ch, c0:c0 + n], dst[:, ch, c0:c0 + n], gt_[:, 0:n], ALU.mult),
                             reads=[gB_, dB_], writes=[dB_], cost=0.6)
                        if func == AF.Silu:
                            S.op("dve", lambda e, ps=ps, ch=ch, c0=c0, n=n: e.tensor_tensor(dst[:, ch, c0:c0 + n], dst[:, ch, c0:c0 + n], ps[:, 0:n], ALU.mult),
                                 reads=[pB, dB_], writes=[dB_], cost=0.6)
                        yield

        def chain(*gens):
            for g_ in gens:
                for _ in g_:
                    yield

        def bg_step(k=1):
            for _ in range(k):
                if bgst["gen"] is None:
                    return
                try:
                    next(bgst["gen"])
                except StopIteration:
                    bgst["gen"] = None
                    return

        def bg_drain():
            while bgst["gen"] is not None:
                bg_step()

        W0 = work_off
        alloc_ptr[0] = W0
        g_prev = {k: T(F32, [8, 4]) for k in ("ig", "lf", "b", "a", "z", "t")}
        g_own = {k: T(F32, [8, 4]) for k in ("ig", "lf", "b", "a", "z", "t")}
        g_smp = {k: T(F32, [4, 4]) for k in ("ig", "lf", "b", "a", "z", "t")}
        Bg = Buf("gates")
        wg = T(BF16, [16, 8]); Bwg = dB("wg")
        SK = ("cm", "mloc", "BL", "ML", "m", "mprev", "mrow", "bm", "inter", "emr", "dec", "t")
        pg = {k: T(F32, [8, 4]) for k in SK + ("G", "wG")}
        og = {k: T(F32, [8, 4]) for k in SK}
        sg_ = {k: T(F32, [4, 4]) for k in SK}
        zero4 = T(F32, [4]); msmp = T(F32, [4, 4])
        minit = T(F32, [4]); Bpg = Buf("pg")
        logdG = [T(F32, [128]) for _ in range(2)]; BlogdG = [Buf("lgA"), Buf("lgB")]
        work1 = alloc_ptr[0]
        S.dma("pool", wg, w_in[:, C_IG:C_IG + 8].rearrange("(k p) c -> p k c", p=128), writes=[Bwg], sembuf=Bwg)

        def gates(L, n, lhs_fn, lhsB, gt):
            for c in range(n):
                mm_group(psC[0:L, c * 8:(c + 1) * 8], [(lhs_fn(kc, c), wg[:, kc, :]) for kc in range(16)],
                         [Bwg] + lhsB, Bc_g)
            if KG < 1:
                return
            pv = psC[0:L, 0:n * 8].rearrange("p (c g) -> p c g", g=8)
            bi3 = bib[0:L, None, :].to_broadcast([L, n, 4])
            bf3 = bfb[0:L, None, :].to_broadcast([L, n, 4])
            ig, lf, b, a, z, t = (gt[k][0:L] for k in ("ig", "lf", "b", "a", "z", "t"))
            S.op("dve", lambda e: e.tensor_tensor(ig, pv[:, :, 0:4], bi3, ALU.add), reads=[Bc_g, Bcst], writes=[Bg])
            S.op("dve", lambda e: e.tensor_tensor(z, pv[:, :, 4:8], bf3, ALU.add), reads=[Bc_g, Bcst], writes=[Bg])
            if KG < 2:
                return
            S.op("dve", lambda e: e.scalar_tensor_tensor(t, z, -1.0, z, ALU.mult, ALU.max), reads=[Bg], writes=[Bg])
            if KG < 2.2:
                return
            S.op("act", lambda e: e.activation(t, t, AF.Exp, scale=-1.0), reads=[Bg], writes=[Bg])
            if KG < 2.4:
                return
            S.op("act", lambda e: e.activation(t, t, AF.Ln, bias=1.0), reads=[Bg], writes=[Bg])
            if KG < 2.6:
                return
            S.op("dve", lambda e: e.tensor_scalar_min(lf, z, 0.0), reads=[Bg], writes=[Bg])
            if KG < 2.75:
                return
            S.op("dve", lambda e: e.tensor_tensor(lf, lf, t, ALU.subtract), reads=[Bg], writes=[Bg])
            if KG < 3:
                return
            lf2 = gt["lf"][0:L].rearrange("p c h -> p (c h)")
            S.op("pe", lambda e: e.matmul(psC[0:L, 256:256 + n * 4], Umat[0:L, 0:L], lf2, start=True, stop=True),
                 reads=[Bg, Bcst], writes=[Bc_st])
            if KG < 3.2:
                return
            b2 = gt["b"][0:L].rearrange("p c h -> p (c h)")
            S.op("dve", lambda e: e.tensor_copy(b2, psC[0:L, 256:256 + n * 4]), reads=[Bc_st], writes=[Bg])
            if KG < 3.4:
                return
            S.op("dve", lambda e: e.tensor_tensor(a, ig, b, ALU.subtract), reads=[Bg], writes=[Bg])

        gates(128, 8, lambda kc, c: bufY[:, kc, c * 128:(c + 1) * 128], [BY], g_prev)
        gates(128, 8, lambda kc, c: xT[:, kc, c * 128:(c + 1) * 128], [BxT], g_own)
        gates(4, 4, lambda kc, c: xT[:, kc, NT + 4 * c:NT + 4 * c + 4], [BxT], g_smp)
        f2 = lambda ap: ap.rearrange("p c h -> p (c h)")

        def stab(gt, n, L, Elast, m0, recur, X):
            b = gt["b"]; k4 = n * 4
            ps, pB = psB[:, 0:512], BpsB4[0]
            S.op("pe", lambda e, ps=ps: e.transpose(ps[0:k4, 0:L], f2(gt["a"][0:L]), ident[0:L, 0:L]), reads=[Bg, Bcst], writes=[pB], cost=0.3)
            S.op("dve", lambda e, ps=ps: e.tensor_copy(logdG[0][0:k4, 0:L], ps[0:k4, 0:L]), reads=[pB], writes=[BlogdG[0]])
            S.op("dve", lambda e: e.tensor_tensor_scan(logdG[1][0:k4, 0:L], logdG[0][0:k4, 0:L], logdG[0][0:k4, 0:L], NEG, ALU.max, ALU.max),
                 reads=[BlogdG[0]], writes=[BlogdG[1]])
            ps2, pB2 = psB[:, 512:1024], BpsB4[1]
            S.op("pe", lambda e, ps2=ps2: e.transpose(ps2[0:L, 0:k4], logdG[1][0:k4, 0:L], ident[0:k4, 0:k4]), reads=[BlogdG[1], Bcst], writes=[pB2], cost=0.3)
            S.op("dve", lambda e, ps2=ps2: e.tensor_copy(f2(X["cm"][0:L]), ps2[0:L, 0:k4]), reads=[pB2], writes=[Bpg])
            S.op("dve", lambda e: e.tensor_tensor(X["mloc"][0:L], b[0:L], X["cm"][0:L], ALU.add), reads=[Bg, Bpg], writes=[Bpg])
            S.op("pe", lambda e: e.matmul(psC[:, 0:k4], Elast[0:L, 0:128], f2(b[0:L]), start=True, stop=True), reads=[Bg, Bcst], writes=[Bc_g])
            S.op("dve", lambda e: e.tensor_copy(f2(X["BL"]), psC[:, 0:k4]), reads=[Bc_g], writes=[Bpg])
            S.op("pe", lambda e: e.matmul(psC[:, 32:32 + k4], Elast[0:L, 0:128], f2(X["mloc"][0:L]), start=True, stop=True),
                 reads=[Bpg, Bcst], writes=[Bc_g])
            S.op("dve", lambda e: e.tensor_copy(f2(X["ML"]), psC[:, 32:32 + k4]), reads=[Bc_g], writes=[Bpg])
            if recur:
                S.op("dve", lambda e: e.tensor_copy(X["mprev"][:, 0, :], m0), reads=[Bpg, Bcst], writes=[Bpg])
                for c in range(n):
                    S.op("dve", lambda e, c=c: e.tensor_tensor(X["t"][:, c, :], X["BL"][:, c, :], X["mprev"][:, c, :], ALU.add), reads=[Bpg], writes=[Bpg])
                    S.op("dve", lambda e, c=c: e.tensor_tensor(X["m"][:, c, :], X["t"][:, c, :], X["ML"][:, c, :], ALU.max), reads=[Bpg], writes=[Bpg])
                    if c + 1 < n:
                        S.op("dve", lambda e, c=c: e.tensor_copy(X["mprev"][:, c + 1, :], X["m"][:, c, :]), reads=[Bpg], writes=[Bpg])
            else:
                S.op("dve", lambda e: e.tensor_copy(X["mprev"], m0), reads=[Bpg, Bcst], writes=[Bpg])
                S.op("dve", lambda e: e.tensor_tensor(X["t"], X["BL"], X["mprev"], ALU.add), reads=[Bpg], writes=[Bpg])
                S.op("dve", lambda e: e.tensor_tensor(X["m"], X["t"], X["ML"], ALU.max), reads=[Bpg], writes=[Bpg])
            S.op("dve", lambda e: e.tensor_tensor(X["dec"], X["t"], X["m"], ALU.subtract), reads=[Bpg], writes=[Bpg])
            S.op("act", lambda e: e.activation(X["dec"], X["dec"], AF.Exp), reads=[Bpg], writes=[Bpg])
            S.op("dve", lambda e: e.tensor_tensor(X["inter"][0:L], b[0:L], X["mprev"][0:L], ALU.add), reads=[Bg, Bpg], writes=[Bpg])
            S.op("dve", lambda e: e.tensor_tensor(X["mrow"][0:L], X["mloc"][0:L], X["inter"][0:L], ALU.max), reads=[Bpg], writes=[Bpg])
            S.op("dve", lambda e: e.tensor_tensor(X["inter"][0:L], X["inter"][0:L], X["mrow"][0:L], ALU.subtract), reads=[Bpg], writes=[Bpg])
            S.op("act", lambda e: e.activation(X["inter"][0:L], X["inter"][0:L], AF.Exp), reads=[Bpg], writes=[Bpg])
            S.op("dve", lambda e: e.tensor_tensor(X["bm"][0:L], b[0:L], X["mrow"][0:L], ALU.subtract), reads=[Bg, Bpg], writes=[Bpg])
            S.op("act", lambda e: e.activation(X["emr"][0:L], X["mrow"][0:L], AF.Exp, scale=-1.0), reads=[Bpg], writes=[Bpg])

        S.op("dve", lambda e: e.memset(zero4, 0.0), writes=[Bpg])
        stab(g_prev, 8, 128, E127, zero4, True, pg)
        P_ = lambda k: pg[k]
        S.op("dve", lambda e: e.tensor_tensor(P_("G"), P_("ML"), P_("m"), ALU.subtract), reads=[Bpg], writes=[Bpg])
        S.op("act", lambda e: e.activation(P_("G"), P_("G"), AF.Exp), reads=[Bpg], writes=[Bpg])
        S.op("dve", lambda e: e.memset(P_("t")[:, 7, :], 1.0), reads=[Bpg], writes=[Bpg])
        for c in range(6, -1, -1):
            S.op("dve", lambda e, c=c: e.tensor_tensor(P_("t")[:, c, :], P_("t")[:, c + 1, :], P_("dec")[:, c + 1, :], ALU.mult), reads=[Bpg], writes=[Bpg])
        S.op("dve", lambda e: e.tensor_tensor(P_("G"), P_("G"), P_("t"), ALU.mult), reads=[Bpg], writes=[Bpg])
        S.op("dve", lambda e: e.tensor_tensor(P_("wG"), g_prev["a"], P_("BL"), ALU.add), reads=[Bg, Bpg], writes=[Bpg])
        S.op("dve", lambda e: e.tensor_tensor(P_("wG"), P_("wG"), P_("ML"), ALU.subtract), reads=[Bpg], writes=[Bpg])
        S.op("act", lambda e: e.activation(P_("wG"), P_("wG"), AF.Exp), reads=[Bpg], writes=[Bpg])
        S.op("dve", lambda e: e.tensor_tensor(P_("wG"), P_("wG"), P_("G"), ALU.mult), reads=[Bpg], writes=[Bpg])
        S.op("dve", lambda e: e.tensor_scalar(minit, P_("m")[:, 7, :], flag[:, 0:1], None, ALU.mult), reads=[Bpg, Bcst], writes=[Bpg])
        stab(g_own, 8, 128, E127, minit, True, og)
        S.dma("pool", f2(msmp), sm_d.partition_broadcast(128), writes=[Bcst], sembuf=Bcst)
        stab(g_smp, 4, 4, E3, msmp, False, sg_)

        if STOP == "G":
            S.barrier(); S.flush(block); return nc
        alloc_ptr[0] = work1
        qTh = T(BF16, [4, TOK]); BqT = Buf("dj_qTh")
        kTh = T(BF16, [4, TOK]); BkT = Buf("dj_kTh")
        kprev = T(BF16, [8, 512]); Bkprev = Buf("dj_kprev")
        vprev = T(BF16, [8, 512]); Bvprev = Buf("dj_vprev")
        Cst = T(F32, [4, 512]); BC = dB("Cst"); BC4 = [Buf("Cst%d" % i) for i in range(4)]
        Cbf = T(BF16, [4, 512]); BCb4 = [Buf("Cbf%d" % i) for i in range(4)]
        nst = T(F32, [4]); Bn = dB("nst")
        nbf = T(BF16, [4]); Bnb = Buf("nbf")
        mcol = T(F32, [1]); Bm = dB("mcol")
        dec = T(F32, [1]); Bdec = Buf("dec")
        logd = T(F32, [128]); Blogd = Buf("logd")
        DT = T(F32, [128]); BDT = Buf("DT")
        PT = T(BF16, [128]); BPT = Buf("PT")
        sm2 = [T(F32, [18]) for _ in range(2)]; Bsm2 = [Buf("smA"), Buf("smB")]
        stats = T(F32, [6]); mv = T(F32, [2]); Bst = Buf("stats")
        kw = T(BF16, [512]); Bkw = Buf("kw")
        assert alloc_ptr[0] <= ARENA_B
        save = alloc_ptr[0]
        alloc_ptr[0] = br_a_off
        ktok = T(BF16, [12, 512]); Bktok = Buf("dj_ktok")
        vtok = T(BF16, [12, 512]); Bvtok = Buf("dj_vtok")
        numI2 = [T(F32, [512]) for _ in range(2)]; BnumI2 = [Buf("numIA"), Buf("numIB")]
        t1 = T(F32, [512]); Bt1 = Buf("t1")
        hnorm = T(BF16, [512]); Bhn = Buf("hnorm")
        kwx = T(BF16, [512]); kw2 = [kw, kwx]; Bkw2 = [Bkw, Buf("kwx")]
        assert alloc_ptr[0] <= br_a_off + 16 * TOK * 2, alloc_ptr[0] - br_a_off
        alloc_ptr[0] = save

        class UC:
            pass

        def unit_ctx(L, gt, X, ci, h, ktok_ap, vtok_ap, Bkv, qcols, slot):
            u = UC()
            u.L, u.gt, u.X, u.ci, u.h, u.ktok, u.vtok, u.Bkv, u.qcols, u.slot = L, gt, X, ci, h, ktok_ap, vtok_ap, Bkv, qcols, slot
            u.qf = lambda kc: qTh[:, kc, qcols:qcols + L]
            u.kf = lambda kc: kTh[:, kc, qcols:qcols + L]
            u.numI = numI2[slot]; u.BnumI = BnumI2[slot]
            u.kw = kw2[slot]; u.Bkw = Bkw2[slot]
            u.sm = sm2[slot]; u.Bsm = Bsm2[slot]
            return u

        def unit_P(u):
            L, X, ci, h = u.L, u.X, u.ci, u.h
            a_col = u.gt["a"][0:L, ci, h:h + 1]
            bm = X["bm"][0:L, ci, h:h + 1]
            S.op("pe", lambda e: e.matmul(psC[0:L, 128:128 + L], bm.to_broadcast([L, L]), ident[0:L, 0:L], start=True, stop=True),
                 reads=[Bpg, Bcst], writes=[Bc_bm])
            S.op("dve", lambda e: e.scalar_tensor_tensor(logd[0:L, 0:L], psC[0:L, 128:128 + L], a_col, maskT[0:L, 0:L], ALU.add, ALU.add),
                 reads=[Bc_bm, Bg, Bcst], writes=[Blogd])
            S.op("act", lambda e: e.activation(DT[0:L, 0:L], logd[0:L, 0:L], AF.Exp), reads=[Blogd], writes=[BDT])
            mm_group(psC[0:L, 256:256 + L], [(u.kf(kc), u.qf(kc)) for kc in range(4)], [BqT, BkT], Bc_st)
            S.op("dve", lambda e: e.tensor_tensor(PT[0:L, 0:L], psC[0:L, 256:256 + L], DT[0:L, 0:L], ALU.mult),
                 reads=[Bc_st, BDT], writes=[BPT])
            S.op("pe", lambda e: e.matmul(psT32[0:L, :], PT[0:L, 0:L], u.vtok, start=True, stop=True),
                 reads=[BPT] + u.Bkv, writes=[BpsT])
            S.op("act", lambda e: e.activation(u.numI[0:L], psT32[0:L, :], AF.Copy), reads=[BpsT], writes=[u.BnumI], cost=0.7)
            S.op("pe", lambda e: e.matmul(psC[0:L, 384:385], PT[0:L, 0:L], onesb[0:L, 0:1], start=True, stop=True),
                 reads=[BPT, Bones], writes=[Bc_dq])
            S.op("dve", lambda e: e.tensor_copy(u.sm[0:L, 0:1], psC[0:L, 384:385]), reads=[Bc_dq], writes=[u.Bsm])
            S.op("dve", lambda e: e.tensor_scalar(u.kw[0:L], u.ktok, DT[0:L, L - 1:L], None, ALU.mult), reads=u.Bkv + [BDT], writes=[u.Bkw], cost=0.7)

        def unit_E1(u):
            L = u.L
            i = u.slot
            mm_group(psA[i][0:L, :], [(u.qf(kc), Cbf[:, kc, :]) for kc in range(4)], [BqT] + BCb4, BpsA[i])
            mm_group(psC[0:L, 385:386], [(u.qf(kc), nbf[:, kc:kc + 1]) for kc in range(4)], [BqT, Bnb], Bc_dq)
            S.op("dve", lambda e: e.tensor_copy(u.sm[0:L, 1:2], psC[0:L, 385:386]), reads=[Bc_dq], writes=[u.Bsm])

        def unit_CC(u, want_bf=True):
            L, X, ci, h = u.L, u.X, u.ci, u.h
            for kc in range(4):
                S.op("pe", lambda e, kc=kc: e.matmul(psB[:, kc * 512:(kc + 1) * 512], u.kw[0:L, kc * 128:(kc + 1) * 128], u.vtok,
                                                     start=True, stop=True), reads=[u.Bkw] + u.Bkv, writes=[BpsB4[kc]], cost=0.27)
            for kc in range(4):
                S.op("pe", lambda e, kc=kc: e.matmul(psC[:, 392 + kc:393 + kc], u.kw[0:L, kc * 128:(kc + 1) * 128], onesb[0:L, 0:1],
                                                     start=True, stop=True), reads=[u.Bkw, Bones], writes=[Bc_nup], signal=(kc == 3))
            dcol = X["dec"][:, ci, h:h + 1]
            for kc in range(4):
                S.op("dve", lambda e, kc=kc: e.scalar_tensor_tensor(Cst[:, kc, :], Cst[:, kc, :], dcol, psB[:, kc * 512:(kc + 1) * 512], ALU.mult, ALU.add),
                     reads=[BC4[kc], Bpg, BpsB4[kc]], writes=[BC4[kc]], cost=0.65)
                if want_bf:
                    S.op("act", lambda e, kc=kc: e.activation(Cbf[:, kc, :], Cst[:, kc, :], AF.Copy), reads=[BC4[kc]], writes=[BCb4[kc]], cost=0.55)
            S.op("dve", lambda e: e.scalar_tensor_tensor(nst, nst, dcol, psC[:, 392:396], ALU.mult, ALU.add), reads=[Bn, Bpg, Bc_nup], writes=[Bn])
            if want_bf:
                S.op("act", lambda e: e.activation(nbf, nst, AF.Copy), reads=[Bn], writes=[Bnb])

        def unit_E2(u):
            L, X, ci, h, qcols = u.L, u.X, u.ci, u.h, u.qcols
            sm = u.sm
            inter = X["inter"][0:L, ci, h:h + 1]; emr = X["emr"][0:L, ci, h:h + 1]
            den = sm[0:L, 2:3]; dabs = sm[0:L, 3:4]; r_ = sm[0:L, 4:5]; ir = sm[0:L, 5:6]
            lnv = sm[0:L, 6:7]; rstd = sm[0:L, 7:8]; nmr = sm[0:L, 8:9]
            st6 = sm[0:L, 10:16]; mvv = sm[0:L, 16:18]
            Bs = u.Bsm
            i = u.slot
            S.op("dve", lambda e: e.scalar_tensor_tensor(den, sm[0:L, 1:2], inter, sm[0:L, 0:1], ALU.mult, ALU.add), reads=[Bs, Bpg], writes=[Bs])
            S.op("dve", lambda e: e.scalar_tensor_tensor(dabs, den, -1.0, den, ALU.mult, ALU.max), reads=[Bs], writes=[Bs])
            S.op("dve", lambda e: e.tensor_tensor(dabs, dabs, emr, ALU.max), reads=[Bs, Bpg], writes=[Bs])
            S.op("dve", lambda e: e.reciprocal(r_, dabs), reads=[Bs], writes=[Bs])
            S.op("dve", lambda e: e.tensor_tensor(ir, inter, r_, ALU.mult), reads=[Bs, Bpg], writes=[Bs])
            S.op("act", lambda e: e.activation(t1[0:L], psA[i][0:L, :], AF.Copy, scale=ir), reads=[BpsA[i], Bs], writes=[Bt1], cost=0.8)
            S.op("dve", lambda e: e.scalar_tensor_tensor(u.numI[0:L], u.numI[0:L], r_, t1[0:L], ALU.mult, ALU.add),
                 reads=[u.BnumI, Bs, Bt1], writes=[u.BnumI], cost=0.75)
            S.op("dve", lambda e: e.bn_stats(st6, u.numI[0:L]), reads=[u.BnumI], writes=[Bs], cost=0.7)
            S.op("dve", lambda e: e.bn_aggr(mvv, st6), reads=[Bs], writes=[Bs])
            S.op("act", lambda e: e.activation(lnv, mvv[:, 1:2], AF.Ln, bias=LN_EPS), reads=[Bs], writes=[Bs])
            S.op("act", lambda e: e.activation(rstd, lnv, AF.Exp, scale=-0.5), reads=[Bs], writes=[Bs])
            S.op("dve", lambda e: e.scalar_tensor_tensor(nmr, mvv[:, 0:1], -1.0, rstd, ALU.mult, ALU.mult), reads=[Bs], writes=[Bs])
            S.op("act", lambda e: e.activation(hnorm[0:L], u.numI[0:L], AF.Identity, bias=nmr, scale=rstd),
                 reads=[u.BnumI, Bs], writes=[Bhn], cost=0.8)
            for kc in range(4):
                S.op("pe", lambda e, kc=kc: e.transpose(psT[:, kc * 128:kc * 128 + L], hnorm[0:L, kc * 128:(kc + 1) * 128],
                                                        identb[0:L, 0:L]),
                     reads=[Bhn, Bcst], writes=[BpsT], signal=(kc == 3))
            pv = psT[:, 0:512].rearrange("p (k l) -> p k l", k=4)[:, :, 0:L]
            S.op("dve", lambda e: e.tensor_tensor(br_m[:, 4 * h:4 * h + 4, qcols:qcols + L], pv,
                                                  nw[:, 4 * h:4 * h + 4, None].to_broadcast([128, 4, L]), ALU.mult),
                 reads=[BpsT, Bcst], writes=[Bbrm4[h]], cost=0.7)

        for h in range(1 if STOP == 'M0' else 4):
            for half in range(2):
                w, wB = load_w(w_in, C_QM + h * 512 + half * GW, GW)

                def epi_q(cc, c0, n, ps, pB, half=half):
                    S.op("act", lambda e: e.activation(qTh[:, half * 2 + cc, c0:c0 + n], ps, AF.Copy, scale=float(512 ** -0.5)),
                         reads=[pB], writes=[BqT])
                fmode(w, wB, GW, xrhs, [BxT], BLK, epi_q)
            for half in range(2):
                w, wB = load_w(w_in, C_KM + h * 512 + half * GW, GW)

                def epi_k(cc, c0, n, ps, pB, half=half):
                    S.op("act", lambda e: e.activation(kTh[:, half * 2 + cc, c0:c0 + n], ps, AF.Copy), reads=[pB], writes=[BkT])
                fmode(w, wB, GW, xrhs, [BxT], BLK, epi_k)
                for c in range(8):
                    tmode(w, wB, GW, lambda kc, c=c: bufY[:, kc, c * 128:(c + 1) * 128], [BY], 128,
                          lambda ps, pB, c=c, half=half: S.op("act", lambda e: e.activation(kprev[:, c, half * GW:(half + 1) * GW], ps, AF.Copy),
                                                              reads=[pB], writes=[Bkprev]))
            for c in range(8):
                for kc in range(4):
                    S.op("pe", lambda e, c=c, kc=kc: e.transpose(psT[:, kc * 128:(kc + 1) * 128], kTh[:, kc, c * 128:(c + 1) * 128], identb),
                         reads=[BkT, Bcst], writes=[BpsT], signal=(kc == 3))
                S.op("act", lambda e, c=c: e.activation(ktok[:, c, :], psT[:, 0:512], AF.Copy), reads=[BpsT], writes=[Bktok])
            for s_ in range(4):
                for kc in range(4):
                    S.op("pe", lambda e, s_=s_, kc=kc: e.transpose(psT[0:4, kc * 128:(kc + 1) * 128], kTh[:, kc, NT + 4 * s_:NT + 4 * s_ + 4], identb),
                         reads=[BkT, Bcst], writes=[BpsT], signal=(kc == 3))
                S.op("act", lambda e, s_=s_: e.activation(ktok[0:4, 8 + s_, :], psT[0:4, 0:512], AF.Copy), reads=[BpsT], writes=[Bktok])
            for half in range(2):
                w, wB = load_w(w_in, C_VM + h * 512 + half * GW, GW)
                for c in range(8):
                    tmode(w, wB, GW, lambda kc, c=c: bufY[:, kc, c * 128:(c + 1) * 128], [BY], 128,
                          lambda ps, pB, c=c, half=half: S.op("act", lambda e: e.activation(vprev[:, c, half * GW:(half + 1) * GW], ps, AF.Copy),
                                                              reads=[pB], writes=[Bvprev]))
                for c in range(8):
                    tmode(w, wB, GW, lambda kc, c=c: xT[:, kc, c * 128:(c + 1) * 128], [BxT], 128,
                          lambda ps, pB, c=c, half=half: S.op("act", lambda e: e.activation(vtok[:, c, half * GW:(half + 1) * GW], ps, AF.Copy),
                                                              reads=[pB], writes=[Bvtok]))
                for s_ in range(4):
                    tmode(w, wB, GW, lambda kc, s_=s_: xT[:, kc, NT + 4 * s_:NT + 4 * s_ + 4], [BxT], 4,
                          lambda ps, pB, s_=s_, half=half: S.op("act", lambda e: e.activation(vtok[0:4, 8 + s_, half * GW:(half + 1) * GW], ps, AF.Copy),
                                                                reads=[pB], writes=[Bvtok]))
            for c in range(8):
                kwb, kwB = kw2[c % 2], Bkw2[c % 2]
                S.op("dve", lambda e, kwb=kwb, c=c, h=h: e.tensor_scalar(kwb, kprev[:, c, :], pg["wG"][:, c, h:h + 1], None, ALU.mult),
                     reads=[Bkprev, Bpg], writes=[kwB])
                for kc in range(4):
                    S.op("pe", lambda e, kwb=kwb, c=c, kc=kc: e.matmul(psB[:, kc * 512:(kc + 1) * 512], kwb[:, kc * 128:(kc + 1) * 128],
                                                                      vprev[:, c, :], start=(c == 0), stop=(c == 7)),
                         reads=[kwB, Bvprev], writes=[BpsB4[kc]], signal=False)
                for kc in range(4):
                    S.op("pe", lambda e, kwb=kwb, c=c, kc=kc: e.matmul(psC[:, 392 + kc:393 + kc], kwb[:, kc * 128:(kc + 1) * 128],
                                                                      onesb[:, 0:1], start=(c == 0 and kc == 0), stop=(c == 7),
                                                                      skip_group_check=True),
                         reads=[kwB, Bones], writes=[Bc_nup], signal=(kc == 3))
            Cf = Cst.rearrange("p k v -> p (k v)")
            for kc in range(4):
                S.op("dve", lambda e, kc=kc: e.tensor_scalar(Cst[:, kc, :], psB[:, kc * 512:(kc + 1) * 512], flag[:, 0:1], None, ALU.mult),
                     reads=[BpsB4[kc], Bcst], writes=[BC4[kc]], cost=0.6)
                S.op("act", lambda e, kc=kc: e.activation(Cbf[:, kc, :], Cst[:, kc, :], AF.Copy), reads=[BC4[kc]], writes=[BCb4[kc]], cost=0.55)
            S.op("dve", lambda e: e.tensor_scalar(nst, psC[:, 392:396], flag[:, 0:1], None, ALU.mult), reads=[Bc_nup, Bcst], writes=[Bn])
            S.op("dve", lambda e, h=h: e.tensor_copy(mcol, minit[:, h:h + 1]), reads=[Bpg], writes=[Bm])
            S.op("act", lambda e: e.activation(nbf, nst, AF.Copy), reads=[Bn], writes=[Bnb])
            if os.environ.get("KDBG") == "pre":
                S.dma("sp", Cp_o[h].rearrange("(k p) v -> p k v", p=128), Cst, reads=BC4, sembuf=BC)
                S.dma("sp", np_o[h], nst, reads=[Bn], sembuf=Bn)
                S.dma("sp", mp_o[h:h + 1, :], mcol[0:1, :], reads=[Bm], sembuf=Bm)
                continue
            if h > 0:
                gl = [2 * (h - 1), 2 * (h - 1) + 1]
                bgst["gen"] = chain(gate_items(C_OM, AF.Sigmoid, br_m, Bbrm4, gl), gate_items(C_ZM, AF.Silu, br_m, Bbrm4, gl))
            us = [unit_ctx(128, g_own, og, c, h, ktok[:, c, :], vtok[:, c, :], [Bktok, Bvtok], c * 128, c % 2) for c in range(8)]
            unit_P(us[0])
            for c in range(8):
                unit_E1(us[c])
                unit_CC(us[c], want_bf=(c < 7))
                if c + 1 < 8:
                    unit_P(us[c + 1])
                unit_E2(us[c])
                bg_step(2)
            S.dma("sp", Cp_o[h].rearrange("(k p) v -> p k v", p=128), Cst, reads=BC4, sembuf=BC)
            S.dma("sp", np_o[h], nst, reads=[Bn], sembuf=Bn)
            S.dma("sp", mp_o[h:h + 1, :], og["m"][0:1, 7, h:h + 1], reads=[Bpg], sembuf=Bm)
            ss = [unit_ctx(4, g_smp, sg_, s_, h, ktok[0:4, 8 + s_, :], vtok[0:4, 8 + s_, :], [Bktok, Bvtok], NT + 4 * s_, s_ % 2) for s_ in range(4)]
            unit_P(ss[0])
            for s_ in range(4):
                S.dma("sp", Cst, sC_d[s_, h].rearrange("(k p) v -> p k v", p=128), writes=BC4, sembuf=BC)
                S.dma("sp", nst, sn_d[s_, h], writes=[Bn], sembuf=Bn)
                for kc in range(4):
                    S.op("act", lambda e, kc=kc: e.activation(Cbf[:, kc, :], Cst[:, kc, :], AF.Copy), reads=[BC4[kc]], writes=[BCb4[kc]], cost=0.55)
                S.op("act", lambda e: e.activation(nbf, nst, AF.Copy), reads=[Bn], writes=[Bnb])
                unit_E1(ss[s_])
                unit_CC(ss[s_], want_bf=False)
                S.dma("sp", Cs_o[s_, h].rearrange("(k p) v -> p k v", p=128), Cst, reads=BC4, sembuf=BC)
                S.dma("sp", ns_o[s_, h], nst, reads=[Bn], sembuf=Bn)
                S.dma("sp", ms_o[s_ * 4 + h:s_ * 4 + h + 1, :], sg_["m"][0:1, s_, h:h + 1], reads=[Bpg], sembuf=Bm)
                if s_ + 1 < 4:
                    unit_P(ss[s_ + 1])
                unit_E2(ss[s_])
                bg_step(2)
            bg_drain()

        if STOP in ("M", "M0"):
            S.barrier(); S.flush(block); return nc
        work2 = work1
        if STOP == "OZ":
            S.barrier(); S.flush(block); return nc
        S.barrier()
        alloc_ptr[0] = W0
        TOKA = 128 + TOK
        kT2 = T(BF16, [4, TOKA]); BkT2 = Buf("dj_kT2")
        vtokA = T(BF16, [9, 256]); BvA = Buf("dj_vtokA")
        vext = T(BF16, [9, 192]); Bvext = Buf("vext")
        qTa = T(BF16, [4, TOK]); BqTa = Buf("dj_qTa")
        pT = [T(BF16, [1024]) for _ in range(4)]; BpT = [Buf("pT%d" % i) for i in range(4)]
        Eprev = T(BF16, [1024]); Ecur = T(BF16, [1024]); Esb = T(BF16, [32]); Esn = T(BF16, [32])
        BEt = Buf("Etab")
        BpsBh = [Buf("psB_lo", excl=True), Buf("psB_hi", excl=True)]
        kbuf = T(BF16, [4, 256]); Bkbuf = dB("kbuf")
        vbuf = T(BF16, [4, 256]); Bvbuf = dB("vbuf")
        kdup = T(BF16, [128]); Bkdup = Buf("kdup")
        kTb = T(BF16, [4, 128]); BkTb = Buf("kTb")
        vbext = T(BF16, [4, 192]); Bvbext = Buf("vbext")
        vstok = T(BF16, [4, 256]); Bvstok = Buf("vstok")
        vexts = T(BF16, [4, 192]); Bvexts = Buf("vexts")
        kvout = T(F32, [512]); Bkvout = dB("kvout")
        kvs = T(F32, [256]); Bkvs = dB("kvs")
        rden = kvout; Brden = Bkvout
        rsh = T(F32, [512]); Brsh = Buf("rsh")
        assert alloc_ptr[0] <= ARENA_B, alloc_ptr[0]
        Bcp = dB("d2d")

        S.dma("pool", kbuf, sk_d.rearrange("s k c -> k s c"), writes=[Bkbuf], sembuf=Bkbuf)
        S.dma("pool", vbuf, sv_d.rearrange("s k c -> k s c"), writes=[Bvbuf], sembuf=Bvbuf)
        for s_ in range(4):
            S.dma("sp", kbs_o[s_, 0:124, :], sk_d[s_, 4:128, :], sembuf=Bcp)
            S.dma("sp", vbs_o[s_, 0:124, :], sv_d[s_, 4:128, :], sembuf=Bcp)
        S.op("dve", lambda e: e.memset(vext.rearrange("p a b -> p (a b)"), 1.0), writes=[Bvext])
        S.op("dve", lambda e: e.memset(vbext.rearrange("p a b -> p (a b)"), 1.0), writes=[Bvbext])
        S.op("dve", lambda e: e.memset(vexts.rearrange("p a b -> p (a b)"), 1.0), writes=[Bvexts])

        wk, wkB = load_w(w_in, C_KA, 256)
        KSRC = [(bufY, BY, 896, 128, 0), (xT, BxT, 0, 512, 128), (xT, BxT, 512, 512, 640), (xT, BxT, 1024, NSM, 1152)]
        for cc in range(2):
            for (src, srcB, c0, n, d0) in KSRC:
                ps, pB = next_psA()
                mm_group(ps[:, 0:n], [(wk[:, kc, cc * 128:(cc + 1) * 128], src[:, kc, c0:c0 + n]) for kc in range(16)], [wkB, srcB], pB)
                for (dr, hh_, sr) in ((slice(0, 64), 2 * cc, slice(0, 64)), (slice(64, 128), 2 * cc, slice(0, 64)),
                                      (slice(64, 128), 2 * cc + 1, slice(64, 128)), (slice(0, 64), 2 * cc + 1, slice(64, 128))):
                    S.op("act", lambda e, ps=ps, dr=dr, hh_=hh_, sr=sr, n=n, d0=d0: e.activation(kT2[dr, hh_, d0:d0 + n], ps[sr, 0:n], AF.Copy),
                         reads=[pB], writes=[BkT2])
        tmode(wk, wkB, 256, lambda kc: xT[:, kc, 896:1024], [BxT], 128,
              lambda ps, pB: S.op("act", lambda e: e.activation(kvout[:, 0:256], ps, AF.Copy), reads=[pB], writes=[Bkvout]))
        for s_ in range(4):
            tmode(wk, wkB, 256, lambda kc, s_=s_: xT[:, kc, NT + 4 * s_:NT + 4 * s_ + 4], [BxT], 4,
                  lambda ps, pB, s_=s_: S.op("act", lambda e: e.activation(kvs[0:4, :], ps, AF.Copy), reads=[pB], writes=[Bkvs]))
            S.dma("sp", kbs_o[s_, 124:128, :], kvs[0:4, :], reads=[Bkvs], sembuf=Bkvs)
        wv, wvB = load_w(w_in, C_VA, 256)
        VSRC = [(bufY, BY, 896)] + [(xT, BxT, c * 128) for c in range(8)]
        for bi_, (src, srcB, c0) in enumerate(VSRC):
            def epi_v(ps, pB, bi_=bi_):
                S.op("act", lambda e: e.activation(vtokA[:, bi_, :], ps, AF.Copy), reads=[pB], writes=[BvA])
                if bi_ == 8:
                    S.op("act", lambda e: e.activation(kvout[:, 256:512], ps, AF.Copy), reads=[pB], writes=[Bkvout])
            tmode(wv, wvB, 256, lambda kc, src=src, c0=c0: src[:, kc, c0:c0 + 128], [srcB], 128, epi_v)
        for s_ in range(4):
            def epi_vs(ps, pB, s_=s_):
                S.op("act", lambda e: e.activation(vstok[0:4, s_, :], ps, AF.Copy), reads=[pB], writes=[Bvstok])
                S.op("act", lambda e: e.activation(kvs[0:4, :], ps, AF.Copy), reads=[pB], writes=[Bkvs])
            tmode(wv, wvB, 256, lambda kc, s_=s_: xT[:, kc, NT + 4 * s_:NT + 4 * s_ + 4], [BxT], 4, epi_vs)
            S.dma("sp", vbs_o[s_, 124:128, :], kvs[0:4, :], reads=[Bkvs], sembuf=Bkvs)
        S.dma("sp", kbp_o, kvout[:, 0:256], reads=[Bkvout], sembuf=Bkvout)
        S.dma("sp", vbp_o, kvout[:, 256:512], reads=[Bkvout], sembuf=Bkvout)

        if STOP == "A1":
            S.barrier(); S.flush(block); return nc

        def attend_A(h, N, qc0, kbs, kbB, slot):
            for bi_, (kTap, vap, Eap, Kn) in enumerate(kbs):
                useflag = Kn < 0
                Kn = abs(Kn)
                pt, ptB = pT[slot * 2 + bi_], BpT[slot * 2 + bi_]
                for g in range(8):
                    hf = g % 2
                    pc = (g % 2) * 512 + (g // 2) * N
                    S.op("pe", lambda e, g=g, hf=hf, bi_=bi_, kTap=kTap, Kn=Kn, pc=pc: e.matmul(
                        psB[0:Kn, bi_ * 1024 + pc:bi_ * 1024 + pc + N], kTap[hf * 64:(hf + 1) * 64, :],
                        qTa[hf * 64:(hf + 1) * 64, g // 2, qc0:qc0 + N], start=True, stop=True),
                        reads=kbB + [BqTa], writes=[BpsBh[bi_]], signal=(g == 7))
                src = psB[0:Kn, bi_ * 1024:(bi_ + 1) * 1024].rearrange("p (t c) -> p t c", t=2)[:, :, 0:4 * N]
                dst = pt[0:Kn, 0:8 * N].rearrange("p (t c) -> p t c", t=2)
                S.op("act", lambda e, src=src, dst=dst: e.activation(dst, src, AF.Exp), reads=[BpsBh[bi_]], writes=[ptB], cost=1.0)
                S.op("dve", lambda e, pt=pt, Kn=Kn, Eap=Eap: e.tensor_tensor(pt[0:Kn, 0:8 * N], pt[0:Kn, 0:8 * N], Eap[0:Kn, 0:8 * N], ALU.mult),
                     reads=[ptB, BEt], writes=[ptB], cost=1.0)
                if useflag:
                    S.op("dve", lambda e, pt=pt: e.tensor_scalar(pt[:, 0:1024], pt[:, 0:1024], flag[:, 0:1], None, ALU.mult),
                         reads=[ptB, Bcst], writes=[ptB])
            return (h, N, qc0, kbs, kbB, slot)

        def attend_B(ctx):
            h, N, qc0, kbs, kbB, slot = ctx
            v3 = lambda ap: ap.rearrange("p (j n) -> p j n", n=N)
            for hf in range(2):
                ps, pB = psA[hf], BpsA[hf]
                pairs = []
                for bi_, (kTap, vap, Eap, Kn) in enumerate(kbs):
                    Kn = abs(Kn)
                    lo = 64 if hf == 0 else 0
                    rhs = pT[slot * 2 + bi_][0:Kn, hf * 4 * N:(hf + 1) * 4 * N]
                    pairs.append((vap[0:Kn, lo:lo + 128], rhs))
                mm_group(ps[:, 0:4 * N], pairs, kbB + [BpT[slot * 2], BpT[slot * 2 + 1]], pB)
                nr = slice(0, 64) if hf == 0 else slice(64, 128)
                dr = slice(64, 128) if hf == 0 else slice(0, 64)
                es_b = es[dr, 8 * h + hf:8 * h + 8:2][:, :, None].to_broadcast([64, 4, N])
                S.op("dve", lambda e, ps=ps, dr=dr, es_b=es_b: e.tensor_tensor(v3(rden[dr, 0:4 * N]), v3(ps[dr, 0:4 * N]), es_b, ALU.add),
                     reads=[pB, Bes], writes=[Brden])
                S.op("act", lambda e, dr=dr: e.activation(rden[dr, 0:4 * N], rden[dr, 0:4 * N], AF.Ln), reads=[Brden], writes=[Brden])
                S.op("act", lambda e, dr=dr, nr=nr: e.activation(rsh[nr, 0:4 * N], rden[dr, 0:4 * N], AF.Exp, scale=-1.0), reads=[Brden], writes=[Brsh])
                S.op("dve", lambda e, ps=ps, nr=nr: e.tensor_tensor(br_a[nr, 4 * h:4 * h + 4, qc0:qc0 + N], v3(ps[nr, 0:4 * N]),
                                                                  v3(rsh[nr, 0:4 * N]), ALU.mult),
                     reads=[pB, Brsh], writes=[Bbra4[h]])

        for h in range(4):
            bg_drain()
            if h == 0:
                bgst["gen"] = chain(gate_items(C_OM, AF.Sigmoid, br_m, Bbrm4, [6, 7]), gate_items(C_ZM, AF.Silu, br_m, Bbrm4, [6, 7]))
            else:
                bgst["gen"] = gate_items(C_ZA, AF.Silu, br_a, Bbra4, [2 * (h - 1), 2 * (h - 1) + 1])
            S.op("act", lambda e, h=h: e.activation(vext[:, :, 64:128], vtokA[:, :, h * 64:(h + 1) * 64], AF.Copy),
                 reads=[BvA], writes=[Bvext])
            for g in range(8):
                sl = -SLOPES[8 * h + g]
                c128 = (g % 2) * 512 + (g // 2) * 128
                c4 = (g % 2) * 16 + (g // 2) * 4
                for (Et, dt_, K_, n_, c_) in ((Eprev, dprev, 128, 128, c128), (Ecur, dcur, 128, 128, c128),
                                              (Esb, dsb, 128, 4, c4), (Esn, dsn, 4, 4, c4)):
                    S.op("act", lambda e, Et=Et, dt_=dt_, K_=K_, n_=n_, c_=c_, sl=sl: e.activation(Et[0:K_, c_:c_ + n_], dt_[0:K_, 0:n_], AF.Exp, scale=sl),
                         reads=[Bcst], writes=[BEt])
            for half in range(2):
                w, wB = load_w(w_in, C_QA + h * 512 + half * GW, GW)

                def epi_qa(cc, c0, n, ps, pB, half=half):
                    S.op("act", lambda e: e.activation(qTa[:, half * 2 + cc, c0:c0 + n], ps, AF.Copy, scale=0.125),
                         reads=[pB], writes=[BqTa])
                fmode(w, wB, GW, xrhs, [BxT], BLK, epi_qa)
            for s_ in range(4):
                S.op("act", lambda e, s_=s_, h=h: e.activation(kdup.rearrange("p (t d) -> p t d", t=2),
                                                              kbuf[:, s_, None, h * 64:(h + 1) * 64].to_broadcast([128, 2, 64]), AF.Copy),
                     reads=[Bkbuf], writes=[Bkdup])
                S.op("pe", lambda e: e.transpose(psT[:, 0:128], kdup, identb), reads=[Bkdup, Bcst], writes=[BpsT])
                S.op("act", lambda e, s_=s_: e.activation(kTb[:, s_, :], psT[:, 0:128], AF.Copy), reads=[BpsT], writes=[BkTb])
                S.op("act", lambda e, s_=s_, h=h: e.activation(vbext[:, s_, 64:128], vbuf[:, s_, h * 64:(h + 1) * 64], AF.Copy),
                     reads=[Bvbuf], writes=[Bvbext])
                S.op("act", lambda e, s_=s_, h=h: e.activation(vexts[0:4, s_, 64:128], vstok[0:4, s_, h * 64:(h + 1) * 64], AF.Copy),
                     reads=[Bvstok], writes=[Bvexts])
            blocks = []
            for qb in range(8):
                blocks.append((h, 128, qb * 128,
                               [(kT2[:, h, qb * 128:(qb + 1) * 128], vext[:, qb, :], Eprev, 128 if qb > 0 else -128),
                                (kT2[:, h, (qb + 1) * 128:(qb + 2) * 128], vext[:, qb + 1, :], Ecur, 128)], [BkT2, Bvext]))
            for s_ in range(4):
                blocks.append((h, 4, NT + 4 * s_,
                               [(kTb[:, s_, :], vbext[:, s_, :], Esb, 128),
                                (kT2[:, h, 1152 + 4 * s_:1152 + 4 * s_ + 4], vexts[:, s_, :], Esn, 4)],
                               [BkTb, Bvbext, BkT2, Bvexts]))
            prev_ctx = None
            for i, blk in enumerate(blocks):
                ctx = attend_A(*blk, slot=i % 2)
                bg_step(1)
                if prev_ctx is not None:
                    attend_B(prev_ctx)
                    bg_step(1)
                prev_ctx = ctx
            attend_B(prev_ctx)
        bg_drain()
        if STOP in ("A3", "A"):
            S.barrier(); S.flush(block); return nc
        S.barrier()
        alloc_ptr[0] = work2
        yT = bufY
        bgst["gen"] = gate_items(C_ZA, AF.Silu, br_a, Bbra4, [6, 7])
        bg_drain()
        sg = T(BF16, [2, TOK]); Bsg = Buf("dj_sg")
        ytmp = T(F32, [2, TOK]); Byt = Buf("ytmp")
        tmp2 = T(F32, [512]); Bt2 = Buf("tmp2")
        for g in range(8):
            def epi_sig(cc, c0, n, ps, pB):
                S.op("act", lambda e: e.activation(sg[:, cc, c0:c0 + n], ps, AF.Sigmoid), reads=[pB], writes=[Bsg])

            def epi_pm(cc, c0, n, ps, pB):
                S.op("dve", lambda e: e.tensor_tensor(ytmp[:, cc, c0:c0 + n], ps, sg[:, cc, c0:c0 + n], ALU.mult),
                     reads=[pB, Bsg], writes=[Byt])

            def epi_pa(cc, c0, n, ps, pB, g=g):
                S.op("dve", lambda e: e.tensor_tensor(tmp2[:, 0:n], ps, sg[:, cc, c0:c0 + n], ALU.mult), reads=[pB, Bsg], writes=[Bt2])
                S.op("dve", lambda e: e.tensor_tensor(yT[:, 2 * g + cc, c0:c0 + n], ytmp[:, cc, c0:c0 + n], tmp2[:, 0:n], ALU.add),
                     reads=[Bt2, Byt], writes=[BY])
            w, wB = load_w(w_in, C_GM + g * GW, GW)
            fmode(w, wB, GW, xrhs, [BxT], BLK, epi_sig)
            w, wB = load_w(w_pm, g * GW, GW)
            fmode(w, wB, GW, lambda kc, c0, n: br_m[:, kc, c0:c0 + n], Bbrm4, BLK, epi_pm)
            w, wB = load_w(w_in, C_GA + g * GW, GW)
            fmode(w, wB, GW, xrhs, [BxT], BLK, epi_sig)
            w, wB = load_w(w_pa, g * GW, GW)
            fmode(w, wB, GW, lambda kc, c0, n: br_a[:, kc, c0:c0 + n], Bbra4, BLK, epi_pa)

        if STOP == "P":
            S.barrier(); S.flush(block); return nc
        S.barrier()
        acc_p = V(br_a_off - 16 * TOK * 2, F32, [8, 2048])
        alloc_ptr[0] = work1
        acc_s = T(F32, [2048])
        lnw_t = T(F32, [2048]); lnb_t = T(F32, [2048]); Bln = dB("ln")
        NXB = 2
        xbs = [T(F32, [2048]) for _ in range(NXB)]; Bxbs = [dB("xb%d" % i) for i in range(NXB)]
        st24s = [T(F32, [24]) for _ in range(2)]; mv2s = [T(F32, [8]) for _ in range(2)]; Bs2s = [Buf("st2a"), Buf("st2b")]
        Bacc = [Buf("dj_acc%d" % i) for i in range(9)]
        assert alloc_ptr[0] <= ARENA_B, alloc_ptr[0]
        S.dma("sp", lnw_t, lnw_d.partition_broadcast(128), writes=[Bln], sembuf=Bln)
        S.dma("sp", lnb_t, lnb_d.partition_broadcast(128), writes=[Bln], sembuf=Bln)
        accv = lambda tb: (acc_p[:, tb, :] if tb < 8 else acc_s)

        def ln_block(tb):
            M = 128 if tb < 8 else NSM
            a_ = accv(tb)[0:M]
            x_ = xbs[tb % NXB][0:M]; Bx = Bxbs[tb % NXB]
            st24, mv2, Bs2 = st24s[tb % 2], mv2s[tb % 2], Bs2s[tb % 2]
            S.dma("sp", x_, xtok_d[tb * 128:tb * 128 + M, :], writes=[Bx], sembuf=Bx)
            S.op("dve", lambda e: e.scalar_tensor_tensor(a_, x_, float(DN_ALPHA), a_, ALU.mult, ALU.add),
                 reads=[Bx, Bacc[tb]], writes=[Bacc[tb]])
            for c in range(4):
                S.op("dve", lambda e, c=c: e.bn_stats(st24[0:M, c * 6:(c + 1) * 6], a_[:, c * 512:(c + 1) * 512]),
                     reads=[Bacc[tb]], writes=[Bs2])
            S.op("dve", lambda e: e.bn_aggr(mv2[0:M, 0:2], st24[0:M, 0:24]), reads=[Bs2], writes=[Bs2])
            S.op("act", lambda e: e.activation(mv2[0:M, 2:3], mv2[0:M, 1:2], AF.Ln, bias=LN_EPS), reads=[Bs2], writes=[Bs2])
            S.op("act", lambda e: e.activation(mv2[0:M, 3:4], mv2[0:M, 2:3], AF.Exp, scale=-0.5), reads=[Bs2], writes=[Bs2])
            S.op("dve", lambda e: e.scalar_tensor_tensor(mv2[0:M, 4:5], mv2[0:M, 0:1], -1.0, mv2[0:M, 3:4], ALU.mult, ALU.mult),
                 reads=[Bs2], writes=[Bs2])
            S.op("act", lambda e: e.activation(x_, a_, AF.Identity, bias=mv2[0:M, 4:5], scale=mv2[0:M, 3:4]),
                 reads=[Bacc[tb], Bs2, Bx], writes=[Bx])
            S.op("dve", lambda e: e.tensor_tensor(x_, x_, lnw_t[0:M], ALU.mult), reads=[Bx, Bln], writes=[Bx])
            S.op("pool", lambda e: e.tensor_tensor(x_, x_, lnb_t[0:M], ALU.add), reads=[Bx, Bln], writes=[Bx])
            S.dma("sp", y_o[tb * 128:tb * 128 + M, :], x_, reads=[Bx], sembuf=Bx)

        def o_item(w, wB, g, tb):
            M = 128 if tb < 8 else NSM
            tmode(w, wB, GW, lambda kc, tb=tb, M=M: yT[:, kc, tb * 128:tb * 128 + M], [BY], M,
                  lambda ps, pB, tb=tb, M=M, g=g: S.op("act", lambda e: e.activation(accv(tb)[0:M, g * GW:(g + 1) * GW], ps, AF.Copy),
                                                       reads=[pB], writes=[Bacc[tb]], cost=0.5))
        for g in range(6):
            w, wB = load_w(w_o, g * GW, GW)
            for tb in range(9):
                o_item(w, wB, g, tb)
        w6, wB6 = load_w(w_o, 6 * GW, GW)
        w7, wB7 = load_w(w_o, 7 * GW, GW)
        for tb in range(9):
            o_item(w6, wB6, 6, tb)
            o_item(w7, wB7, 7, tb)
            ln_block(tb)

        S.barrier()
        S.flush(block)
    return nc


_NC = None


def _consts():
    c = np.zeros((128, NCST), np.float32)
    i = np.arange(128)
    c[:, 0:128] = np.eye(128)
    c[:, 128:256] = (i[:, None] <= i[None, :])
    c[:, 256:384] = np.where(i[None, :] <= i[:, None], 0.0, NEG)
    c[:, 384:512] = np.where(i[:, None] <= i[None, :], 0.0, NEG)
    c[127, 512:640] = 1.0
    c[3, 640:768] = 1.0
    k = i[:, None]; q = i[None, :]
    c[:, 768:896] = np.where(k > q, q + 128 - k, BIG)
    c[:, 896:1024] = np.where(k <= q, q - k, BIG)
    q4 = np.arange(4)[None, :]
    c[:, 1024:1028] = np.where(k > q4, 128 + q4 - k, BIG)
    k4 = np.arange(4)[:, None]
    c[0:4, 1028:1032] = np.where(k4 <= q4, q4 - k4, BIG)
    return c


def kernel(x_prompt, x_sample, state_mlstm_C, state_mlstm_n, state_mlstm_m, state_attn_k, state_attn_v,
           w_in, b_igate, b_fgate, mlstm_norm_w, attn_sinks, w_proj_m, w_proj_a, w_out, ln_w, ln_b):
    global _NC
    f = lambda a: np.ascontiguousarray(np.asarray(a, dtype=np.float32))
    x_prompt, x_sample = f(x_prompt), f(x_sample)
    if _NC is None:
        _NC = build_program()
    cst = _consts()
    shared = {"w_in": f(w_in), "w_proj_m": f(w_proj_m), "w_proj_a": f(w_proj_a), "w_out": f(w_out), "cst": cst,
              "b_igate": f(b_igate), "b_fgate": f(b_fgate), "nw": f(np.asarray(mlstm_norm_w).reshape(16, 128).T),
              "attn_sinks": f(attn_sinks), "ln_w": f(ln_w), "ln_b": f(ln_b)}
    sC = f(state_mlstm_C); sn = f(state_mlstm_n); sm = f(state_mlstm_m)
    sk = f(state_attn_k).reshape(32, 128, 256); sv = f(state_attn_v).reshape(32, 128, 256)
    in_maps = []
    for c in range(8):
        b, half = c // 2, c % 2
        xo = x_prompt[b, half * 1024:(half + 1) * 1024]
        xs = x_sample[4 * c:4 * c + 4].reshape(16, D)
        xtok = np.concatenate([xo, xs], 0)
        xp = x_prompt[b, 0:1024] if half == 1 else np.zeros((1024, D), np.float32)
        m = dict(shared)
        m["xT"] = f(xtok.T)
        m["xTp"] = f(xp.T)
        m["xtok"] = f(xtok)
        m["dprev0"] = f(cst[:, 768:896]) if half == 1 else np.full((128, 128), BIG, np.float32)
        m["flag"] = np.full((128, 1), float(half), np.float32)
        m["sC"] = f(sC[4 * c:4 * c + 4])
        m["sn"] = f(sn[4 * c:4 * c + 4].reshape(4, 4, 4, 128).transpose(0, 1, 3, 2))
        m["sm"] = f(sm[4 * c:4 * c + 4].reshape(16))
        m["sk"] = f(sk[4 * c:4 * c + 4])
        m["sv"] = f(sv[4 * c:4 * c + 4])
        in_maps.append(m)
    res = run_bass_kernel_spmd(_NC, in_maps[:KCORES], core_ids=list(range(KCORES)))
    R = list(res.results) + [res.results[0]] * (8 - KCORES)
    y_p = np.zeros((4, 2048, D), np.float32); y_s = np.zeros((32, 4, D), np.float32)
    C_p = np.zeros((4, 4, 512, 512), np.float32); n_p = np.zeros((4, 4, 512), np.float32); m_p = np.zeros((4, 4), np.float32)
    kb_p = np.zeros((4, 128, 4, 64), np.float32); vb_p = np.zeros((4, 128, 4, 64), np.float32)
    C_s = np.zeros((32, 4, 512, 512), np.float32); n_s = np.zeros((32, 4, 512), np.float32); m_s = np.zeros((32, 4), np.float32)
    kb_s = np.zeros((32, 128, 4, 64), np.float32); vb_s = np.zeros((32, 128, 4, 64), np.float32)
    for c in range(8):
        b, half = c // 2, c % 2
        r = R[c]
        y_p[b, half * 1024:(half + 1) * 1024] = r["y"][0:1024]
        y_s[4 * c:4 * c + 4] = r["y"][1024:1040].reshape(4, 4, D)
        if half == 1:
            C_p[b] = r["Cp"]
            n_p[b] = r["np"].transpose(0, 2, 1).reshape(4, 512)
            m_p[b] = r["mp"][:, 0]
            kb_p[b] = r["kbp"].reshape(128, 4, 64)
            vb_p[b] = r["vbp"].reshape(128, 4, 64)
        C_s[4 * c:4 * c + 4] = r["Cs"]
        n_s[4 * c:4 * c + 4] = r["ns"].transpose(0, 1, 3, 2).reshape(4, 4, 512)
        m_s[4 * c:4 * c + 4] = r["ms"].reshape(4, 4)
        kb_s[4 * c:4 * c + 4] = r["kbs"].reshape(4, 128, 4, 64)
        vb_s[4 * c:4 * c + 4] = r["vbs"].reshape(4, 128, 4, 64)
    return (y_p, y_s, C_p, n_p, m_p, kb_p, vb_p, C_s, n_s, m_s, kb_s, vb_s)
```

```python
import numpy as np
from contextlib import ExitStack
import concourse.bass as bass
import concourse.mybir as mybir
from concourse.bass_utils import run_bass_kernel_spmd

F32 = mybir.dt.float32
BF16 = mybir.dt.bfloat16
AF = mybir.ActivationFunctionType
ALU = mybir.AluOpType
AX = mybir.AxisListType

D = 2048
NT = 1024
NSM = 16
TOK = NT + NSM
IN_COLS = 18952
GW = 256
DN_ALPHA = 2.0 ** 0.25
LN_EPS = 1e-5
C_QM, C_KM, C_VM, C_OM, C_ZM = 0, 2048, 4096, 6144, 8192
C_QA, C_KA, C_VA, C_ZA = 10240, 12288, 12544, 12800
C_GM, C_GA, C_IG = 14848, 16896, 18944
SLOPES = [float(2.0 ** (-8.0 * (i + 1) / 32)) for i in range(32)]
NCST = 1032
BIG = 1.0e9
NEG = -1.0e30
import os
STOP = os.environ.get('KSTOP', '')
KG = float(os.environ.get('KG', '99'))
KCORES = int(os.environ.get('KCORES', '8'))


LOOKAHEAD = int(os.environ.get("KLOOK", "2048"))
SYNC_LAT = 0.12


class Buf:
    __slots__ = ("name", "w", "r", "dsem", "dcount", "excl")

    def __init__(self, name="", dsem=None, excl=False):
        self.excl = excl
        self.name = name
        self.w = None
        self.r = []
        self.dsem = dsem
        self.dcount = 0


class Op:
    __slots__ = ("idx", "eng", "builds", "deps", "cost", "is_dma", "sembuf", "epoch", "open", "ticket",
                 "start", "finish", "done", "barrier")

    def __init__(self, idx, eng, epoch):
        self.idx = idx; self.eng = eng; self.builds = []; self.deps = {}; self.cost = 0.0
        self.is_dma = False; self.sembuf = None; self.epoch = epoch; self.open = False
        self.ticket = None; self.start = 0.0; self.finish = 0.0; self.done = False; self.barrier = False


class EngQ:
    def __init__(self, name, sem):
        self.name = name
        self.sem = sem
        self.count = 0
        self.seen = {}
        self.q = []
        self.openop = None


class Sched:
    def __init__(self, nc, sems):
        self.nc = nc
        self.E = {k: EngQ(k, s) for k, s in sems.items()}
        self.dbufs = []
        self.ops = []
        self.epoch = 0

    def dbuf(self, name, sem):
        b = Buf(name, sem)
        self.dbufs.append(b)
        return b

    def _adddeps(self, op, reads, writes):
        ex = [b for b in reads if b.excl]
        if ex:
            reads = [b for b in reads if not b.excl]
            writes = list(writes) + ex
        RANK = {"raw": 3, "waw": 2, "wawdj": 1, "war": 0}

        def put(d, k):
            old_ = op.deps.get(d)
            if old_ is None or RANK[k] > RANK[old_]:
                op.deps[d] = k
        for b in reads:
            if b.w is not None and b.w is not op:
                put(b.w, "raw")
        for b in writes:
            if b.w is not None and b.w is not op:
                put(b.w, "wawdj" if b.name.startswith("dj_") else "waw")
            for r in b.r:
                if r is not op:
                    put(r, "war")
        for b in reads:
            if not b.r or b.r[-1] is not op:
                b.r.append(op)
        for b in writes:
            b.w = op
            b.r = []

    def op(self, eng, build, reads=(), writes=(), signal=True, cost=None):
        E = self.E[eng]
        o = E.openop
        if o is None:
            o = Op(len(self.ops), eng, self.epoch)
            self.ops.append(o)
        o.builds.append(build)
        if cost is None:
            cost = 0.07 if eng == "pe" else 0.25
        o.cost += cost
        self._adddeps(o, reads, writes)
        E.openop = None if signal else o

    def dma(self, eng, out, in_, reads=(), writes=(), sembuf=None, cost=3.0, **kw):
        o = Op(len(self.ops), eng, self.epoch)
        self.ops.append(o)
        o.builds.append(lambda e: e.dma_start(out=out, in_=in_, **kw))
        o.is_dma = True
        o.sembuf = sembuf
        o.cost = cost
        self._adddeps(o, reads, writes)

    def barrier(self):
        for E in self.E.values():
            assert E.openop is None
        self.epoch += 1

    def _schedule_epoch(self, ops, free_at):
        n = len(ops)
        pos = 0
        sched = [False] * n
        order = []
        while pos < n:
            best = None; best_t = None
            hi = min(n, pos + LOOKAHEAD)
            for j in range(pos, hi):
                if sched[j]:
                    continue
                o = ops[j]
                t = free_at[o.eng]
                ok = True
                for d in o.deps:
                    if d.epoch != o.epoch:
                        continue
                    if not d.done:
                        ok = False
                        break
                    f = d.finish + (SYNC_LAT if (d.eng != o.eng or d.is_dma) else 0.0)
                    if f > t:
                        t = f
                if not ok:
                    continue
                if best is None or t < best_t - 1e-9:
                    best, best_t = j, t
                    if t <= free_at[o.eng] + 1e-9 and j == pos:
                        break
            o = ops[best]
            sched[best] = True
            o.done = True
            o.start = best_t
            if o.is_dma:
                o.finish = best_t + o.cost
                free_at[o.eng] = best_t + 0.15
            else:
                o.finish = best_t + o.cost
                free_at[o.eng] = o.finish
            order.append(o)
            while pos < n and sched[pos]:
                pos += 1
        return order

    def flush(self, block):
        for E in self.E.values():
            assert E.openop is None
        nep = self.epoch + 1
        by_ep = [[] for _ in range(nep)]
        for o in self.ops:
            by_ep[o.epoch].append(o)
        free_at = {k: 0.0 for k in self.E}
        prog = {k: [] for k in self.E}
        for ep in range(nep):
            order = self._schedule_epoch(by_ep[ep], free_at)
            if os.environ.get("KDUMP") and ep == 0:
                for o in order[:400]:
                    print("SCHED", o.idx, o.eng, "dma" if o.is_dma else "", round(o.start, 2), round(o.finish, 2), round(o.cost, 2), len(o.builds))
            tmax = max(free_at.values())
            for o in order:
                if o.finish > tmax:
                    tmax = o.finish
            for k in free_at:
                free_at[k] = tmax
            for o in order:
                E = self.E[o.eng]
                if o.is_dma:
                    o.sembuf.dcount += 16
                    o.ticket = (o.sembuf.dsem, o.sembuf.dcount)
                else:
                    E.count += 1
                    o.ticket = (E.sem, E.count)
            for o in order:
                E = self.E[o.eng]
                need = {}
                for d, kind in o.deps.items():
                    if d.epoch != o.epoch:
                        continue
                    if (not d.is_dma) and d.eng == o.eng and (kind in ("war", "wawdj") or o.eng == "pe"):
                        continue
                    sem, val = d.ticket
                    if E.seen.get(sem, 0) >= val:
                        continue
                    if need.get(sem, 0) < val:
                        need[sem] = val
                for sem, val in need.items():
                    E.seen[sem] = val
                inc = (o.ticket[0], 16) if o.is_dma else (o.ticket[0], 1)
                prog[o.eng].append((list(need.items()), o.builds, inc))
            for E in self.E.values():
                need = {}
                for O in self.E.values():
                    if O is not E and O.count > 0 and E.seen.get(O.sem, 0) < O.count:
                        need[O.sem] = O.count
                for b in self.dbufs:
                    if b.dcount > 0 and E.seen.get(b.dsem, 0) < b.dcount:
                        need[b.dsem] = b.dcount
                for sem, val in need.items():
                    E.seen[sem] = val
                prog[E.name].append((list(need.items()), [], None))

        def run(name):
            def body(e):
                for waits, builds, inc in prog[name]:
                    for sem, val in waits:
                        e.wait_ge(sem, val)
                    ins = None
                    for b in builds:
                        ins = b(e)
                    if ins is not None and inc is not None:
                        ins.then_inc(inc[0], inc[1])
            return body
        block.tensor(run("pe"))
        block.scalar(run("act"))
        block.vector(run("dve"))
        block.gpsimd(run("pool"))
        block.sync(run("sp"))


def build_program():
    nc = bass.Bass("TRN2", target_bir_lowering=False)

    def din(name, shape):
        return nc.dram_tensor(name, list(shape), F32, kind="ExternalInput").ap()

    def dout(name, shape):
        return nc.dram_tensor(name, list(shape), F32, kind="ExternalOutput").ap()

    xT_d = din("xT", [D, TOK])
    xTp_d = din("xTp", [D, NT])
    xtok_d = din("xtok", [TOK, D])
    w_in = din("w_in", [D, IN_COLS])
    w_pm = din("w_proj_m", [D, D])
    w_pa = din("w_proj_a", [D, D])
    w_o = din("w_out", [D, D])
    cst_d = din("cst", [128, NCST])
    dprev0_d = din("dprev0", [128, 128])
    flag_d = din("flag", [128, 1])
    bi_d = din("b_igate", [4])
    bf_d = din("b_fgate", [4])
    nw_d = din("nw", [128, 16])
    sinks_d = din("attn_sinks", [32])
    lnw_d = din("ln_w", [D])
    lnb_d = din("ln_b", [D])
    sC_d = din("sC", [4, 4, 512, 512])
    sn_d = din("sn", [4, 4, 128, 4])
    sm_d = din("sm", [16])
    sk_d = din("sk", [4, 128, 256])
    sv_d = din("sv", [4, 128, 256])

    y_o = dout("y", [TOK, D])
    Cp_o = dout("Cp", [4, 512, 512])
    np_o = dout("np", [4, 128, 4])
    mp_o = dout("mp", [4, 1])
    kbp_o = dout("kbp", [128, 256])
    vbp_o = dout("vbp", [128, 256])
    Cs_o = dout("Cs", [4, 4, 512, 512])
    ns_o = dout("ns", [4, 4, 128, 4])
    ms_o = dout("ms", [16, 1])
    kbs_o = dout("kbs", [4, 128, 256])
    vbs_o = dout("vbs", [4, 128, 256])

    with ExitStack() as st:
        sems = {k: st.enter_context(nc.semaphore(k)) for k in ["pe", "act", "dve", "pool", "sp"]}
        S = Sched(nc, sems)
        ARENA_B = 207 * 1024 + 512
        arena = st.enter_context(nc.sbuf_tensor("arena", [128, ARENA_B // 2], BF16))
        alloc_ptr = [0]

        def alloc(nbytes):
            o = alloc_ptr[0]
            alloc_ptr[0] = o + ((nbytes + 31) // 32) * 32
            assert alloc_ptr[0] <= ARENA_B, ("SBUF overflow", alloc_ptr[0])
            return o

        def V(off, dt, shape):
            n = 1
            for s_ in shape:
                n *= s_
            sz = 4 if dt == F32 else 2
            ap = arena[:, off // 2: off // 2 + n * sz // 2]
            if dt == F32:
                ap = ap.bitcast(F32)
            if len(shape) == 2:
                return ap.rearrange("p (a b) -> p a b", a=shape[0])
            if len(shape) == 3:
                return ap.rearrange("p (a b c) -> p a b c", a=shape[0], b=shape[1])
            return ap

        def T(dt, shape):
            n = 1
            for s_ in shape:
                n *= s_
            return V(alloc(n * (4 if dt == F32 else 2)), dt, shape)

        def dB(name):
            return S.dbuf(name, st.enter_context(nc.semaphore("d_" + name)))

        psA = [st.enter_context(nc.psum_tensor(f"psA{i}", [128, 512], F32)) for i in range(2)]
        psB = st.enter_context(nc.psum_tensor("psB", [128, 2048], F32))
        psC = st.enter_context(nc.psum_tensor("psC", [128, 512], F32))
        psT = st.enter_context(nc.psum_tensor("psT", [128, 1024], BF16))
        psT32 = psT[:, :].bitcast(F32)
        BpsA = [Buf("psA0", excl=True), Buf("psA1", excl=True)]
        BpsB4 = [Buf("psB%d" % i, excl=True) for i in range(4)]
        BpsT = Buf("psT", excl=True)
        Bc_abc = Bc_bm = Bc_st = Bc_dq = Bc_md = Bc_nup = Bc_g = Buf("psC", excl=True)

        xT = T(BF16, [16, TOK]); BxT = dB("xT")
        bufY = T(BF16, [16, TOK]); BY = dB("bufY")
        br_m = T(BF16, [16, TOK]); Bbrm4 = [Buf("dj_br_m%d" % i) for i in range(4)]
        br_a_off = alloc(16 * TOK * 2)
        br_a = V(br_a_off, BF16, [16, TOK]); Bbra4 = [Buf("dj_br_a%d" % i) for i in range(4)]
        wsl = [T(BF16, [16, GW]) for _ in range(2)]
        Bw = [dB("w0"), dB("w1")]
        cst = T(F32, [NCST]); Bcst = dB("cst")
        identb = T(BF16, [128])
        onesb = T(BF16, [2]); Bones = Buf("ones")
        dprev0 = T(F32, [128])
        flag = T(F32, [1])
        bib = T(F32, [4]); bfb = T(F32, [4])
        nw = T(F32, [16])
        es = T(F32, [32]); Bes = Buf("es")
        gtmp2 = [T(BF16, [512]) for _ in range(2)]; Bgt2 = [Buf("gtA"), Buf("gtB")]
        work_off = alloc_ptr[0]
        block = st.enter_context(nc.Block())

        ident = cst[:, 0:128]; Umat = cst[:, 128:256]; mask = cst[:, 256:384]; maskT = cst[:, 384:512]
        E127 = cst[:, 512:640]; E3 = cst[:, 640:768]
        dprev = cst[:, 768:896]; dcur = cst[:, 896:1024]; dsb = cst[:, 1024:1028]; dsn = cst[:, 1028:1032]

        S.dma("pool", cst, cst_d, writes=[Bcst], sembuf=Bcst)
        S.dma("pool", dprev0, dprev0_d, writes=[Bcst], sembuf=Bcst)
        S.dma("pool", flag, flag_d, writes=[Bcst], sembuf=Bcst)
        S.dma("pool", bib, bi_d.partition_broadcast(128), writes=[Bcst], sembuf=Bcst)
        S.dma("pool", bfb, bf_d.partition_broadcast(128), writes=[Bcst], sembuf=Bcst)
        S.dma("pool", nw, nw_d, writes=[Bcst], sembuf=Bcst)
        S.dma("pool", es, sinks_d.partition_broadcast(128), writes=[Bcst], sembuf=Bcst)
        S.dma("pool", identb, cst_d[:, 0:128], writes=[Bcst], sembuf=Bcst)
        S.dma("pool", xT, xT_d.rearrange("(k p) t -> p k t", p=128), writes=[BxT], sembuf=BxT, cost=30.0)
        S.dma("pool", bufY[:, :, 0:NT], xTp_d.rearrange("(k p) t -> p k t", p=128), reads=[BxT], writes=[BY], sembuf=BY, cost=30.0)
        S.op("dve", lambda e: e.memset(onesb, 1.0), writes=[Bones])
        S.op("act", lambda e: e.activation(es, es, AF.Exp), reads=[Bcst], writes=[Bes])

        if STOP == "I":
            S.barrier(); S.flush(block); return nc
        wctr = [0]

        def load_w(src, c0, ncols):
            i = wctr[0] % 2
            wctr[0] += 1
            S.dma("pool", wsl[i][:, :, 0:ncols], src[:, c0:c0 + ncols].rearrange("(k p) c -> p k c", p=128),
                  writes=[Bw[i]], sembuf=Bw[i], cost=9.0)
            return wsl[i], Bw[i]

        pctr = [0]

        def next_psA():
            i = pctr[0] % 2
            pctr[0] += 1
            return psA[i], BpsA[i]

        def mm_group(out_ap, pairs, reads, wbuf, split=None):
            n = len(pairs)
            c_ = max(64, int(out_ap.shape[-1])) / 1900.0
            for i, (l, r) in enumerate(pairs):
                sig = (i == n - 1) or (split is not None and (i % split) == split - 1)
                S.op("pe", (lambda e, l=l, r=r, i=i: e.matmul(out_ap, l, r, start=(i == 0), stop=(i == n - 1))),
                     reads=reads, writes=[wbuf], signal=sig, cost=c_)

        def fmode(w, wB, ncols, rhs_fn, rhsB, blocks, epi, lhs_cols=None):
            for cc in range(ncols // 128):
                for (c0, n) in blocks:
                    ps, pB = next_psA()
                    if lhs_cols is None:
                        lf_ = lambda kc: w[:, kc, cc * 128:(cc + 1) * 128]
                    else:
                        lf_ = lambda kc: lhs_cols(kc, cc)
                    mm_group(ps[:, 0:n], [(lf_(kc), rhs_fn(kc, c0, n)) for kc in range(16)], [wB] + rhsB, pB, split=8)
                    epi(cc, c0, n, ps[:, 0:n], pB)

        def tmode(w, wB, ncols, lhs_fn, lhsB, M, epi):
            ps, pB = next_psA()
            mm_group(ps[0:M, 0:ncols], [(lhs_fn(kc), w[:, kc, 0:ncols]) for kc in range(16)], [wB] + lhsB, pB)
            epi(ps[0:M, 0:ncols], pB)

        BLK = [(0, 347), (347, 347), (694, 346)]
        xrhs = lambda kc, c0, n: xT[:, kc, c0:c0 + n]

        bgst = {"gen": None, "k": 0}

        def gate_items(cbase, func, dst, dstB4, g_list):
            loaded = {}
            for i, g in enumerate(g_list):
                if g not in loaded:
                    loaded[g] = load_w(w_in, cbase + g * GW, GW)
                if i + 1 < len(g_list) and g_list[i + 1] not in loaded:
                    loaded[g_list[i + 1]] = load_w(w_in, cbase + g_list[i + 1] * GW, GW)
                w, wB = loaded[g]
                for cc in range(2):
                    ch = g * 2 + cc
                    dB_ = dstB4[ch // 4]
                    for (c0, n) in BLK:
                        ps, pB = next_psA()
                        gi = bgst["k"] % 2
                        bgst["k"] += 1
                        gt_, gB_ = gtmp2[gi], Bgt2[gi]
                        mm_group(ps[:, 0:n], [(w[:, kc, cc * 128:(cc + 1) * 128], xT[:, kc, c0:c0 + n]) for kc in range(16)],
                                 [wB, BxT], pB, split=8)
                        S.op("act", lambda e, ps=ps, n=n, gt_=gt_: e.activation(gt_[:, 0:n], ps[:, 0:n], AF.Exp, scale=-1.0), reads=[pB], writes=[gB_], cost=0.5)
                        S.op("act", lambda e, n=n, gt_=gt_: e.activation(gt_[:, 0:n], gt_[:, 0:n], AF.Ln, bias=1.0), reads=[gB_], writes=[gB_], cost=0.5)
                        S.op("act", lambda e, n=n, gt_=gt_: e.activation(gt_[:, 0:n], gt_[:, 0:n], AF.Exp, scale=-1.0), reads=[gB_], writes=[gB_], cost=0.5)
                        S.op("dve", lambda e, ch=ch, c0=c0, n=n, gt_=gt_: e.tensor_tensor(dst[:, ch, c0:c0 + n], dst[:, ch, c0:c0 + n], gt_[:, 0:n], ALU.mult),
                             reads=[gB_, dB_], writes=[dB_], cost=0.6)
                        if func == AF.Silu:
                            S.op("dve", lambda e, ps=ps, ch=ch, c0=c0, n=n: e.tensor_tensor(dst[:, ch, c0:c0 + n], dst[:, ch, c0:c0 + n], ps[:, 0:n], ALU.mult),
                                 reads=[pB, dB_], writes=[dB_], cost=0.6)
                        yield

        def chain(*gens):
            for g_ in gens:
                for _ in g_:
                    yield

        def bg_step(k=1):
            for _ in range(k):
                if bgst["gen"] is None:
                    return
                try:
                    next(bgst["gen"])
                except StopIteration:
                    bgst["gen"] = None
                    return

        def bg_drain():
            while bgst["gen"] is not None:
                bg_step()

        W0 = work_off
        alloc_ptr[0] = W0
        g_prev = {k: T(F32, [8, 4]) for k in ("ig", "lf", "b", "a", "z", "t")}
        g_own = {k: T(F32, [8, 4]) for k in ("ig", "lf", "b", "a", "z", "t")}
        g_smp = {k: T(F32, [4, 4]) for k in ("ig", "lf", "b", "a", "z", "t")}
        Bg = Buf("gates")
        wg = T(BF16, [16, 8]); Bwg = dB("wg")
        SK = ("cm", "mloc", "BL", "ML", "m", "mprev", "mrow", "bm", "inter", "emr", "dec", "t")
        pg = {k: T(F32, [8, 4]) for k in SK + ("G", "wG")}
        og = {k: T(F32, [8, 4]) for k in SK}
        sg_ = {k: T(F32, [4, 4]) for k in SK}
        zero4 = T(F32, [4]); msmp = T(F32, [4, 4])
        minit = T(F32, [4]); Bpg = Buf("pg")
        logdG = [T(F32, [128]) for _ in range(2)]; BlogdG = [Buf("lgA"), Buf("lgB")]
        work1 = alloc_ptr[0]
        S.dma("pool", wg, w_in[:, C_IG:C_IG + 8].rearrange("(k p) c -> p k c", p=128), writes=[Bwg], sembuf=Bwg)

        def gates(L, n, lhs_fn, lhsB, gt):
            for c in range(n):
                mm_group(psC[0:L, c * 8:(c + 1) * 8], [(lhs_fn(kc, c), wg[:, kc, :]) for kc in range(16)],
                         [Bwg] + lhsB, Bc_g)
            if KG < 1:
                return
            pv = psC[0:L, 0:n * 8].rearrange("p (c g) -> p c g", g=8)
            bi3 = bib[0:L, None, :].to_broadcast([L, n, 4])
            bf3 = bfb[0:L, None, :].to_broadcast([L, n, 4])
            ig, lf, b, a, z, t = (gt[k][0:L] for k in ("ig", "lf", "b", "a", "z", "t"))
            S.op("dve", lambda e: e.tensor_tensor(ig, pv[:, :, 0:4], bi3, ALU.add), reads=[Bc_g, Bcst], writes=[Bg])
            S.op("dve", lambda e: e.tensor_tensor(z, pv[:, :, 4:8], bf3, ALU.add), reads=[Bc_g, Bcst], writes=[Bg])
            if KG < 2:
                return
            S.op("dve", lambda e: e.scalar_tensor_tensor(t, z, -1.0, z, ALU.mult, ALU.max), reads=[Bg], writes=[Bg])
            if KG < 2.2:
                return
            S.op("act", lambda e: e.activation(t, t, AF.Exp, scale=-1.0), reads=[Bg], writes=[Bg])
            if KG < 2.4:
                return
            S.op("act", lambda e: e.activation(t, t, AF.Ln, bias=1.0), reads=[Bg], writes=[Bg])
            if KG < 2.6:
                return
            S.op("dve", lambda e: e.tensor_scalar_min(lf, z, 0.0), reads=[Bg], writes=[Bg])
            if KG < 2.75:
                return
            S.op("dve", lambda e: e.tensor_tensor(lf, lf, t, ALU.subtract), reads=[Bg], writes=[Bg])
            if KG < 3:
                return
            lf2 = gt["lf"][0:L].rearrange("p c h -> p (c h)")
            S.op("pe", lambda e: e.matmul(psC[0:L, 256:256 + n * 4], Umat[0:L, 0:L], lf2, start=True, stop=True),
                 reads=[Bg, Bcst], writes=[Bc_st])
            if KG < 3.2:
                return
            b2 = gt["b"][0:L].rearrange("p c h -> p (c h)")
            S.op("dve", lambda e: e.tensor_copy(b2, psC[0:L, 256:256 + n * 4]), reads=[Bc_st], writes=[Bg])
            if KG < 3.4:
                return
            S.op("dve", lambda e: e.tensor_tensor(a, ig, b, ALU.subtract), reads=[Bg], writes=[Bg])

        gates(128, 8, lambda kc, c: bufY[:, kc, c * 128:(c + 1) * 128], [BY], g_prev)
        gates(128, 8, lambda kc, c: xT[:, kc, c * 128:(c + 1) * 128], [BxT], g_own)
        gates(4, 4, lambda kc, c: xT[:, kc, NT + 4 * c:NT + 4 * c + 4], [BxT], g_smp)
        f2 = lambda ap: ap.rearrange("p c h -> p (c h)")

        def stab(gt, n, L, Elast, m0, recur, X):
            b = gt["b"]; k4 = n * 4
            ps, pB = psB[:, 0:512], BpsB4[0]
            S.op("pe", lambda e, ps=ps: e.transpose(ps[0:k4, 0:L], f2(gt["a"][0:L]), ident[0:L, 0:L]), reads=[Bg, Bcst], writes=[pB], cost=0.3)
            S.op("dve", lambda e, ps=ps: e.tensor_copy(logdG[0][0:k4, 0:L], ps[0:k4, 0:L]), reads=[pB], writes=[BlogdG[0]])
            S.op("dve", lambda e: e.tensor_tensor_scan(logdG[1][0:k4, 0:L], logdG[0][0:k4, 0:L], logdG[0][0:k4, 0:L], NEG, ALU.max, ALU.max),
                 reads=[BlogdG[0]], writes=[BlogdG[1]])
            ps2, pB2 = psB[:, 512:1024], BpsB4[1]
            S.op("pe", lambda e, ps2=ps2: e.transpose(ps2[0:L, 0:k4], logdG[1][0:k4, 0:L], ident[0:k4, 0:k4]), reads=[BlogdG[1], Bcst], writes=[pB2], cost=0.3)
            S.op("dve", lambda e, ps2=ps2: e.tensor_copy(f2(X["cm"][0:L]), ps2[0:L, 0:k4]), reads=[pB2], writes=[Bpg])
            S.op("dve", lambda e: e.tensor_tensor(X["mloc"][0:L], b[0:L], X["cm"][0:L], ALU.add), reads=[Bg, Bpg], writes=[Bpg])
            S.op("pe", lambda e: e.matmul(psC[:, 0:k4], Elast[0:L, 0:128], f2(b[0:L]), start=True, stop=True), reads=[Bg, Bcst], writes=[Bc_g])
            S.op("dve", lambda e: e.tensor_copy(f2(X["BL"]), psC[:, 0:k4]), reads=[Bc_g], writes=[Bpg])
            S.op("pe", lambda e: e.matmul(psC[:, 32:32 + k4], Elast[0:L, 0:128], f2(X["mloc"][0:L]), start=True, stop=True),
                 reads=[Bpg, Bcst], writes=[Bc_g])
            S.op("dve", lambda e: e.tensor_copy(f2(X["ML"]), psC[:, 32:32 + k4]), reads=[Bc_g], writes=[Bpg])
            if recur:
                S.op("dve", lambda e: e.tensor_copy(X["mprev"][:, 0, :], m0), reads=[Bpg, Bcst], writes=[Bpg])
                for c in range(n):
                    S.op("dve", lambda e, c=c: e.tensor_tensor(X["t"][:, c, :], X["BL"][:, c, :], X["mprev"][:, c, :], ALU.add), reads=[Bpg], writes=[Bpg])
                    S.op("dve", lambda e, c=c: e.tensor_tensor(X["m"][:, c, :], X["t"][:, c, :], X["ML"][:, c, :], ALU.max), reads=[Bpg], writes=[Bpg])
                    if c + 1 < n:
                        S.op("dve", lambda e, c=c: e.tensor_copy(X["mprev"][:, c + 1, :], X["m"][:, c, :]), reads=[Bpg], writes=[Bpg])
            else:
                S.op("dve", lambda e: e.tensor_copy(X["mprev"], m0), reads=[Bpg, Bcst], writes=[Bpg])
                S.op("dve", lambda e: e.tensor_tensor(X["t"], X["BL"], X["mprev"], ALU.add), reads=[Bpg], writes=[Bpg])
                S.op("dve", lambda e: e.tensor_tensor(X["m"], X["t"], X["ML"], ALU.max), reads=[Bpg], writes=[Bpg])
            S.op("dve", lambda e: e.tensor_tensor(X["dec"], X["t"], X["m"], ALU.subtract), reads=[Bpg], writes=[Bpg])
            S.op("act", lambda e: e.activation(X["dec"], X["dec"], AF.Exp), reads=[Bpg], writes=[Bpg])
            S.op("dve", lambda e: e.tensor_tensor(X["inter"][0:L], b[0:L], X["mprev"][0:L], ALU.add), reads=[Bg, Bpg], writes=[Bpg])
            S.op("dve", lambda e: e.tensor_tensor(X["mrow"][0:L], X["mloc"][0:L], X["inter"][0:L], ALU.max), reads=[Bpg], writes=[Bpg])
            S.op("dve", lambda e: e.tensor_tensor(X["inter"][0:L], X["inter"][0:L], X["mrow"][0:L], ALU.subtract), reads=[Bpg], writes=[Bpg])
            S.op("act", lambda e: e.activation(X["inter"][0:L], X["inter"][0:L], AF.Exp), reads=[Bpg], writes=[Bpg])
            S.op("dve", lambda e: e.tensor_tensor(X["bm"][0:L], b[0:L], X["mrow"][0:L], ALU.subtract), reads=[Bg, Bpg], writes=[Bpg])
            S.op("act", lambda e: e.activation(X["emr"][0:L], X["mrow"][0:L], AF.Exp, scale=-1.0), reads=[Bpg], writes=[Bpg])

        S.op("dve", lambda e: e.memset(zero4, 0.0), writes=[Bpg])
        stab(g_prev, 8, 128, E127, zero4, True, pg)
        P_ = lambda k: pg[k]
        S.op("dve", lambda e: e.tensor_tensor(P_("G"), P_("ML"), P_("m"), ALU.subtract), reads=[Bpg], writes=[Bpg])
        S.op("act", lambda e: e.activation(P_("G"), P_("G"), AF.Exp), reads=[Bpg], writes=[Bpg])
        S.op("dve", lambda e: e.memset(P_("t")[:, 7, :], 1.0), reads=[Bpg], writes=[Bpg])
        for c in range(6, -1, -1):
            S.op("dve", lambda e, c=c: e.tensor_tensor(P_("t")[:, c, :], P_("t")[:, c + 1, :], P_("dec")[:, c + 1, :], ALU.mult), reads=[Bpg], writes=[Bpg])
        S.op("dve", lambda e: e.tensor_tensor(P_("G"), P_("G"), P_("t"), ALU.mult), reads=[Bpg], writes=[Bpg])
        S.op("dve", lambda e: e.tensor_tensor(P_("wG"), g_prev["a"], P_("BL"), ALU.add), reads=[Bg, Bpg], writes=[Bpg])
        S.op("dve", lambda e: e.tensor_tensor(P_("wG"), P_("wG"), P_("ML"), ALU.subtract), reads=[Bpg], writes=[Bpg])
        S.op("act", lambda e: e.activation(P_("wG"), P_("wG"), AF.Exp), reads=[Bpg], writes=[Bpg])
        S.op("dve", lambda e: e.tensor_tensor(P_("wG"), P_("wG"), P_("G"), ALU.mult), reads=[Bpg], writes=[Bpg])
        S.op("dve", lambda e: e.tensor_scalar(minit, P_("m")[:, 7, :], flag[:, 0:1], None, ALU.mult), reads=[Bpg, Bcst], writes=[Bpg])
        stab(g_own, 8, 128, E127, minit, True, og)
        S.dma("pool", f2(msmp), sm_d.partition_broadcast(128), writes=[Bcst], sembuf=Bcst)
        stab(g_smp, 4, 4, E3, msmp, False, sg_)

        if STOP == "G":
            S.barrier(); S.flush(block); return nc
        alloc_ptr[0] = work1
        qTh = T(BF16, [4, TOK]); BqT = Buf("dj_qTh")
        kTh = T(BF16, [4, TOK]); BkT = Buf("dj_kTh")
        kprev = T(BF16, [8, 512]); Bkprev = Buf("dj_kprev")
        vprev = T(BF16, [8, 512]); Bvprev = Buf("dj_vprev")
        Cst = T(F32, [4, 512]); BC = dB("Cst"); BC4 = [Buf("Cst%d" % i) for i in range(4)]
        Cbf = T(BF16, [4, 512]); BCb4 = [Buf("Cbf%d" % i) for i in range(4)]
        nst = T(F32, [4]); Bn = dB("nst")
        nbf = T(BF16, [4]); Bnb = Buf("nbf")
        mcol = T(F32, [1]); Bm = dB("mcol")
        dec = T(F32, [1]); Bdec = Buf("dec")
        logd = T(F32, [128]); Blogd = Buf("logd")
        DT = T(F32, [128]); BDT = Buf("DT")
        PT = T(BF16, [128]); BPT = Buf("PT")
        sm2 = [T(F32, [18]) for _ in range(2)]; Bsm2 = [Buf("smA"), Buf("smB")]
        stats = T(F32, [6]); mv = T(F32, [2]); Bst = Buf("stats")
        kw = T(BF16, [512]); Bkw = Buf("kw")
        assert alloc_ptr[0] <= ARENA_B
        save = alloc_ptr[0]
        alloc_ptr[0] = br_a_off
        ktok = T(BF16, [12, 512]); Bktok = Buf("dj_ktok")
        vtok = T(BF16, [12, 512]); Bvtok = Buf("dj_vtok")
        numI2 = [T(F32, [512]) for _ in range(2)]; BnumI2 = [Buf("numIA"), Buf("numIB")]
        t1 = T(F32, [512]); Bt1 = Buf("t1")
        hnorm = T(BF16, [512]); Bhn = Buf("hnorm")
        kwx = T(BF16, [512]); kw2 = [kw, kwx]; Bkw2 = [Bkw, Buf("kwx")]
        assert alloc_ptr[0] <= br_a_off + 16 * TOK * 2, alloc_ptr[0] - br_a_off
        alloc_ptr[0] = save

        class UC:
            pass

        def unit_ctx(L, gt, X, ci, h, ktok_ap, vtok_ap, Bkv, qcols, slot):
            u = UC()
            u.L, u.gt, u.X, u.ci, u.h, u.ktok, u.vtok, u.Bkv, u.qcols, u.slot = L, gt, X, ci, h, ktok_ap, vtok_ap, Bkv, qcols, slot
            u.qf = lambda kc: qTh[:, kc, qcols:qcols + L]
            u.kf = lambda kc: kTh[:, kc, qcols:qcols + L]
            u.numI = numI2[slot]; u.BnumI = BnumI2[slot]
            u.kw = kw2[slot]; u.Bkw = Bkw2[slot]
            u.sm = sm2[slot]; u.Bsm = Bsm2[slot]
            return u

        def unit_P(u):
            L, X, ci, h = u.L, u.X, u.ci, u.h
            a_col = u.gt["a"][0:L, ci, h:h + 1]
            bm = X["bm"][0:L, ci, h:h + 1]
            S.op("pe", lambda e: e.matmul(psC[0:L, 128:128 + L], bm.to_broadcast([L, L]), ident[0:L, 0:L], start=True, stop=True),
                 reads=[Bpg, Bcst], writes=[Bc_bm])
            S.op("dve", lambda e: e.scalar_tensor_tensor(logd[0:L, 0:L], psC[0:L, 128:128 + L], a_col, maskT[0:L, 0:L], ALU.add, ALU.add),
                 reads=[Bc_bm, Bg, Bcst], writes=[Blogd])
            S.op("act", lambda e: e.activation(DT[0:L, 0:L], logd[0:L, 0:L], AF.Exp), reads=[Blogd], writes=[BDT])
            mm_group(psC[0:L, 256:256 + L], [(u.kf(kc), u.qf(kc)) for kc in range(4)], [BqT, BkT], Bc_st)
            S.op("dve", lambda e: e.tensor_tensor(PT[0:L, 0:L], psC[0:L, 256:256 + L], DT[0:L, 0:L], ALU.mult),
                 reads=[Bc_st, BDT], writes=[BPT])
            S.op("pe", lambda e: e.matmul(psT32[0:L, :], PT[0:L, 0:L], u.vtok, start=True, stop=True),
                 reads=[BPT] + u.Bkv, writes=[BpsT])
            S.op("act", lambda e: e.activation(u.numI[0:L], psT32[0:L, :], AF.Copy), reads=[BpsT], writes=[u.BnumI], cost=0.7)
            S.op("pe", lambda e: e.matmul(psC[0:L, 384:385], PT[0:L, 0:L], onesb[0:L, 0:1], start=True, stop=True),
                 reads=[BPT, Bones], writes=[Bc_dq])
            S.op("dve", lambda e: e.tensor_copy(u.sm[0:L, 0:1], psC[0:L, 384:385]), reads=[Bc_dq], writes=[u.Bsm])
            S.op("dve", lambda e: e.tensor_scalar(u.kw[0:L], u.ktok, DT[0:L, L - 1:L], None, ALU.mult), reads=u.Bkv + [BDT], writes=[u.Bkw], cost=0.7)

        def unit_E1(u):
            L = u.L
            i = u.slot
            mm_group(psA[i][0:L, :], [(u.qf(kc), Cbf[:, kc, :]) for kc in range(4)], [BqT] + BCb4, BpsA[i])
            mm_group(psC[0:L, 385:386], [(u.qf(kc), nbf[:, kc:kc + 1]) for kc in range(4)], [BqT, Bnb], Bc_dq)
            S.op("dve", lambda e: e.tensor_copy(u.sm[0:L, 1:2], psC[0:L, 385:386]), reads=[Bc_dq], writes=[u.Bsm])

        def unit_CC(u, want_bf=True):
            L, X, ci, h = u.L, u.X, u.ci, u.h
            for kc in range(4):
                S.op("pe", lambda e, kc=kc: e.matmul(psB[:, kc * 512:(kc + 1) * 512], u.kw[0:L, kc * 128:(kc + 1) * 128], u.vtok,
                                                     start=True, stop=True), reads=[u.Bkw] + u.Bkv, writes=[BpsB4[kc]], cost=0.27)
            for kc in range(4):
                S.op("pe", lambda e, kc=kc: e.matmul(psC[:, 392 + kc:393 + kc], u.kw[0:L, kc * 128:(kc + 1) * 128], onesb[0:L, 0:1],
                                                     start=True, stop=True), reads=[u.Bkw, Bones], writes=[Bc_nup], signal=(kc == 3))
            dcol = X["dec"][:, ci, h:h + 1]
            for kc in range(4):
                S.op("dve", lambda e, kc=kc: e.scalar_tensor_tensor(Cst[:, kc, :], Cst[:, kc, :], dcol, psB[:, kc * 512:(kc + 1) * 512], ALU.mult, ALU.add),
                     reads=[BC4[kc], Bpg, BpsB4[kc]], writes=[BC4[kc]], cost=0.65)
                if want_bf:
                    S.op("act", lambda e, kc=kc: e.activation(Cbf[:, kc, :], Cst[:, kc, :], AF.Copy), reads=[BC4[kc]], writes=[BCb4[kc]], cost=0.55)
            S.op("dve", lambda e: e.scalar_tensor_tensor(nst, nst, dcol, psC[:, 392:396], ALU.mult, ALU.add), reads=[Bn, Bpg, Bc_nup], writes=[Bn])
            if want_bf:
                S.op("act", lambda e: e.activation(nbf, nst, AF.Copy), reads=[Bn], writes=[Bnb])

        def unit_E2(u):
            L, X, ci, h, qcols = u.L, u.X, u.ci, u.h, u.qcols
            sm = u.sm
            inter = X["inter"][0:L, ci, h:h + 1]; emr = X["emr"][0:L, ci, h:h + 1]
            den = sm[0:L, 2:3]; dabs = sm[0:L, 3:4]; r_ = sm[0:L, 4:5]; ir = sm[0:L, 5:6]
            lnv = sm[0:L, 6:7]; rstd = sm[0:L, 7:8]; nmr = sm[0:L, 8:9]
            st6 = sm[0:L, 10:16]; mvv = sm[0:L, 16:18]
            Bs = u.Bsm
            i = u.slot
            S.op("dve", lambda e: e.scalar_tensor_tensor(den, sm[0:L, 1:2], inter, sm[0:L, 0:1], ALU.mult, ALU.add), reads=[Bs, Bpg], writes=[Bs])
            S.op("dve", lambda e: e.scalar_tensor_tensor(dabs, den, -1.0, den, ALU.mult, ALU.max), reads=[Bs], writes=[Bs])
            S.op("dve", lambda e: e.tensor_tensor(dabs, dabs, emr, ALU.max), reads=[Bs, Bpg], writes=[Bs])
            S.op("dve", lambda e: e.reciprocal(r_, dabs), reads=[Bs], writes=[Bs])
            S.op("dve", lambda e: e.tensor_tensor(ir, inter, r_, ALU.mult), reads=[Bs, Bpg], writes=[Bs])
            S.op("act", lambda e: e.activation(t1[0:L], psA[i][0:L, :], AF.Copy, scale=ir), reads=[BpsA[i], Bs], writes=[Bt1], cost=0.8)
            S.op("dve", lambda e: e.scalar_tensor_tensor(u.numI[0:L], u.numI[0:L], r_, t1[0:L], ALU.mult, ALU.add),
                 reads=[u.BnumI, Bs, Bt1], writes=[u.BnumI], cost=0.75)
            S.op("dve", lambda e: e.bn_stats(st6, u.numI[0:L]), reads=[u.BnumI], writes=[Bs], cost=0.7)
            S.op("dve", lambda e: e.bn_aggr(mvv, st6), reads=[Bs], writes=[Bs])
            S.op("act", lambda e: e.activation(lnv, mvv[:, 1:2], AF.Ln, bias=LN_EPS), reads=[Bs], writes=[Bs])
            S.op("act", lambda e: e.activation(rstd, lnv, AF.Exp, scale=-0.5), reads=[Bs], writes=[Bs])
            S.op("dve", lambda e: e.scalar_tensor_tensor(nmr, mvv[:, 0:1], -1.0, rstd, ALU.mult, ALU.mult), reads=[Bs], writes=[Bs])
            S.op("act", lambda e: e.activation(hnorm[0:L], u.numI[0:L], AF.Identity, bias=nmr, scale=rstd),
                 reads=[u.BnumI, Bs], writes=[Bhn], cost=0.8)
            for kc in range(4):
                S.op("pe", lambda e, kc=kc: e.transpose(psT[:, kc * 128:kc * 128 + L], hnorm[0:L, kc * 128:(kc + 1) * 128],
                                                        identb[0:L, 0:L]),
                     reads=[Bhn, Bcst], writes=[BpsT], signal=(kc == 3))
            pv = psT[:, 0:512].rearrange("p (k l) -> p k l", k=4)[:, :, 0:L]
            S.op("dve", lambda e: e.tensor_tensor(br_m[:, 4 * h:4 * h + 4, qcols:qcols + L], pv,
                                                  nw[:, 4 * h:4 * h + 4, None].to_broadcast([128, 4, L]), ALU.mult),
                 reads=[BpsT, Bcst], writes=[Bbrm4[h]], cost=0.7)

        for h in range(1 if STOP == 'M0' else 4):
            for half in range(2):
                w, wB = load_w(w_in, C_QM + h * 512 + half * GW, GW)

                def epi_q(cc, c0, n, ps, pB, half=half):
                    S.op("act", lambda e: e.activation(qTh[:, half * 2 + cc, c0:c0 + n], ps, AF.Copy, scale=float(512 ** -0.5)),
                         reads=[pB], writes=[BqT])
                fmode(w, wB, GW, xrhs, [BxT], BLK, epi_q)
            for half in range(2):
                w, wB = load_w(w_in, C_KM + h * 512 + half * GW, GW)

                def epi_k(cc, c0, n, ps, pB, half=half):
                    S.op("act", lambda e: e.activation(kTh[:, half * 2 + cc, c0:c0 + n], ps, AF.Copy), reads=[pB], writes=[BkT])
                fmode(w, wB, GW, xrhs, [BxT], BLK, epi_k)
                for c in range(8):
                    tmode(w, wB, GW, lambda kc, c=c: bufY[:, kc, c * 128:(c + 1) * 128], [BY], 128,
                          lambda ps, pB, c=c, half=half: S.op("act", lambda e: e.activation(kprev[:, c, half * GW:(half + 1) * GW], ps, AF.Copy),
                                                              reads=[pB], writes=[Bkprev]))
            for c in range(8):
                for kc in range(4):
                    S.op("pe", lambda e, c=c, kc=kc: e.transpose(psT[:, kc * 128:(kc + 1) * 128], kTh[:, kc, c * 128:(c + 1) * 128], identb),
                         reads=[BkT, Bcst], writes=[BpsT], signal=(kc == 3))
                S.op("act", lambda e, c=c: e.activation(ktok[:, c, :], psT[:, 0:512], AF.Copy), reads=[BpsT], writes=[Bktok])
            for s_ in range(4):
                for kc in range(4):
                    S.op("pe", lambda e, s_=s_, kc=kc: e.transpose(psT[0:4, kc * 128:(kc + 1) * 128], kTh[:, kc, NT + 4 * s_:NT + 4 * s_ + 4], identb),
                         reads=[BkT, Bcst], writes=[BpsT], signal=(kc == 3))
                S.op("act", lambda e, s_=s_: e.activation(ktok[0:4, 8 + s_, :], psT[0:4, 0:512], AF.Copy), reads=[BpsT], writes=[Bktok])
            for half in range(2):
                w, wB = load_w(w_in, C_VM + h * 512 + half * GW, GW)
                for c in range(8):
                    tmode(w, wB, GW, lambda kc, c=c: bufY[:, kc, c * 128:(c + 1) * 128], [BY], 128,
                          lambda ps, pB, c=c, half=half: S.op("act", lambda e: e.activation(vprev[:, c, half * GW:(half + 1) * GW], ps, AF.Copy),
                                                              reads=[pB], writes=[Bvprev]))
                for c in range(8):
                    tmode(w, wB, GW, lambda kc, c=c: xT[:, kc, c * 128:(c + 1) * 128], [BxT], 128,
                          lambda ps, pB, c=c, half=half: S.op("act", lambda e: e.activation(vtok[:, c, half * GW:(half + 1) * GW], ps, AF.Copy),
                                                              reads=[pB], writes=[Bvtok]))
                for s_ in range(4):
                    tmode(w, wB, GW, lambda kc, s_=s_: xT[:, kc, NT + 4 * s_:NT + 4 * s_ + 4], [BxT], 4,
                          lambda ps, pB, s_=s_, half=half: S.op("act", lambda e: e.activation(vtok[0:4, 8 + s_, half * GW:(half + 1) * GW], ps, AF.Copy),
                                                                reads=[pB], writes=[Bvtok]))
            for c in range(8):
                kwb, kwB = kw2[c % 2], Bkw2[c % 2]
                S.op("dve", lambda e, kwb=kwb, c=c, h=h: e.tensor_scalar(kwb, kprev[:, c, :], pg["wG"][:, c, h:h + 1], None, ALU.mult),
                     reads=[Bkprev, Bpg], writes=[kwB])
                for kc in range(4):
                    S.op("pe", lambda e, kwb=kwb, c=c, kc=kc: e.matmul(psB[:, kc * 512:(kc + 1) * 512], kwb[:, kc * 128:(kc + 1) * 128],
                                                                      vprev[:, c, :], start=(c == 0), stop=(c == 7)),
                         reads=[kwB, Bvprev], writes=[BpsB4[kc]], signal=False)
                for kc in range(4):
                    S.op("pe", lambda e, kwb=kwb, c=c, kc=kc: e.matmul(psC[:, 392 + kc:393 + kc], kwb[:, kc * 128:(kc + 1) * 128],
                                                                      onesb[:, 0:1], start=(c == 0 and kc == 0), stop=(c == 7),
                                                                      skip_group_check=True),
                         reads=[kwB, Bones], writes=[Bc_nup], signal=(kc == 3))
            Cf = Cst.rearrange("p k v -> p (k v)")
            for kc in range(4):
                S.op("dve", lambda e, kc=kc: e.tensor_scalar(Cst[:, kc, :], psB[:, kc * 512:(kc + 1) * 512], flag[:, 0:1], None, ALU.mult),
                     reads=[BpsB4[kc], Bcst], writes=[BC4[kc]], cost=0.6)
                S.op("act", lambda e, kc=kc: e.activation(Cbf[:, kc, :], Cst[:, kc, :], AF.Copy), reads=[BC4[kc]], writes=[BCb4[kc]], cost=0.55)
            S.op("dve", lambda e: e.tensor_scalar(nst, psC[:, 392:396], flag[:, 0:1], None, ALU.mult), reads=[Bc_nup, Bcst], writes=[Bn])
            S.op("dve", lambda e, h=h: e.tensor_copy(mcol, minit[:, h:h + 1]), reads=[Bpg], writes=[Bm])
            S.op("act", lambda e: e.activation(nbf, nst, AF.Copy), reads=[Bn], writes=[Bnb])
            if os.environ.get("KDBG") == "pre":
                S.dma("sp", Cp_o[h].rearrange("(k p) v -> p k v", p=128), Cst, reads=BC4, sembuf=BC)
                S.dma("sp", np_o[h], nst, reads=[Bn], sembuf=Bn)
                S.dma("sp", mp_o[h:h + 1, :], mcol[0:1, :], reads=[Bm], sembuf=Bm)
                continue
            if h > 0:
                gl = [2 * (h - 1), 2 * (h - 1) + 1]
                bgst["gen"] = chain(gate_items(C_OM, AF.Sigmoid, br_m, Bbrm4, gl), gate_items(C_ZM, AF.Silu, br_m, Bbrm4, gl))
            us = [unit_ctx(128, g_own, og, c, h, ktok[:, c, :], vtok[:, c, :], [Bktok, Bvtok], c * 128, c % 2) for c in range(8)]
            unit_P(us[0])
            for c in range(8):
                unit_E1(us[c])
                unit_CC(us[c], want_bf=(c < 7))
                if c + 1 < 8:
                    unit_P(us[c + 1])
                unit_E2(us[c])
                bg_step(2)
            S.dma("sp", Cp_o[h].rearrange("(k p) v -> p k v", p=128), Cst, reads=BC4, sembuf=BC)
            S.dma("sp", np_o[h], nst, reads=[Bn], sembuf=Bn)
            S.dma("sp", mp_o[h:h + 1, :], og["m"][0:1, 7, h:h + 1], reads=[Bpg], sembuf=Bm)
            ss = [unit_ctx(4, g_smp, sg_, s_, h, ktok[0:4, 8 + s_, :], vtok[0:4, 8 + s_, :], [Bktok, Bvtok], NT + 4 * s_, s_ % 2) for s_ in range(4)]
            unit_P(ss[0])
            for s_ in range(4):
                S.dma("sp", Cst, sC_d[s_, h].rearrange("(k p) v -> p k v", p=128), writes=BC4, sembuf=BC)
                S.dma("sp", nst, sn_d[s_, h], writes=[Bn], sembuf=Bn)
                for kc in range(4):
                    S.op("act", lambda e, kc=kc: e.activation(Cbf[:, kc, :], Cst[:, kc, :], AF.Copy), reads=[BC4[kc]], writes=[BCb4[kc]], cost=0.55)
                S.op("act", lambda e: e.activation(nbf, nst, AF.Copy), reads=[Bn], writes=[Bnb])
                unit_E1(ss[s_])
                unit_CC(ss[s_], want_bf=False)
                S.dma("sp", Cs_o[s_, h].rearrange("(k p) v -> p k v", p=128), Cst, reads=BC4, sembuf=BC)
                S.dma("sp", ns_o[s_, h], nst, reads=[Bn], sembuf=Bn)
                S.dma("sp", ms_o[s_ * 4 + h:s_ * 4 + h + 1, :], sg_["m"][0:1, s_, h:h + 1], reads=[Bpg], sembuf=Bm)
                if s_ + 1 < 4:
                    unit_P(ss[s_ + 1])
                unit_E2(ss[s_])
                bg_step(2)
            bg_drain()

        if STOP in ("M", "M0"):
            S.barrier(); S.flush(block); return nc
        work2 = work1
        if STOP == "OZ":
            S.barrier(); S.flush(block); return nc
        S.barrier()
        alloc_ptr[0] = W0
        TOKA = 128 + TOK
        kT2 = T(BF16, [4, TOKA]); BkT2 = Buf("dj_kT2")
        vtokA = T(BF16, [9, 256]); BvA = Buf("dj_vtokA")
        vext = T(BF16, [9, 192]); Bvext = Buf("vext")
        qTa = T(BF16, [4, TOK]); BqTa = Buf("dj_qTa")
        pT = [T(BF16, [1024]) for _ in range(4)]; BpT = [Buf("pT%d" % i) for i in range(4)]
        Eprev = T(BF16, [1024]); Ecur = T(BF16, [1024]); Esb = T(BF16, [32]); Esn = T(BF16, [32])
        BEt = Buf("Etab")
        BpsBh = [Buf("psB_lo", excl=True), Buf("psB_hi", excl=True)]
        kbuf = T(BF16, [4, 256]); Bkbuf = dB("kbuf")
        vbuf = T(BF16, [4, 256]); Bvbuf = dB("vbuf")
        kdup = T(BF16, [128]); Bkdup = Buf("kdup")
        kTb = T(BF16, [4, 128]); BkTb = Buf("kTb")
        vbext = T(BF16, [4, 192]); Bvbext = Buf("vbext")
        vstok = T(BF16, [4, 256]); Bvstok = Buf("vstok")
        vexts = T(BF16, [4, 192]); Bvexts = Buf("vexts")
        kvout = T(F32, [512]); Bkvout = dB("kvout")
        kvs = T(F32, [256]); Bkvs = dB("kvs")
        rden = kvout; Brden = Bkvout
        rsh = T(F32, [512]); Brsh = Buf("rsh")
        assert alloc_ptr[0] <= ARENA_B, alloc_ptr[0]
        Bcp = dB("d2d")

        S.dma("pool", kbuf, sk_d.rearrange("s k c -> k s c"), writes=[Bkbuf], sembuf=Bkbuf)
        S.dma("pool", vbuf, sv_d.rearrange("s k c -> k s c"), writes=[Bvbuf], sembuf=Bvbuf)
        for s_ in range(4):
            S.dma("sp", kbs_o[s_, 0:124, :], sk_d[s_, 4:128, :], sembuf=Bcp)
            S.dma("sp", vbs_o[s_, 0:124, :], sv_d[s_, 4:128, :], sembuf=Bcp)
        S.op("dve", lambda e: e.memset(vext.rearrange("p a b -> p (a b)"), 1.0), writes=[Bvext])
        S.op("dve", lambda e: e.memset(vbext.rearrange("p a b -> p (a b)"), 1.0), writes=[Bvbext])
        S.op("dve", lambda e: e.memset(vexts.rearrange("p a b -> p (a b)"), 1.0), writes=[Bvexts])

        wk, wkB = load_w(w_in, C_KA, 256)
        KSRC = [(bufY, BY, 896, 128, 0), (xT, BxT, 0, 512, 128), (xT, BxT, 512, 512, 640), (xT, BxT, 1024, NSM, 1152)]
        for cc in range(2):
            for (src, srcB, c0, n, d0) in KSRC:
                ps, pB = next_psA()
                mm_group(ps[:, 0:n], [(wk[:, kc, cc * 128:(cc + 1) * 128], src[:, kc, c0:c0 + n]) for kc in range(16)], [wkB, srcB], pB)
                for (dr, hh_, sr) in ((slice(0, 64), 2 * cc, slice(0, 64)), (slice(64, 128), 2 * cc, slice(0, 64)),
                                      (slice(64, 128), 2 * cc + 1, slice(64, 128)), (slice(0, 64), 2 * cc + 1, slice(64, 128))):
                    S.op("act", lambda e, ps=ps, dr=dr, hh_=hh_, sr=sr, n=n, d0=d0: e.activation(kT2[dr, hh_, d0:d0 + n], ps[sr, 0:n], AF.Copy),
                         reads=[pB], writes=[BkT2])
        tmode(wk, wkB, 256, lambda kc: xT[:, kc, 896:1024], [BxT], 128,
              lambda ps, pB: S.op("act", lambda e: e.activation(kvout[:, 0:256], ps, AF.Copy), reads=[pB], writes=[Bkvout]))
        for s_ in range(4):
            tmode(wk, wkB, 256, lambda kc, s_=s_: xT[:, kc, NT + 4 * s_:NT + 4 * s_ + 4], [BxT], 4,
                  lambda ps, pB, s_=s_: S.op("act", lambda e: e.activation(kvs[0:4, :], ps, AF.Copy), reads=[pB], writes=[Bkvs]))
            S.dma("sp", kbs_o[s_, 124:128, :], kvs[0:4, :], reads=[Bkvs], sembuf=Bkvs)
        wv, wvB = load_w(w_in, C_VA, 256)
        VSRC = [(bufY, BY, 896)] + [(xT, BxT, c * 128) for c in range(8)]
        for bi_, (src, srcB, c0) in enumerate(VSRC):
            def epi_v(ps, pB, bi_=bi_):
                S.op("act", lambda e: e.activation(vtokA[:, bi_, :], ps, AF.Copy), reads=[pB], writes=[BvA])
                if bi_ == 8:
                    S.op("act", lambda e: e.activation(kvout[:, 256:512], ps, AF.Copy), reads=[pB], writes=[Bkvout])
            tmode(wv, wvB, 256, lambda kc, src=src, c0=c0: src[:, kc, c0:c0 + 128], [srcB], 128, epi_v)
        for s_ in range(4):
            def epi_vs(ps, pB, s_=s_):
                S.op("act", lambda e: e.activation(vstok[0:4, s_, :], ps, AF.Copy), reads=[pB], writes=[Bvstok])
                S.op("act", lambda e: e.activation(kvs[0:4, :], ps, AF.Copy), reads=[pB], writes=[Bkvs])
            tmode(wv, wvB, 256, lambda kc, s_=s_: xT[:, kc, NT + 4 * s_:NT + 4 * s_ + 4], [BxT], 4, epi_vs)
            S.dma("sp", vbs_o[s_, 124:128, :], kvs[0:4, :], reads=[Bkvs], sembuf=Bkvs)
        S.dma("sp", kbp_o, kvout[:, 0:256], reads=[Bkvout], sembuf=Bkvout)
        S.dma("sp", vbp_o, kvout[:, 256:512], reads=[Bkvout], sembuf=Bkvout)

        if STOP == "A1":
            S.barrier(); S.flush(block); return nc

        def attend_A(h, N, qc0, kbs, kbB, slot):
            for bi_, (kTap, vap, Eap, Kn) in enumerate(kbs):
                useflag = Kn < 0
                Kn = abs(Kn)
                pt, ptB = pT[slot * 2 + bi_], BpT[slot * 2 + bi_]
                for g in range(8):
                    hf = g % 2
                    pc = (g % 2) * 512 + (g // 2) * N
                    S.op("pe", lambda e, g=g, hf=hf, bi_=bi_, kTap=kTap, Kn=Kn, pc=pc: e.matmul(
                        psB[0:Kn, bi_ * 1024 + pc:bi_ * 1024 + pc + N], kTap[hf * 64:(hf + 1) * 64, :],
                        qTa[hf * 64:(hf + 1) * 64, g // 2, qc0:qc0 + N], start=True, stop=True),
                        reads=kbB + [BqTa], writes=[BpsBh[bi_]], signal=(g == 7))
                src = psB[0:Kn, bi_ * 1024:(bi_ + 1) * 1024].rearrange("p (t c) -> p t c", t=2)[:, :, 0:4 * N]
                dst = pt[0:Kn, 0:8 * N].rearrange("p (t c) -> p t c", t=2)
                S.op("act", lambda e, src=src, dst=dst: e.activation(dst, src, AF.Exp), reads=[BpsBh[bi_]], writes=[ptB], cost=1.0)
                S.op("dve", lambda e, pt=pt, Kn=Kn, Eap=Eap: e.tensor_tensor(pt[0:Kn, 0:8 * N], pt[0:Kn, 0:8 * N], Eap[0:Kn, 0:8 * N], ALU.mult),
                     reads=[ptB, BEt], writes=[ptB], cost=1.0)
                if useflag:
                    S.op("dve", lambda e, pt=pt: e.tensor_scalar(pt[:, 0:1024], pt[:, 0:1024], flag[:, 0:1], None, ALU.mult),
                         reads=[ptB, Bcst], writes=[ptB])
            return (h, N, qc0, kbs, kbB, slot)

        def attend_B(ctx):
            h, N, qc0, kbs, kbB, slot = ctx
            v3 = lambda ap: ap.rearrange("p (j n) -> p j n", n=N)
            for hf in range(2):
                ps, pB = psA[hf], BpsA[hf]
                pairs = []
                for bi_, (kTap, vap, Eap, Kn) in enumerate(kbs):
                    Kn = abs(Kn)
                    lo = 64 if hf == 0 else 0
                    rhs = pT[slot * 2 + bi_][0:Kn, hf * 4 * N:(hf + 1) * 4 * N]
                    pairs.append((vap[0:Kn, lo:lo + 128], rhs))
                mm_group(ps[:, 0:4 * N], pairs, kbB + [BpT[slot * 2], BpT[slot * 2 + 1]], pB)
                nr = slice(0, 64) if hf == 0 else slice(64, 128)
                dr = slice(64, 128) if hf == 0 else slice(0, 64)
                es_b = es[dr, 8 * h + hf:8 * h + 8:2][:, :, None].to_broadcast([64, 4, N])
                S.op("dve", lambda e, ps=ps, dr=dr, es_b=es_b: e.tensor_tensor(v3(rden[dr, 0:4 * N]), v3(ps[dr, 0:4 * N]), es_b, ALU.add),
                     reads=[pB, Bes], writes=[Brden])
                S.op("act", lambda e, dr=dr: e.activation(rden[dr, 0:4 * N], rden[dr, 0:4 * N], AF.Ln), reads=[Brden], writes=[Brden])
                S.op("act", lambda e, dr=dr, nr=nr: e.activation(rsh[nr, 0:4 * N], rden[dr, 0:4 * N], AF.Exp, scale=-1.0), reads=[Brden], writes=[Brsh])
                S.op("dve", lambda e, ps=ps, nr=nr: e.tensor_tensor(br_a[nr, 4 * h:4 * h + 4, qc0:qc0 + N], v3(ps[nr, 0:4 * N]),
                                                                  v3(rsh[nr, 0:4 * N]), ALU.mult),
                     reads=[pB, Brsh], writes=[Bbra4[h]])

        for h in range(4):
            bg_drain()
            if h == 0:
                bgst["gen"] = chain(gate_items(C_OM, AF.Sigmoid, br_m, Bbrm4, [6, 7]), gate_items(C_ZM, AF.Silu, br_m, Bbrm4, [6, 7]))
            else:
                bgst["gen"] = gate_items(C_ZA, AF.Silu, br_a, Bbra4, [2 * (h - 1), 2 * (h - 1) + 1])
            S.op("act", lambda e, h=h: e.activation(vext[:, :, 64:128], vtokA[:, :, h * 64:(h + 1) * 64], AF.Copy),
                 reads=[BvA], writes=[Bvext])
            for g in range(8):
                sl = -SLOPES[8 * h + g]
                c128 = (g % 2) * 512 + (g // 2) * 128
                c4 = (g % 2) * 16 + (g // 2) * 4
                for (Et, dt_, K_, n_, c_) in ((Eprev, dprev, 128, 128, c128), (Ecur, dcur, 128, 128, c128),
                                              (Esb, dsb, 128, 4, c4), (Esn, dsn, 4, 4, c4)):
                    S.op("act", lambda e, Et=Et, dt_=dt_, K_=K_, n_=n_, c_=c_, sl=sl: e.activation(Et[0:K_, c_:c_ + n_], dt_[0:K_, 0:n_], AF.Exp, scale=sl),
                         reads=[Bcst], writes=[BEt])
            for half in range(2):
                w, wB = load_w(w_in, C_QA + h * 512 + half * GW, GW)

                def epi_qa(cc, c0, n, ps, pB, half=half):
                    S.op("act", lambda e: e.activation(qTa[:, half * 2 + cc, c0:c0 + n], ps, AF.Copy, scale=0.125),
                         reads=[pB], writes=[BqTa])
                fmode(w, wB, GW, xrhs, [BxT], BLK, epi_qa)
            for s_ in range(4):
                S.op("act", lambda e, s_=s_, h=h: e.activation(kdup.rearrange("p (t d) -> p t d", t=2),
                                                              kbuf[:, s_, None, h * 64:(h + 1) * 64].to_broadcast([128, 2, 64]), AF.Copy),
                     reads=[Bkbuf], writes=[Bkdup])
                S.op("pe", lambda e: e.transpose(psT[:, 0:128], kdup, identb), reads=[Bkdup, Bcst], writes=[BpsT])
                S.op("act", lambda e, s_=s_: e.activation(kTb[:, s_, :], psT[:, 0:128], AF.Copy), reads=[BpsT], writes=[BkTb])
                S.op("act", lambda e, s_=s_, h=h: e.activation(vbext[:, s_, 64:128], vbuf[:, s_, h * 64:(h + 1) * 64], AF.Copy),
                     reads=[Bvbuf], writes=[Bvbext])
                S.op("act", lambda e, s_=s_, h=h: e.activation(vexts[0:4, s_, 64:128], vstok[0:4, s_, h * 64:(h + 1) * 64], AF.Copy),
                     reads=[Bvstok], writes=[Bvexts])
            blocks = []
            for qb in range(8):
                blocks.append((h, 128, qb * 128,
                               [(kT2[:, h, qb * 128:(qb + 1) * 128], vext[:, qb, :], Eprev, 128 if qb > 0 else -128),
                                (kT2[:, h, (qb + 1) * 128:(qb + 2) * 128], vext[:, qb + 1, :], Ecur, 128)], [BkT2, Bvext]))
            for s_ in range(4):
                blocks.append((h, 4, NT + 4 * s_,
                               [(kTb[:, s_, :], vbext[:, s_, :], Esb, 128),
                                (kT2[:, h, 1152 + 4 * s_:1152 + 4 * s_ + 4], vexts[:, s_, :], Esn, 4)],
                               [BkTb, Bvbext, BkT2, Bvexts]))
            prev_ctx = None
            for i, blk in enumerate(blocks):
                ctx = attend_A(*blk, slot=i % 2)
                bg_step(1)
                if prev_ctx is not None:
                    attend_B(prev_ctx)
                    bg_step(1)
                prev_ctx = ctx
            attend_B(prev_ctx)
        bg_drain()
        if STOP in ("A3", "A"):
            S.barrier(); S.flush(block); return nc
        S.barrier()
        alloc_ptr[0] = work2
        yT = bufY
        bgst["gen"] = gate_items(C_ZA, AF.Silu, br_a, Bbra4, [6, 7])
        bg_drain()
        sg = T(BF16, [2, TOK]); Bsg = Buf("dj_sg")
        ytmp = T(F32, [2, TOK]); Byt = Buf("ytmp")
        tmp2 = T(F32, [512]); Bt2 = Buf("tmp2")
        for g in range(8):
            def epi_sig(cc, c0, n, ps, pB):
                S.op("act", lambda e: e.activation(sg[:, cc, c0:c0 + n], ps, AF.Sigmoid), reads=[pB], writes=[Bsg])

            def epi_pm(cc, c0, n, ps, pB):
                S.op("dve", lambda e: e.tensor_tensor(ytmp[:, cc, c0:c0 + n], ps, sg[:, cc, c0:c0 + n], ALU.mult),
                     reads=[pB, Bsg], writes=[Byt])

            def epi_pa(cc, c0, n, ps, pB, g=g):
                S.op("dve", lambda e: e.tensor_tensor(tmp2[:, 0:n], ps, sg[:, cc, c0:c0 + n], ALU.mult), reads=[pB, Bsg], writes=[Bt2])
                S.op("dve", lambda e: e.tensor_tensor(yT[:, 2 * g + cc, c0:c0 + n], ytmp[:, cc, c0:c0 + n], tmp2[:, 0:n], ALU.add),
                     reads=[Bt2, Byt], writes=[BY])
            w, wB = load_w(w_in, C_GM + g * GW, GW)
            fmode(w, wB, GW, xrhs, [BxT], BLK, epi_sig)
            w, wB = load_w(w_pm, g * GW, GW)
            fmode(w, wB, GW, lambda kc, c0, n: br_m[:, kc, c0:c0 + n], Bbrm4, BLK, epi_pm)
            w, wB = load_w(w_in, C_GA + g * GW, GW)
            fmode(w, wB, GW, xrhs, [BxT], BLK, epi_sig)
            w, wB = load_w(w_pa, g * GW, GW)
            fmode(w, wB, GW, lambda kc, c0, n: br_a[:, kc, c0:c0 + n], Bbra4, BLK, epi_pa)

        if STOP == "P":
            S.barrier(); S.flush(block); return nc
        S.barrier()
        acc_p = V(br_a_off - 16 * TOK * 2, F32, [8, 2048])
        alloc_ptr[0] = work1
        acc_s = T(F32, [2048])
        lnw_t = T(F32, [2048]); lnb_t = T(F32, [2048]); Bln = dB("ln")
        xbs = [T(F32, [2048]) for _ in range(2)]; Bxbs = [dB("xb0"), dB("xb1")]
        st24 = T(F32, [24]); mv2 = T(F32, [8]); Bs2 = Buf("st2")
        Bacc = [Buf("dj_acc%d" % i) for i in range(9)]
        assert alloc_ptr[0] <= ARENA_B, alloc_ptr[0]
        S.dma("sp", lnw_t, lnw_d.partition_broadcast(128), writes=[Bln], sembuf=Bln)
        S.dma("sp", lnb_t, lnb_d.partition_broadcast(128), writes=[Bln], sembuf=Bln)
        accv = lambda tb: (acc_p[:, tb, :] if tb < 8 else acc_s)

        def ln_block(tb):
            M = 128 if tb < 8 else NSM
            a_ = accv(tb)[0:M]
            x_ = xbs[tb % 2][0:M]; Bx = Bxbs[tb % 2]
            S.dma("sp", x_, xtok_d[tb * 128:tb * 128 + M, :], writes=[Bx], sembuf=Bx)
            S.op("dve", lambda e: e.scalar_tensor_tensor(a_, x_, float(DN_ALPHA), a_, ALU.mult, ALU.add),
                 reads=[Bx, Bacc[tb]], writes=[Bacc[tb]])
            for c in range(4):
                S.op("dve", lambda e, c=c: e.bn_stats(st24[0:M, c * 6:(c + 1) * 6], a_[:, c * 512:(c + 1) * 512]),
                     reads=[Bacc[tb]], writes=[Bs2])
            S.op("dve", lambda e: e.bn_aggr(mv2[0:M, 0:2], st24[0:M, 0:24]), reads=[Bs2], writes=[Bs2])
            S.op("act", lambda e: e.activation(mv2[0:M, 2:3], mv2[0:M, 1:2], AF.Ln, bias=LN_EPS), reads=[Bs2], writes=[Bs2])
            S.op("act", lambda e: e.activation(mv2[0:M, 3:4], mv2[0:M, 2:3], AF.Exp, scale=-0.5), reads=[Bs2], writes=[Bs2])
            S.op("dve", lambda e: e.scalar_tensor_tensor(mv2[0:M, 4:5], mv2[0:M, 0:1], -1.0, mv2[0:M, 3:4], ALU.mult, ALU.mult),
                 reads=[Bs2], writes=[Bs2])
            S.op("act", lambda e: e.activation(x_, a_, AF.Identity, bias=mv2[0:M, 4:5], scale=mv2[0:M, 3:4]),
                 reads=[Bacc[tb], Bs2, Bx], writes=[Bx])
            S.op("dve", lambda e: e.tensor_tensor(x_, x_, lnw_t[0:M], ALU.mult), reads=[Bx, Bln], writes=[Bx])
            S.op("pool", lambda e: e.tensor_tensor(x_, x_, lnb_t[0:M], ALU.add), reads=[Bx, Bln], writes=[Bx])
            S.dma("sp", y_o[tb * 128:tb * 128 + M, :], x_, reads=[Bx], sembuf=Bx)

        def o_item(w, wB, g, tb):
            M = 128 if tb < 8 else NSM
            tmode(w, wB, GW, lambda kc, tb=tb, M=M: yT[:, kc, tb * 128:tb * 128 + M], [BY], M,
                  lambda ps, pB, tb=tb, M=M, g=g: S.op("act", lambda e: e.activation(accv(tb)[0:M, g * GW:(g + 1) * GW], ps, AF.Copy),
                                                       reads=[pB], writes=[Bacc[tb]], cost=0.5))
        for g in range(6):
            w, wB = load_w(w_o, g * GW, GW)
            for tb in range(9):
                o_item(w, wB, g, tb)
        w6, wB6 = load_w(w_o, 6 * GW, GW)
        w7, wB7 = load_w(w_o, 7 * GW, GW)
        for tb in range(9):
            o_item(w6, wB6, 6, tb)
            o_item(w7, wB7, 7, tb)
            ln_block(tb)

        S.barrier()
        S.flush(block)
    return nc


_NC = None


def _consts():
    c = np.zeros((128, NCST), np.float32)
    i = np.arange(128)
    c[:, 0:128] = np.eye(128)
    c[:, 128:256] = (i[:, None] <= i[None, :])
    c[:, 256:384] = np.where(i[None, :] <= i[:, None], 0.0, NEG)
    c[:, 384:512] = np.where(i[:, None] <= i[None, :], 0.0, NEG)
    c[127, 512:640] = 1.0
    c[3, 640:768] = 1.0
    k = i[:, None]; q = i[None, :]
    c[:, 768:896] = np.where(k > q, q + 128 - k, BIG)
    c[:, 896:1024] = np.where(k <= q, q - k, BIG)
    q4 = np.arange(4)[None, :]
    c[:, 1024:1028] = np.where(k > q4, 128 + q4 - k, BIG)
    k4 = np.arange(4)[:, None]
    c[0:4, 1028:1032] = np.where(k4 <= q4, q4 - k4, BIG)
    return c


def kernel(x_prompt, x_sample, state_mlstm_C, state_mlstm_n, state_mlstm_m, state_attn_k, state_attn_v,
           w_in, b_igate, b_fgate, mlstm_norm_w, attn_sinks, w_proj_m, w_proj_a, w_out, ln_w, ln_b):
    global _NC
    f = lambda a: np.ascontiguousarray(np.asarray(a, dtype=np.float32))
    x_prompt, x_sample = f(x_prompt), f(x_sample)
    if _NC is None:
        _NC = build_program()
    cst = _consts()
    shared = {"w_in": f(w_in), "w_proj_m": f(w_proj_m), "w_proj_a": f(w_proj_a), "w_out": f(w_out), "cst": cst,
              "b_igate": f(b_igate), "b_fgate": f(b_fgate), "nw": f(np.asarray(mlstm_norm_w).reshape(16, 128).T),
              "attn_sinks": f(attn_sinks), "ln_w": f(ln_w), "ln_b": f(ln_b)}
    sC = f(state_mlstm_C); sn = f(state_mlstm_n); sm = f(state_mlstm_m)
    sk = f(state_attn_k).reshape(32, 128, 256); sv = f(state_attn_v).reshape(32, 128, 256)
    in_maps = []
    for c in range(8):
        b, half = c // 2, c % 2
        xo = x_prompt[b, half * 1024:(half + 1) * 1024]
        xs = x_sample[4 * c:4 * c + 4].reshape(16, D)
        xtok = np.concatenate([xo, xs], 0)
        xp = x_prompt[b, 0:1024] if half == 1 else np.zeros((1024, D), np.float32)
        m = dict(shared)
        m["xT"] = f(xtok.T)
        m["xTp"] = f(xp.T)
        m["xtok"] = f(xtok)
        m["dprev0"] = f(cst[:, 768:896]) if half == 1 else np.full((128, 128), BIG, np.float32)
        m["flag"] = np.full((128, 1), float(half), np.float32)
        m["sC"] = f(sC[4 * c:4 * c + 4])
        m["sn"] = f(sn[4 * c:4 * c + 4].reshape(4, 4, 4, 128).transpose(0, 1, 3, 2))
        m["sm"] = f(sm[4 * c:4 * c + 4].reshape(16))
        m["sk"] = f(sk[4 * c:4 * c + 4])
        m["sv"] = f(sv[4 * c:4 * c + 4])
        in_maps.append(m)
    res = run_bass_kernel_spmd(_NC, in_maps[:KCORES], core_ids=list(range(KCORES)))
    R = list(res.results) + [res.results[0]] * (8 - KCORES)
    y_p = np.zeros((4, 2048, D), np.float32); y_s = np.zeros((32, 4, D), np.float32)
    C_p = np.zeros((4, 4, 512, 512), np.float32); n_p = np.zeros((4, 4, 512), np.float32); m_p = np.zeros((4, 4), np.float32)
    kb_p = np.zeros((4, 128, 4, 64), np.float32); vb_p = np.zeros((4, 128, 4, 64), np.float32)
    C_s = np.zeros((32, 4, 512, 512), np.float32); n_s = np.zeros((32, 4, 512), np.float32); m_s = np.zeros((32, 4), np.float32)
    kb_s = np.zeros((32, 128, 4, 64), np.float32); vb_s = np.zeros((32, 128, 4, 64), np.float32)
    for c in range(8):
        b, half = c // 2, c % 2
        r = R[c]
        y_p[b, half * 1024:(half + 1) * 1024] = r["y"][0:1024]
        y_s[4 * c:4 * c + 4] = r["y"][1024:1040].reshape(4, 4, D)
        if half == 1:
            C_p[b] = r["Cp"]
            n_p[b] = r["np"].transpose(0, 2, 1).reshape(4, 512)
            m_p[b] = r["mp"][:, 0]
            kb_p[b] = r["kbp"].reshape(128, 4, 64)
            vb_p[b] = r["vbp"].reshape(128, 4, 64)
        C_s[4 * c:4 * c + 4] = r["Cs"]
        n_s[4 * c:4 * c + 4] = r["ns"].transpose(0, 1, 3, 2).reshape(4, 4, 512)
        m_s[4 * c:4 * c + 4] = r["ms"].reshape(4, 4)
        kb_s[4 * c:4 * c + 4] = r["kbs"].reshape(4, 128, 4, 64)
        vb_s[4 * c:4 * c + 4] = r["vbs"].reshape(4, 128, 4, 64)
    return (y_p, y_s, C_p, n_p, m_p, kb_p, vb_p, C_s, n_s, m_s, kb_s, vb_s)
```
